# Optimizing a Trainium2 kernel written in Bass

```python
import math
import jax, jax.numpy as jnp
from jax import lax
import numpy as np

D_MODEL = 1024
BATCH = 32
SEQ = 2048
DEPTH = 2

CTX_LEN = 256
GRID_W = 64
D_MIX = D_MODEL
NORM_EPS = 1e-6
A_WIDTH = D_MIX // 4
A_GROUPS = 4
A_CHUNK = 128
A_LN_EPS = 1e-5
B_HEADS = 4
B_HEAD_DIM = D_MIX // 16
B_WIDTH = B_HEADS * 2 * B_HEAD_DIM
B_QBLOCK = 128
ROPE_THETA = 10000.0
C_HEADS = 4
C_HEAD_DIM = D_MIX // 16
C_WIDTH = C_HEADS * C_HEAD_DIM
C_CONV = 5
C_CHUNK = 64
IN_SPLITS = [A_WIDTH] * 3 + [B_WIDTH] * 4 + [C_WIDTH] * 4 + [C_HEADS] * 4
D_IN = sum(IN_SPLITS)
IN_OFFSETS = [int(o) for o in np.cumsum(IN_SPLITS)[:-1]]

kernel_name = 'hybrid_gmlp_diffattn_gdn_prefix_block'


def rms_norm(x, w, eps=NORM_EPS):
    xf = x.astype(jnp.float32)
    y = xf * lax.rsqrt(jnp.mean(xf * xf, axis=-1, keepdims=True) + eps)
    return (y * w.astype(jnp.float32)).astype(x.dtype)


def layer_norm(x, w, eps=A_LN_EPS):
    xf = x.astype(jnp.float32)
    xc = xf - jnp.mean(xf, axis=-1, keepdims=True)
    y = xc * lax.rsqrt(jnp.mean(xc * xc, axis=-1, keepdims=True) + eps)
    return (y * w.astype(jnp.float32)).astype(x.dtype)


def l2_norm(x, eps=1e-6):
    xf = x.astype(jnp.float32)
    return (xf * lax.rsqrt(jnp.sum(xf * xf, axis=-1, keepdims=True) + eps)).astype(x.dtype)


def split_cols(p):
    return jnp.split(p, IN_OFFSETS, axis=-1)


def chunk_mlp(u, v, z, ln_w, w_s, b_s):
    bsz, n, _ = u.shape
    vc = layer_norm(v, ln_w).reshape(bsz, n // A_CHUNK, A_CHUNK, A_GROUPS, A_WIDTH // A_GROUPS)
    s = jnp.einsum('gij,bcjgd->bcigd', w_s, vc) + b_s.T[None, None, :, :, None]
    return u * s.reshape(bsz, n, A_WIDTH) * jax.nn.silu(z)


def rope_1d(x, pos):
    n = x.shape[-1]
    inv_freq = ROPE_THETA ** (-jnp.arange(0, n, 2, dtype=jnp.float32) / n)
    ang = pos.astype(jnp.float32)[:, None] * inv_freq[None, :]
    ang = jnp.concatenate([ang, ang], axis=-1)[None, :, None, :]
    x1, x2 = x[..., : n // 2], x[..., n // 2:]
    rot = jnp.concatenate([-x2, x1], axis=-1)
    return x * jnp.cos(ang).astype(x.dtype) + rot * jnp.sin(ang).astype(x.dtype)


def rope_2d(x, rows, cols):
    half = x.shape[-1] // 2
    return jnp.concatenate([rope_1d(x[..., :half], rows), rope_1d(x[..., half:], cols)], axis=-1)


def diff_attn_core(q, k, v, lam):
    s = jnp.einsum('bqhd,bkhd->bhqk', q, k, preferred_element_type=jnp.float32) * (q.shape[-1] ** -0.5)
    p = jax.nn.softmax(s, axis=-1)
    bsz, _, nq, nk = p.shape
    p = p.reshape(bsz, B_HEADS, 2, nq, nk)
    p = p[:, :, 0] - lam * p[:, :, 1]
    return jnp.einsum('bhqk,bkhe->bqhe', p.astype(v.dtype), v)


def diff_attn_latent(q, k_all, v_all, lam):
    bsz, n, h2, d = q.shape
    nb = n // B_QBLOCK
    qb = q.reshape(bsz, nb, B_QBLOCK, h2, d).transpose(1, 0, 2, 3, 4)
    out = lax.map(lambda qi: diff_attn_core(qi, k_all, v_all, lam), qb)
    return out.transpose(1, 0, 2, 3, 4).reshape(bsz, n, B_HEADS, 2 * B_HEAD_DIM)


def short_conv(x, w):
    n = x.shape[1]
    pad = C_CONV // 2
    xp = jnp.pad(x, ((0, 0), (pad, pad), (0, 0)))
    out = xp[:, 0:n] * w[0]
    for j in range(1, C_CONV):
        out = out + xp[:, j:j + n] * w[j]
    return out


def gated_delta_chunked(q, k, v, g, beta, state):
    f32 = jnp.float32
    bsz, n, h, dk = q.shape
    dv = v.shape[-1]
    nc = n // C_CHUNK

    def chunks(t):
        t = t.astype(f32).reshape((bsz, nc, C_CHUNK, h) + t.shape[3:])
        return jnp.moveaxis(t, (1, 3), (0, 2))

    qc = chunks(q) * (dk ** -0.5)
    kc = chunks(k)
    vc = chunks(v)
    gc = jnp.cumsum(chunks(g), axis=-1)
    bc = chunks(beta)
    tril = jnp.tril(jnp.ones((C_CHUNK, C_CHUNK), dtype=bool))
    strict = jnp.tril(jnp.ones((C_CHUNK, C_CHUNK), dtype=bool), -1)
    diff = gc[..., :, None] - gc[..., None, :]
    decay = jnp.where(tril, jnp.exp(jnp.where(tril, diff, 0.0)), 0.0)
    kb = kc * bc[..., None]
    lmat = jnp.where(strict, jnp.einsum('...id,...jd->...ij', kb, kc) * decay, 0.0)
    eye = jnp.eye(C_CHUNK, dtype=f32)
    tmat = lax.linalg.triangular_solve(lmat + eye, jnp.broadcast_to(eye, lmat.shape),
                                       left_side=True, lower=True, unit_diagonal=True)
    u = tmat @ (vc * bc[..., None])
    w = tmat @ (kb * jnp.exp(gc)[..., None])
    attn = jnp.where(tril, jnp.einsum('...id,...jd->...ij', qc, kc) * decay, 0.0)
    q_dec = qc * jnp.exp(gc)[..., None]
    g_last = gc[..., -1]
    k_dec = kc * jnp.exp(g_last[..., None] - gc)[..., None]

    def step(s, xs):
        q_i, k_i, u_i, w_i, a_i, gl_i = xs
        v_new = u_i - w_i @ s
        o_i = q_i @ s + a_i @ v_new
        s = s * jnp.exp(gl_i)[..., None, None] + jnp.swapaxes(k_i, -1, -2) @ v_new
        return s, o_i

    s_fin, o = lax.scan(step, state.astype(f32), (q_dec, k_dec, u, w, attn, g_last))
    o = jnp.moveaxis(o, (0, 2), (1, 3)).reshape(bsz, n, h, dv)
    return o.astype(v.dtype), s_fin


def gdn_prep(q, k, v, b_f, b_b, a_f, a_b, conv_w, a_log, dt_bias):
    f32 = jnp.float32
    bsz, n, _ = q.shape
    qkv = jax.nn.silu(short_conv(jnp.concatenate([q, k, v], axis=-1), conv_w))
    q, k, v = jnp.split(qkv, 3, axis=-1)
    hs = lambda t: t.reshape(bsz, n, C_HEADS, C_HEAD_DIM)
    q, k, v = l2_norm(hs(q)), l2_norm(hs(k)), hs(v)
    dirs = []
    for d, (b, a) in enumerate(((b_f, a_f), (b_b, a_b))):
        beta = jax.nn.sigmoid(b.astype(f32))
        g = -jnp.exp(a_log[d].astype(f32)) * jax.nn.softplus(a.astype(f32) + dt_bias[d].astype(f32))
        dirs.append((g, beta))
    return q, k, v, dirs


def hybrid_layer(x, xc, c_act, cc_act, w_ada, b_ada, norm_w, w_in, w_out,
                 a_ln_w, a_ws, a_bs, b_lam, b_subln_w,
                 c_conv_w, c_a_log, c_dt_bias, c_norm_w,
                 rows, cols, layer_idx, ctx_out):
    bsz, n, _ = x.shape
    shift, scale, gate = jnp.split((c_act @ w_ada + b_ada)[:, None, :], 3, axis=-1)
    shift_c, scale_c, gate_c = jnp.split(cc_act @ w_ada + b_ada, 3, axis=-1)
    h = rms_norm(x, norm_w) * (1.0 + scale) + shift
    hc = rms_norm(xc, norm_w) * (1.0 + scale_c) + shift_c
    pa = split_cols(h @ w_in)
    pc = split_cols(hc @ w_in)

    y_a = chunk_mlp(pa[0], pa[1], pa[2], a_ln_w, a_ws, a_bs)

    lam_init = 0.8 - 0.6 * math.exp(-0.3 * layer_idx)
    lf = b_lam.astype(jnp.float32)
    lam = jnp.exp(jnp.sum(lf[0] * lf[1])) - jnp.exp(jnp.sum(lf[2] * lf[3])) + lam_init

    def heads(q, k, v):
        m = q.shape[1]
        return (q.reshape(bsz, m, 2 * B_HEADS, B_HEAD_DIM), k.reshape(bsz, m, 2 * B_HEADS, B_HEAD_DIM),
                v.reshape(bsz, m, B_HEADS, 2 * B_HEAD_DIM))

    def finish_b(o, z):
        o = rms_norm(o, b_subln_w) * (1.0 - lam_init)
        return o.reshape(o.shape[0], o.shape[1], B_WIDTH) * jax.nn.silu(z)

    q_l, k_l, v_l = heads(pa[3], pa[4], pa[5])
    q_l, k_l = rope_2d(q_l, rows, cols), rope_2d(k_l, rows, cols)
    q_c, k_c, v_c = heads(pc[3], pc[4], pc[5])
    k_all = jnp.concatenate([k_c, k_l], axis=1)
    v_all = jnp.concatenate([v_c, v_l], axis=1)
    y_b = finish_b(diff_attn_latent(q_l, k_all, v_all, lam), pa[6])

    lat = gdn_prep(pa[7], pa[8], pa[9], pa[11], pa[12], pa[13], pa[14], c_conv_w, c_a_log, c_dt_bias)
    cxt = gdn_prep(pc[7], pc[8], pc[9], pc[11], pc[12], pc[13], pc[14], c_conv_w, c_a_log, c_dt_bias)
    zero = jnp.zeros((bsz, C_HEADS, C_HEAD_DIM, C_HEAD_DIM), jnp.float32)
    o_lat, o_ctx = [], []
    for d in range(2):
        fl = (lambda t: jnp.flip(t, axis=1)) if d == 1 else (lambda t: t)
        ql, kl, vl, dl = lat
        qc_, kc_, vc_, dc = cxt
        oc, s_c = gated_delta_chunked(fl(qc_), fl(kc_), fl(vc_), fl(dc[d][0]), fl(dc[d][1]), zero)
        ol, _ = gated_delta_chunked(fl(ql), fl(kl), fl(vl), fl(dl[d][0]), fl(dl[d][1]), s_c)
        o_lat.append(fl(ol))
        o_ctx.append(fl(oc))

    def finish_c(o, z):
        o = rms_norm(o, c_norm_w)
        return o.reshape(o.shape[0], o.shape[1], C_WIDTH) * jax.nn.silu(z)

    y_c = finish_c(o_lat[0] + o_lat[1], pa[10])

    x = x + gate * (jnp.concatenate([y_a, y_b, y_c], axis=-1) @ w_out)
    if ctx_out:
        yc_a = chunk_mlp(pc[0], pc[1], pc[2], a_ln_w, a_ws, a_bs)
        yc_b = finish_b(diff_attn_core(q_c, k_c, v_c, lam), pc[6])
        yc_c = finish_c(o_ctx[0] + o_ctx[1], pc[10])
        xc = xc + gate_c * (jnp.concatenate([yc_a, yc_b, yc_c], axis=-1) @ w_out)
    return x, xc


def setup_inputs(seed: int = 0) -> dict:
    key = jax.random.key(seed)
    ks = jax.random.split(key, 20)
    f32 = jnp.float32
    nrm = lambda k, s: jax.random.normal(k, s, dtype=f32)
    dt = jnp.exp(jax.random.uniform(ks[17], (DEPTH, 2, C_HEADS), dtype=f32,
                                    minval=math.log(1e-3), maxval=math.log(1e-1)))
    return {
        'x': nrm(ks[0], (BATCH, SEQ, D_MODEL)),
        'c': nrm(ks[1], (BATCH, D_MODEL)),
        'ctx': nrm(ks[2], (BATCH, CTX_LEN, D_MODEL)),
        'c_ctx': nrm(ks[3], (D_MODEL,)),
        'w_ada': nrm(ks[4], (DEPTH, D_MODEL, 3 * D_MODEL)) * (0.5 * D_MODEL ** -0.5),
        'b_ada': nrm(ks[5], (DEPTH, 3 * D_MODEL)) * 0.02,
        'norm_w': 1.0 + 0.02 * nrm(ks[6], (DEPTH, D_MODEL)),
        'w_in': nrm(ks[7], (DEPTH, D_MODEL, D_IN)) * D_MODEL ** -0.5,
        'w_out': nrm(ks[8], (DEPTH, D_MIX, D_MODEL)) * D_MIX ** -0.5,
        'a_ln_w': 1.0 + 0.02 * nrm(ks[9], (DEPTH, A_WIDTH)),
        'a_ws': nrm(ks[10], (DEPTH, A_GROUPS, A_CHUNK, A_CHUNK)) * A_CHUNK ** -0.5,
        'a_bs': 1.0 + 0.02 * nrm(ks[11], (DEPTH, A_GROUPS, A_CHUNK)),
        'b_lam': nrm(ks[12], (DEPTH, 4, B_HEAD_DIM)) * 0.1,
        'b_subln_w': 1.0 + 0.02 * nrm(ks[13], (DEPTH, 2 * B_HEAD_DIM)),
        'c_conv_w': nrm(ks[14], (DEPTH, C_CONV, 3 * C_WIDTH)) * C_CONV ** -0.5,
        'c_a_log': jnp.log(jax.random.uniform(ks[15], (DEPTH, 2, C_HEADS), dtype=f32, minval=1.0, maxval=16.0)),
        'c_dt_bias': dt + jnp.log(-jnp.expm1(-dt)),
        'c_norm_w': 1.0 + 0.02 * nrm(ks[16], (DEPTH, C_HEAD_DIM)),
        'final_norm_w': 1.0 + 0.02 * nrm(ks[18], (D_MODEL,)),
    }


def reference(x, c, ctx, c_ctx, w_ada, b_ada, norm_w, w_in, w_out, a_ln_w, a_ws, a_bs,
              b_lam, b_subln_w, c_conv_w, c_a_log, c_dt_bias, c_norm_w, final_norm_w):
    n = x.shape[1]
    rows_count = n // GRID_W
    rows = jnp.repeat(jnp.arange(rows_count), GRID_W)
    cols = jnp.tile(jnp.arange(GRID_W), rows_count)
    c_act = jax.nn.silu(c)
    cc_act = jax.nn.silu(c_ctx)
    xc = ctx
    for i in range(DEPTH):
        x, xc = hybrid_layer(x, xc, c_act, cc_act, w_ada[i], b_ada[i], norm_w[i], w_in[i], w_out[i],
                             a_ln_w[i], a_ws[i], a_bs[i], b_lam[i], b_subln_w[i],
                             c_conv_w[i], c_a_log[i], c_dt_bias[i], c_norm_w[i],
                             rows, cols, i, i < DEPTH - 1)
    return rms_norm(x, final_norm_w)
```

```python
import math
import numpy as np
import concourse.bass as bass
import concourse.mybir as mybir
from concourse.bass_utils import run_bass_kernel_spmd

F32 = mybir.dt.float32
BF16 = mybir.dt.bfloat16
AF = mybir.ActivationFunctionType
ALU = mybir.AluOpType
AX = mybir.AxisListType

NCORES = 8
DM = 1024
SEQ = 2048
CTXL = 256
T = SEQ + CTXL
NT = T // 128
DIN = 3856
EPS = 1e-6


class _Res:
    __slots__ = ("last_w", "readers")

    def __init__(self):
        self.last_w = None
        self.readers = []


class _Op:
    __slots__ = ("eng", "fn", "deps", "dma", "idx", "sig", "needed", "dsem", "dval", "dprev")

    def __init__(self, eng, fn, deps, dma, idx):
        self.eng = eng
        self.fn = fn
        self.deps = deps
        self.dma = dma
        self.idx = idx
        self.sig = None
        self.needed = False
        self.dsem = None
        self.dval = 0
        self.dprev = 0


class Prog:
    ENGS = ("pe", "act", "dve", "pool", "sp")

    def __init__(self, nc, n_dma_sems=8):
        self.nc = nc
        self.ops = []
        self.res = {}
        self.n_dma_sems = n_dma_sems
        self.cur_barrier = None
        self.last_on = {}
        self.dmas_since = []

    def _st(self, r):
        s = self.res.get(r)
        if s is None:
            s = self.res[r] = _Res()
        return s

    def op(self, eng, fn, reads=(), writes=(), dma=False):
        idx = len(self.ops)
        deps = set()
        for r in reads:
            st = self._st(r)
            if st.last_w is not None:
                deps.add(st.last_w)
        for w in writes:
            st = self._st(w)
            if st.last_w is not None:
                deps.add(st.last_w)
            deps.update(st.readers)
        for r in reads:
            self._st(r).readers.append(idx)
        for w in writes:
            st = self._st(w)
            st.last_w = idx
            st.readers = []
        if self.cur_barrier is not None:
            deps.add(self.cur_barrier)
        deps.discard(idx)
        self.ops.append(_Op(eng, fn, deps, dma, idx))
        if dma:
            self.dmas_since.append(idx)
        else:
            self.last_on[eng] = idx
        return idx

    def dma(self, eng, fn, reads=(), writes=()):
        return self.op(eng, fn, reads, writes, dma=True)

    def barrier(self):
        deps = set(self.last_on.values()) | set(self.dmas_since)
        idx = len(self.ops)
        o = _Op("pool", lambda e: e.nop(), deps, False, idx)
        self.ops.append(o)
        self.cur_barrier = idx
        self.last_on = {"pool": idx}
        self.dmas_since = []
        return idx

    def emit(self):
        nc = self.nc
        ops = self.ops
        for o in ops:
            for d in o.deps:
                ops[d].needed = True
        cnt = {e: 0 for e in self.ENGS}
        for o in ops:
            if o.dma:
                continue
            if o.needed:
                cnt[o.eng] += 1
                o.sig = cnt[o.eng]
        dcount = {}
        dma_n = {e: 0 for e in self.ENGS}
        for o in ops:
            if not o.dma:
                continue
            k = dma_n[o.eng] % self.n_dma_sems
            dma_n[o.eng] += 1
            key = (o.eng, k)
            o.dsem = key
            o.dprev = dcount.get(key, 0)
            o.dval = o.dprev + 16
            dcount[key] = o.dval
        sem = {}
        self._handles = []
        for e in self.ENGS:
            h = nc.semaphore("sg_" + e)
            self._handles.append(h)
            sem[e] = h.__enter__()
        for key in dcount:
            h = nc.semaphore("sd_%s_%d" % key)
            self._handles.append(h)
            sem[key] = h.__enter__()
        per_eng = {e: [o for o in ops if o.eng == e] for e in self.ENGS}
        stats = {"ops": len(ops), "waits": 0, "sig": dict(cnt)}

        def run_engine(ename, eng):
            seen = {}
            for o in per_eng[ename]:
                waits = {}
                for d in o.deps:
                    p = ops[d]
                    if p.dma:
                        key, val = p.dsem, p.dval
                    else:
                        if p.eng == ename and ename == "pe":
                            continue
                        key, val = p.eng, p.sig
                    if seen.get(key, 0) >= val:
                        continue
                    if waits.get(key, 0) < val:
                        waits[key] = val
                if o.dma and o.dprev > 0 and seen.get(o.dsem, 0) < o.dprev:
                    if waits.get(o.dsem, 0) < o.dprev:
                        waits[o.dsem] = o.dprev
                wl = list(waits.items())
                for key, val in wl:
                    seen[key] = val
                stats["waits"] += len(wl)
                for key, val in wl[:-1]:
                    eng.wait_ge(sem[key], val)
                r = o.fn(eng)
                if isinstance(r, (list, tuple)):
                    first, last = r[0], r[-1]
                else:
                    first = last = r
                if wl:
                    key, val = wl[-1]
                    first._wait_ge(sem[key], val)
                if o.dma:
                    last.then_inc(sem[o.dsem], 16)
                elif o.sig is not None:
                    last.then_inc(sem[ename], 1)
            if ename == "sp":
                for key, val in dcount.items():
                    if seen.get(key, 0) < val:
                        eng.wait_ge(sem[key], val)

        with nc.Block() as block:
            @block.tensor
            def _(eng):
                run_engine("pe", eng)

            @block.scalar
            def _(eng):
                run_engine("act", eng)

            @block.vector
            def _(eng):
                run_engine("dve", eng)

            @block.gpsimd
            def _(eng):
                run_engine("pool", eng)

            @block.sync
            def _(eng):
                run_engine("sp", eng)
        return stats


class Arena:
    def __init__(self, ap_f32):
        self.t = ap_f32
        self.cap = ap_f32.shape[1] * 4
        self.off = 0

    def reset(self):
        self.off = 0

    def alloc(self, n, dt=F32):
        size = n * (4 if dt == F32 else 2)
        size = (size + 63) // 64 * 64
        assert self.off + size <= self.cap, ("arena overflow", self.off, size, self.cap)
        v = self.t[:, self.off // 4:(self.off + size) // 4]
        self.off += size
        if dt != F32:
            v = v.bitcast(dt)
        return v[:, 0:n]


def build(NB, cfg):
    useA, useB, useC = cfg.get("A", True), cfg.get("B", True), cfg.get("C", True)
    nlayers = cfg.get("layers", 2)
    nc = bass.Bass("TRN2", target_bir_lowering=False)
    P = Prog(nc)
    RR = NB + 1

    def din(name, shape):
        return nc.dram_tensor(name, shape, F32, kind="ExternalInput").ap()

    x_d = din("x", [NB, SEQ, DM])
    ctx_d = din("ctx", [NB, CTXL, DM])
    cT_d = din("cT", [128, 8 * RR])
    w_ada_d = din("w_ada", [2, DM, 3 * DM])
    b_adaT_d = din("b_adaT", [128, 48])
    norm_wT_d = din("norm_wT", [128, 16])
    w_in_d = din("w_in", [2, DM, DIN])
    w_out_d = din("w_out", [2, DM, DM])
    a_ln_w_d = din("a_ln_w", [2, 256])
    a_wsT_d = din("a_wsT", [2, 128, 512])
    a_bsT_d = din("a_bsT", [2, 128, 4])
    b_lam_d = din("b_lam", [2, 256])
    b_subln_d = din("b_subln_w", [2, 128])
    conv_wT_d = din("conv_wT", [2, 128, 30])
    c_alog_d = din("c_a_log", [2, 8])
    c_dtb_d = din("c_dt_bias", [2, 8])
    c_normw_d = din("c_norm_w", [2, 64])
    fnw_d = din("final_norm_w", [1, DM])
    ident_d = din("ident", [128, 128])
    perm_d = din("perm", [128, 128])
    cos_d = din("ropecos", [128, SEQ])
    sin_d = din("ropesin", [128, SEQ])
    gmask_d = din("gmasks", [128, 8 * 128])
    out_d = nc.dram_tensor("out", [NB, SEQ, DM], F32, kind="ExternalOutput").ap()
    dbg_d = {}
    for nm, shp in cfg.get("dbg", {}).items():
        dbg_d[nm] = nc.dram_tensor("dbg_" + nm, shp, F32, kind="ExternalOutput").ap()

    cnt = [0]

    def sb(shape, dt=F32, name=None):
        cnt[0] += 1
        return nc.alloc_sbuf_tensor("s_" + (name or ("t%d" % cnt[0])), shape, dt).ap()

    pb = [nc.alloc_psum_tensor("pb%d" % i, [128, 512], F32).ap() for i in range(8)]
    pbn = ["pb%d" % i for i in range(8)]

    X = sb([128, NT * DM], F32, "X")
    X3 = X.rearrange("p (t d) -> p t d", t=NT)
    hT = sb([128, 8 * T], BF16, "hT")
    hT3 = hT.rearrange("p (k t) -> p k t", k=8)
    ident_f = sb([128, 128], F32, "ident_f")
    ident_b = sb([128, 128], BF16, "ident_b")
    ones_f = sb([128, 128], F32, "ones_f")
    cT = sb([128, 8 * RR], F32, "cT")
    cact = sb([128, 8 * RR], BF16, "cact")
    cact3 = cact.rearrange("p (k r) -> p k r", k=8)
    b_adaT = sb([128, 48], F32, "b_adaT")
    norm_wT = sb([128, 16], F32, "norm_wT")
    modT = [sb([128, 24 * RR], F32, "modT%d" % l) for l in range(2)]
    modT3 = [m.rearrange("p (t r) -> p t r", t=24) for m in modT]
    g1T = [sb([128, 8], F32, "g1T%d" % i) for i in range(2)]
    gate_bc = [sb([128, DM], F32, "gate_bc%d" % i) for i in range(2)]
    dg = [sb([128, 128], F32, "dg%d" % i) for i in range(2)]
    xn = sb([128, DM], BF16, "xn")
    stat = sb([128, 64], F32, "stat")
    ARENA_BYTES = cfg.get("arena", 82 * 1024)
    arena = Arena(sb([128, ARENA_BYTES // 4], F32, "arena"))

    def ACT(out, in_, func, reads, writes, **kw):
        P.op("act", lambda e: e.activation(out, in_, func, **kw), reads, writes)

    def MM(out, pairs, reads, writes):
        def fn(e):
            n = len(pairs)
            return [e.matmul(out, l, r, start=(i == 0), stop=(i == n - 1)) for i, (l, r) in enumerate(pairs)]
        P.op("pe", fn, reads, writes)

    def TRS(items, reads, writes, ident):
        P.op("pe", lambda e: [e.transpose(o, i, ident) for (o, i) in items], reads, writes)

    def DVE(fn, reads, writes):
        P.op("dve", fn, reads, writes)

    def POOL(fn, reads, writes):
        P.op("pool", fn, reads, writes)

    def DMA(q, out, in_, reads, writes):
        P.dma(q, lambda e: e.dma_start(out=out, in_=in_), reads, writes)

    def dump(name, src_ap, reads, dst=None):
        if name in dbg_d:
            d = dbg_d[name] if dst is None else dst
            DMA("pool", d, src_ap, reads, [])

    DMA("sp", ident_f, ident_d, [], ["ident_f"])
    DMA("pool", ident_b, ident_d, [], ["ident_b"])
    POOL(lambda e: e.memset(ones_f, 1.0), [], ["ones_f"])
    DMA("sp", cT, cT_d, [], ["cT"])
    DMA("sp", b_adaT, b_adaT_d, [], ["b_adaT"])
    DMA("sp", norm_wT, norm_wT_d, [], ["norm_wT"])
    ACT(cact, cT, AF.Silu, ["cT"], ["cact"])

    arena.reset()
    wa_buf = [arena.alloc(8 * 512, BF16) for _ in range(2)]
    for l in range(nlayers):
        wsrc = w_ada_d[l].rearrange("(k p) n -> p k n", p=128)
        for g in range(6):
            wa = wa_buf[g % 2]
            wa3 = wa.rearrange("p (k n) -> p k n", k=8)
            wn = "wa%d" % (g % 2)
            DMA("pool", wa3, wsrc[:, :, g * 512:(g + 1) * 512], [], [wn])
            for t in range(4):
                tt = g * 4 + t
                MM(pb[0][:, tt * RR:(tt + 1) * RR],
                   [(wa3[:, k, t * 128:(t + 1) * 128], cact3[:, k, :]) for k in range(8)],
                   [wn, "cact"], ["pb0"])
        DVE(lambda e, l=l: e.tensor_tensor(
            modT3[l], pb[0][:, 0:24 * RR].rearrange("p (t r) -> p t r", t=24),
            b_adaT[:, l * 24:(l + 1) * 24].unsqueeze(2).broadcast_to([128, 24, RR]), ALU.add),
            ["pb0", "b_adaT"], ["modT%d" % l])
    P.barrier()

    def mod_prep(b, l, ctx_out):
        rows = [(0, b)] + ([(1, NB)] if True else [])
        for which, r in rows:
            DVE(lambda e, which=which, r=r: e.scalar_tensor_tensor(
                out=g1T[which], in0=modT3[l][:, 8:16, r], scalar=1.0, in1=norm_wT[:, l * 8:(l + 1) * 8],
                op0=ALU.add, op1=ALU.mult), ["modT%d" % l, "norm_wT"], ["g1T%d" % which])
            if which == 1 and not ctx_out:
                continue
            for k in range(8):
                d = dg[k % 2]
                dn = "dg%d" % (k % 2)
                DVE(lambda e, d=d, k=k, r=r: e.tensor_scalar(d, ident_f, modT3[l][:, 16 + k, r:r + 1], None, ALU.mult),
                    ["ident_f", "modT%d" % l], [dn])
                bank = 6 + k // 4
                MM(pb[bank][:, (k % 4) * 128:(k % 4 + 1) * 128], [(ones_f, d)], ["ones_f", dn], [pbn[bank]])
            ACT(gate_bc[which][:, 0:512], pb[6], AF.Copy, ["pb6"], ["gate_bc%d" % which])
            ACT(gate_bc[which][:, 512:1024], pb[7], AF.Copy, ["pb7"], ["gate_bc%d" % which])

    def phase1(b, l):
        if l == 0:
            DMA("sp", X3[:, 0:2, :], ctx_d[b].rearrange("(t p) d -> p t d", p=128), [], ["X0", "X1"])
            for g in range(4):
                DMA("sp", X3[:, 2 + 4 * g:6 + 4 * g, :],
                    x_d[b, g * 512:(g + 1) * 512, :].rearrange("(t p) d -> p t d", p=128),
                    [], ["X%d" % (2 + 4 * g + i) for i in range(4)])
        for p in range(NT):
            which = 1 if p < 2 else 0
            r = NB if p < 2 else b
            xs = X3[:, p, :]
            ss = stat[:, 0:1]
            sd = stat[:, 1:2]
            rstd = stat[:, 2:3]
            ACT(xn, xs, AF.Square, ["X%d" % p], ["xn", "st0"], accum_out=ss)
            ACT(sd, ss, AF.Sqrt, ["st0"], ["st1"], scale=1.0 / DM, bias=EPS)
            DVE(lambda e: e.reciprocal(rstd, sd), ["st1"], ["st2"])
            ACT(xn, xs, AF.Copy, ["X%d" % p, "st2"], ["xn"], scale=rstd)
            pT = pb[p % 2].bitcast(BF16)
            pTn = pbn[p % 2]
            TRS([(pT[:, k * 128:(k + 1) * 128], xn[:, k * 128:(k + 1) * 128]) for k in range(8)],
                ["xn", "ident_b"], [pTn], ident_b)
            for k in range(8):
                o = hT3[:, k, p * 128:(p + 1) * 128]
                i_ = pT[:, k * 128:(k + 1) * 128]
                sc = g1T[which][:, k:k + 1]
                bi = modT3[l][:, k, r:r + 1]
                if k % 2 == 0:
                    ACT(o, i_, AF.Identity, [pTn, "g1T%d" % which, "modT%d" % l], ["hT%d" % p], scale=sc, bias=bi)
                else:
                    DVE(lambda e, o=o, i_=i_, sc=sc, bi=bi: e.tensor_scalar(o, i_, sc, bi, ALU.mult, ALU.add),
                        [pTn, "g1T%d" % which, "modT%d" % l], ["hT%d" % p])

    def x_update(p, which, pa, pan, pbk, pbkn, tmp, tmpn):
        for half, (bank, bn) in enumerate(((pa, pan), (pbk, pbkn))):
            sl = slice(half * 512, (half + 1) * 512)
            DVE(lambda e, bank=bank, sl=sl: e.tensor_tensor(tmp[:, sl], bank, gate_bc[which][:, sl], ALU.mult),
                [bn, "gate_bc%d" % which], [tmpn])
            POOL(lambda e, sl=sl: e.tensor_tensor(X3[:, p, sl], X3[:, p, sl], tmp[:, sl], ALU.add),
                 [tmpn, "X%d" % p], ["X%d" % p])

    def phaseA(b, l, ctx_out):
        arena.reset()
        wA = arena.alloc(8 * 768, BF16)
        wA3 = wA.rearrange("p (k n) -> p k n", k=8)
        woA = arena.alloc(2 * DM, BF16)
        woA3 = woA.rearrange("p (k n) -> p k n", k=2)
        awsT = arena.alloc(512, BF16)
        abs_ = arena.alloc(4, F32)
        aln = arena.alloc(256, F32)
        vsb = arena.alloc(256, F32)
        vc = arena.alloc(256, BF16)
        sz = arena.alloc(256, F32)
        uz = arena.alloc(256, F32)
        ya = arena.alloc(256, BF16)
        yT = arena.alloc(256, BF16)
        tmp = arena.alloc(DM, F32)
        st = arena.alloc(16, F32)
        DMA("pool", wA3, w_in_d[l].rearrange("(k p) n -> p k n", p=128)[:, :, 0:768], [], ["wA"])
        DMA("pool", woA3, w_out_d[l, 0:256, :].rearrange("(k p) n -> p k n", p=128), [], ["woA"])
        DMA("pool", awsT, a_wsT_d[l], [], ["awsT"])
        DMA("sp", abs_, a_bsT_d[l], [], ["abs"])
        DMA("sp", aln, a_ln_w_d[l:l + 1, :].partition_broadcast(128), [], ["aln"])
        for p in range(NT):
            if p < 2 and not ctx_out:
                continue
            which = 1 if p < 2 else 0
            tok = slice(p * 128, (p + 1) * 128)
            MM(pb[2], [(hT3[:, k, tok], wA3[:, k, 0:512]) for k in range(8)], ["hT%d" % p, "wA"], ["pb2"])
            MM(pb[3][:, 0:256], [(hT3[:, k, tok], wA3[:, k, 512:768]) for k in range(8)], ["hT%d" % p, "wA"], ["pb3"])
            DVE(lambda e: e.bn_stats(st[:, 0:6], pb[2][:, 256:512]), ["pb2"], ["Ast"])
            DVE(lambda e: e.bn_aggr(st[:, 6:8], st[:, 0:6]), ["Ast"], ["Amv"])
            ACT(st[:, 8:9], st[:, 7:8], AF.Sqrt, ["Amv"], ["Asd"], bias=1e-5)
            DVE(lambda e: e.reciprocal(st[:, 9:10], st[:, 8:9]), ["Asd"], ["Ars"])
            DVE(lambda e: e.scalar_tensor_tensor(out=st[:, 10:11], in0=st[:, 6:7], scalar=-1.0, in1=st[:, 9:10],
                                                 op0=ALU.mult, op1=ALU.mult), ["Amv", "Ars"], ["Anb"])
            ACT(vsb, pb[2][:, 256:512], AF.Identity, ["pb2", "Ars", "Anb"], ["vsb"], scale=st[:, 9:10], bias=st[:, 10:11])
            DVE(lambda e: e.tensor_tensor(vc, vsb, aln, ALU.mult), ["vsb", "aln"], ["vc"])
            awsT3 = awsT.rearrange("p (g i) -> p g i", g=4)
            P.op("pe", lambda e: [e.matmul(pb[7][:, g * 64:(g + 1) * 64], awsT3[:, g, :], vc[:, g * 64:(g + 1) * 64],
                                           start=True, stop=True) for g in range(4)], ["awsT", "vc"], ["pb7"])
            ACT(sz, pb[3][:, 0:256], AF.Silu, ["pb3"], ["sz"])
            DVE(lambda e: e.tensor_tensor(uz, pb[2][:, 0:256], sz, ALU.mult), ["pb2", "sz"], ["uz"])
            for g in range(4):
                cs = slice(g * 64, (g + 1) * 64)
                DVE(lambda e, g=g, cs=cs: e.scalar_tensor_tensor(
                    out=ya[:, cs], in0=pb[7][:, g * 64:(g + 1) * 64], scalar=abs_[:, g:g + 1], in1=uz[:, cs],
                    op0=ALU.add, op1=ALU.mult), ["pb7", "uz", "abs"], ["ya"])
            pT = pb[4].bitcast(BF16)
            TRS([(pT[:, c * 128:(c + 1) * 128], ya[:, c * 128:(c + 1) * 128]) for c in range(2)], ["ya", "ident_b"], ["pb4"], ident_b)
            ACT(yT, pT[:, 0:256], AF.Copy, ["pb4"], ["yT"])
            for half in range(2):
                MM(pb[5 + half], [(yT[:, c * 128:(c + 1) * 128], woA3[:, c, half * 512:(half + 1) * 512]) for c in range(2)],
                   ["yT", "woA"], [pbn[5 + half]])
            x_update(p, which, pb[5], "pb5", pb[6], "pb6", tmp, "tmpA")
        P.barrier()

    def phaseB(b, l, ctx_out):
        arena.reset()
        lam_init = 0.8 - 0.6 * math.exp(-0.3 * l)
        cosb = arena.alloc(SEQ, BF16)
        sinb = arena.alloc(SEQ, BF16)
        permb = arena.alloc(128, BF16)
        woB = arena.alloc(4 * DM, BF16)
        woB3 = woB.rearrange("p (k n) -> p k n", k=4)
        lamb = arena.alloc(256, F32)
        lamb4 = lamb.rearrange("p (a w d) -> p a w d", a=2, w=2)
        lprod = arena.alloc(128, F32)
        lst = arena.alloc(16, F32)
        sublnw = arena.alloc(128, F32)
        wB = arena.alloc(8 * 512, BF16)
        wB3 = wB.rearrange("p (k n) -> p k n", k=8)
        qT = arena.alloc(T, BF16)
        kT = arena.alloc(T, BF16)
        vaug = arena.alloc(NT * 130, BF16)
        vaug3 = vaug.rearrange("p (t c) -> p t c", t=NT)
        szw = arena.alloc(NT * 128, BF16)
        szw3 = szw.rearrange("p (t c) -> p t c", t=NT)
        qraw = [arena.alloc(512, BF16) for _ in range(2)]
        t1 = [arena.alloc(512, F32) for _ in range(2)]
        t2 = [arena.alloc(512, F32) for _ in range(2)]
        PT = [arena.alloc(512, BF16) for _ in range(3)]
        szf = arena.alloc(128, F32)
        o32 = arena.alloc(128, F32)
        junk = arena.alloc(128, BF16)
        yb = arena.alloc(128, BF16)
        yTb = arena.alloc(128, BF16)
        fst = arena.alloc(16, F32)
        tmp = arena.alloc(DM, F32)
        DMA("pool", cosb, cos_d, [], ["cosb"])
        DMA("pool", sinb, sin_d, [], ["sinb"])
        DMA("pool", permb, perm_d, [], ["permb"])
        DMA("pool", woB3, w_out_d[l, 256:768, :].rearrange("(k p) n -> p k n", p=128), [], ["woB"])
        DMA("sp", lamb, b_lam_d[l:l + 1, :].partition_broadcast(128), [], ["lamb"])
        DMA("sp", sublnw, b_subln_d[l:l + 1, :].partition_broadcast(128), [], ["sublnw"])
        ACT(sublnw, sublnw, AF.Copy, ["sublnw"], ["sublnw"], scale=(1.0 - lam_init))
        POOL(lambda e: e.memset(vaug3[:, :, 128:130], 1.0), [], ["vaug_ones"])
        lprod3 = lprod.rearrange("p (a d) -> p a d", a=2)
        DVE(lambda e: e.tensor_tensor(lprod3, lamb4[:, :, 0, :], lamb4[:, :, 1, :], ALU.mult), ["lamb"], ["lprod"])
        DVE(lambda e: e.reduce_sum(lst[:, 0:2], lprod3, AX.X), ["lprod"], ["lst0"])
        ACT(lst[:, 2:4], lst[:, 0:2], AF.Exp, ["lst0"], ["lst1"])
        DVE(lambda e: e.tensor_tensor(lst[:, 4:5], lst[:, 3:4], lst[:, 2:3], ALU.subtract), ["lst1"], ["lst2"])
        DVE(lambda e: e.tensor_scalar(lst[:, 5:6], lst[:, 4:5], -lam_init, None, ALU.add), ["lst2"], ["nlam"])
        nlam = lst[:, 5:6]
        wsrc = w_in_d[l].rearrange("(k p) n -> p k n", p=128)
        groups = [(0, 256, [0, 1])] + [(256 + 512 * g, 512, [2 + 4 * g + i for i in range(4)]) for g in range(4)]
        for hp in range(4):
            for ci, c0 in enumerate((768, 1280, 1792, 2304)):
                DMA("pool", wB3[:, :, ci * 128:(ci + 1) * 128], wsrc[:, :, c0 + hp * 128:c0 + (hp + 1) * 128], [], ["wB%d" % ci])
            cntr = 0
            for wi, (dst, dname) in enumerate(((qT, "qT"), (kT, "kT"))):
                for gi, (t0, n, tiles) in enumerate(groups):
                    if wi == 0 and gi == 0 and not ctx_out:
                        continue
                    pp = pb[cntr % 2]
                    ppn = pbn[cntr % 2]
                    MM(pp[:, 0:n], [(wB3[:, k, wi * 128:(wi + 1) * 128], hT3[:, k, t0:t0 + n]) for k in range(8)],
                       ["wB%d" % wi] + ["hT%d" % p for p in tiles], [ppn])
                    rn = "%s%d" % (dname, gi)
                    if gi == 0:
                        ACT(dst[:, t0:t0 + n], pp[:, 0:n], AF.Copy, [ppn], [rn])
                    else:
                        qr = qraw[cntr % 2]
                        qrn = "qraw%d" % (cntr % 2)
                        pq = pb[2 + cntr % 2]
                        pqn = pbn[2 + cntr % 2]
                        ta = t1[cntr % 2]
                        tan = "t1_%d" % (cntr % 2)
                        tb = t2[cntr % 2]
                        tbn = "t2_%d" % (cntr % 2)
                        ps = slice(t0 - 256, t0 - 256 + n)
                        ACT(qr, pp, AF.Copy, [ppn], [qrn])
                        MM(pq, [(permb, qr)], ["permb", qrn], [pqn])
                        POOL(lambda e, ta=ta, qr=qr, ps=ps: e.tensor_tensor(ta, qr, cosb[:, ps], ALU.mult), [qrn, "cosb"], [tan])
                        DVE(lambda e, tb=tb, pq=pq, ps=ps: e.tensor_tensor(tb, pq, sinb[:, ps], ALU.mult), [pqn, "sinb"], [tbn])
                        POOL(lambda e, dst=dst, t0=t0, n=n, ta=ta, tb=tb: e.tensor_tensor(dst[:, t0:t0 + n], ta, tb, ALU.add),
                             [tan, tbn], [rn])
                    cntr += 1
            for p in range(NT):
                tok = slice(p * 128, (p + 1) * 128)
                need_z = (p >= 2) or ctx_out
                ncols = 256 if need_z else 128
                pp = pb[4 + p % 2]
                ppn = pbn[4 + p % 2]
                MM(pp[:, 0:ncols], [(hT3[:, k, tok], wB3[:, k, 256:256 + ncols]) for k in range(8)],
                   ["hT%d" % p, "wB2", "wB3"], [ppn])
                ACT(vaug3[:, p, 0:128], pp[:, 0:128], AF.Copy, [ppn], ["vaug%d" % p])
                if need_z:
                    ACT(szf, pp[:, 128:256], AF.Silu, [ppn], ["szf"])
                    DVE(lambda e, p=p: e.tensor_tensor(szw3[:, p, :], szf, sublnw, ALU.mult), ["szf", "sublnw"], ["szw%d" % p])
            def kgroup(kt):
                return 0 if kt < 2 else 1 + (kt - 2) // 4
            qgroups = [(gi, g) for gi, g in enumerate(groups) if gi > 0]
            if ctx_out:
                qgroups = [(0, groups[0])] + qgroups
            ei = 0
            for gi, (t0, n, tiles) in qgroups:
                nq = len(tiles)
                keyt = [0, 1] if gi == 0 else list(range(NT))

                def oslot(j, qi):
                    s_ = j * nq + qi
                    return pb[2 + s_ // 3][:, (s_ % 3) * 129:(s_ % 3) * 129 + 129]
                for j in range(2):
                    hs = slice(j * 64, (j + 1) * 64)
                    for ki, kt in enumerate(keyt):
                        sp_ = pb[ei % 2]
                        spn = pbn[ei % 2]
                        pt = PT[ei % 3]
                        ptn = "PT%d" % (ei % 3)
                        MM(sp_[:, 0:n], [(kT[hs, kt * 128:(kt + 1) * 128], qT[hs, t0:t0 + n])],
                           ["kT%d" % kgroup(kt), "qT%d" % gi], [spn])
                        ACT(pt[:, 0:n], sp_[:, 0:n], AF.Exp, [spn], [ptn], scale=0.125)

                        def pv(e, j=j, kt=kt, ki=ki, pt=pt, nq=nq, nk=len(keyt), oslot=oslot):
                            seen_b = set()
                            r_ = []
                            for qi in range(nq):
                                bank = 2 + (j * nq + qi) // 3
                                st_ = (ki == 0) and (bank not in seen_b)
                                seen_b.add(bank)
                                r_.append(e.matmul(oslot(j, qi), pt[:, qi * 128:(qi + 1) * 128], vaug3[:, kt, 0:129],
                                                   start=st_, stop=(ki == nk - 1), skip_group_check=True))
                            return r_
                        P.op("pe", pv, [ptn, "vaug%d" % kt, "vaug_ones"], ["pb2", "pb3", "pb4"])
                        ei += 1
                for qi, p in enumerate(tiles):
                    which = 1 if p < 2 else 0
                    O0 = oslot(0, qi)
                    O1 = oslot(1, qi)
                    b0n = pbn[2 + (0 * nq + qi) // 3]
                    b1n = pbn[2 + (1 * nq + qi) // 3]
                    DVE(lambda e, O0=O0: e.reciprocal(fst[:, 0:1], O0[:, 128:129]), [b0n], ["f0"])
                    DVE(lambda e, O1=O1: e.reciprocal(fst[:, 1:2], O1[:, 128:129]), [b1n], ["f1"])
                    DVE(lambda e: e.tensor_tensor(fst[:, 2:3], fst[:, 1:2], nlam, ALU.mult), ["f1", "nlam"], ["f2"])
                    DVE(lambda e, O0=O0: e.tensor_scalar(o32, O0[:, 0:128], fst[:, 0:1], None, ALU.mult), [b0n, "f0"], ["o32"])
                    DVE(lambda e, O1=O1: e.scalar_tensor_tensor(out=o32, in0=O1[:, 0:128], scalar=fst[:, 2:3], in1=o32,
                                                              op0=ALU.mult, op1=ALU.add), [b1n, "f2", "o32"], ["o32"])
                    ACT(junk, o32, AF.Square, ["o32"], ["junkB", "f3"], accum_out=fst[:, 3:4])
                    ACT(fst[:, 4:5], fst[:, 3:4], AF.Sqrt, ["f3"], ["f4"], scale=1.0 / 128, bias=EPS)
                    DVE(lambda e: e.reciprocal(fst[:, 5:6], fst[:, 4:5]), ["f4"], ["f5"])
                    DVE(lambda e, p=p: e.scalar_tensor_tensor(out=yb, in0=o32, scalar=fst[:, 5:6], in1=szw3[:, p, :],
                                                             op0=ALU.mult, op1=ALU.mult), ["o32", "f5", "szw%d" % p], ["yb"])
                    pT = pb[5].bitcast(BF16)
                    TRS([(pT[:, 0:128], yb)], ["yb", "ident_b"], ["pb5"], ident_b)
                    ACT(yTb, pT[:, 0:128], AF.Copy, ["pb5"], ["yTb"])
                    for half in range(2):
                        MM(pb[6 + half], [(yTb, woB3[:, hp, half * 512:(half + 1) * 512])], ["yTb", "woB"], [pbn[6 + half]])
                    x_update(p, which, pb[6], "pb6", pb[7], "pb7", tmp, "tmpB")
        P.barrier()

    TC = T + 4

    def coff(p):
        return p * 128 if p < 2 else 4 + p * 128

    def phaseC(b, l, ctx_out):
        arena.reset()
        A = arena
        convw = A.alloc(30)
        alog = A.alloc(8)
        dtb = A.alloc(8)
        nea = A.alloc(8)
        cn4 = A.alloc(256)
        gm = A.alloc(8 * 128)
        gm3 = gm.rearrange("p (m c) -> p m c", m=8)
        sameb = A.alloc(128, BF16)
        negones = A.alloc(128)
        woC = A.alloc(2 * DM, BF16)
        woC3 = woC.rearrange("p (k n) -> p k n", k=2)
        qkvT = A.alloc(6 * TC, BF16)
        qkvT3 = qkvT.rearrange("p (c t) -> p c t", c=6)
        szc = A.alloc(NT * 256, BF16)
        szc3 = szc.rearrange("p (t c) -> p t c", t=NT)
        g_all = A.alloc(NT * 8)
        g3 = g_all.rearrange("p (t c) -> p t c", t=NT)
        beta_all = A.alloc(NT * 8)
        beta3 = beta_all.rearrange("p (t c) -> p t c", t=NT)
        lnb_all = A.alloc(NT * 8)
        lnb3 = lnb_all.rearrange("p (t c) -> p t c", t=NT)
        mark = A.off
        xin = A.alloc(TC + 4, BF16)
        acc = A.alloc(TC)
        sqb = xin[:, 0:TC]
        wC = [A.alloc(8 * 128, BF16) for _ in range(2)]
        wCz = A.alloc(8 * 272, BF16)
        wCz3 = wCz.rearrange("p (k n) -> p k n", k=8)
        rin = [A.alloc(512) for _ in range(2)]
        szt = A.alloc(256)
        gt = A.alloc(64)
        DMA("sp", convw, conv_wT_d[l], [], ["convw"])
        DMA("sp", alog, c_alog_d[l:l + 1, :].partition_broadcast(128), [], ["alog"])
        DMA("sp", dtb, c_dtb_d[l:l + 1, :].partition_broadcast(128), [], ["dtb"])
        for h in range(4):
            DMA("sp", cn4[:, h * 64:(h + 1) * 64], c_normw_d[l:l + 1, :].partition_broadcast(128), [], ["cn4"])
        DMA("sp", gm, gmask_d, [], ["gm"])
        DMA("pool", sameb, gmask_d[:, 256:384], [], ["sameb"])
        POOL(lambda e: e.memset(negones, -1.0), [], ["negones"])
        DMA("pool", woC3, w_out_d[l, 768:1024, :].rearrange("(k p) n -> p k n", p=128), [], ["woC"])
        POOL(lambda e: e.memset(xin, 0.0), [], ["xin"])
        ACT(nea, alog, AF.Exp, ["alog"], ["nea"])
        DVE(lambda e: e.tensor_scalar(nea, nea, -1.0, None, ALU.mult), ["nea"], ["nea"])
        wsrc = w_in_d[l].rearrange("(k p) n -> p k n", p=128)
        DMA("pool", wCz3, wsrc[:, :, 3584:3856], [], ["wCz"])
        groups = [(0, 256, [0, 1])] + [(256 + 512 * g, 512, [2 + 4 * g + i for i in range(4)]) for g in range(4)]
        convw3 = convw.rearrange("p (c j) -> p c j", c=6)
        for ct in range(6):
            w_ = wC[ct % 2]
            w3 = w_.rearrange("p (k n) -> p k n", k=8)
            wn = "wC%d" % (ct % 2)
            DMA("pool", w3, wsrc[:, :, 2816 + ct * 128:2816 + (ct + 1) * 128], [], [wn])
            if ct > 0:
                POOL(lambda e: e.memset(xin, 0.0), [], ["xin"])
            for gi, (t0, n, tiles) in enumerate(groups):
                pp = pb[gi % 2]
                ppn = pbn[gi % 2]
                MM(pp[:, 0:n], [(w3[:, k, :], hT3[:, k, t0:t0 + n]) for k in range(8)], [wn] + ["hT%d" % p for p in tiles], [ppn])
                c0 = 2 + (t0 if gi == 0 else t0 + 4)
                ACT(xin[:, c0:c0 + n], pp[:, 0:n], AF.Copy, [ppn], ["xin"])
            DVE(lambda e, ct=ct: e.tensor_scalar(acc, xin[:, 0:TC], convw3[:, ct, 0:1], None, ALU.mult), ["xin", "convw"], ["acc"])
            for j in range(1, 5):
                DVE(lambda e, ct=ct, j=j: e.scalar_tensor_tensor(out=acc, in0=xin[:, j:j + TC], scalar=convw3[:, ct, j:j + 1], in1=acc,
                                                                 op0=ALU.mult, op1=ALU.add), ["xin", "convw", "acc"], ["acc"])
            if ct >= 4:
                ACT(qkvT3[:, ct, :], acc, AF.Silu, ["acc"], ["qkvT%d" % ct])
            else:
                ACT(acc, acc, AF.Silu, ["acc"], ["acc"])
                POOL(lambda e: e.tensor_tensor(sqb, acc, acc, ALU.mult), ["acc"], ["xin"])
                for ci, c0 in enumerate(range(0, TC, 512)):
                    n = min(512, TC - c0)
                    pp = pb[2 + ci % 2]
                    ppn = pbn[2 + ci % 2]
                    r_ = rin[ci % 2]
                    rn = "rin%d" % (ci % 2)
                    MM(pp[:, 0:n], [(sameb, sqb[:, c0:c0 + n])], ["sameb", "xin"], [ppn])
                    ACT(r_[:, 0:n], pp[:, 0:n], AF.Sqrt, [ppn], [rn], bias=1e-6)
                    DVE(lambda e, r_=r_, n=n: e.reciprocal(r_[:, 0:n], r_[:, 0:n]), [rn], [rn])
                    sc = 0.125 if ct < 2 else 1.0
                    DVE(lambda e, ct=ct, c0=c0, n=n, r_=r_, sc=sc: e.scalar_tensor_tensor(
                        out=qkvT3[:, ct, c0:c0 + n], in0=acc[:, c0:c0 + n], scalar=sc, in1=r_[:, 0:n], op0=ALU.mult, op1=ALU.mult),
                        ["acc", rn], ["qkvT%d" % ct])
        for p in range(NT):
            tok = slice(p * 128, (p + 1) * 128)
            pp = pb[4 + p % 2]
            ppn = pbn[4 + p % 2]
            MM(pp[:, 0:272], [(hT3[:, k, tok], wCz3[:, k, :]) for k in range(8)], ["hT%d" % p, "wCz"], [ppn])
            if p >= 2 or ctx_out:
                ACT(szt, pp[:, 0:256], AF.Silu, [ppn], ["szt"])
                DVE(lambda e, p=p: e.tensor_tensor(szc3[:, p, :], szt, cn4, ALU.mult), ["szt", "cn4"], ["szc%d" % p])
            ACT(gt[:, 0:8], pp[:, 256:264], AF.Exp, [ppn], ["gt0"], scale=-1.0)
            DVE(lambda e: e.tensor_scalar(gt[:, 0:8], gt[:, 0:8], 1.0, None, ALU.add), ["gt0"], ["gt0"])
            DVE(lambda e, p=p: e.reciprocal(beta3[:, p, :], gt[:, 0:8]), ["gt0"], ["gates%d" % p])
            ACT(gt[:, 8:16], gt[:, 0:8], AF.Ln, ["gt0"], ["gt1"])
            DVE(lambda e, p=p: e.tensor_scalar(lnb3[:, p, :], gt[:, 8:16], -1.0, None, ALU.mult), ["gt1"], ["gates%d" % p])
            DVE(lambda e, pp=pp: e.tensor_tensor(gt[:, 16:24], pp[:, 264:272], dtb, ALU.add), [ppn, "dtb"], ["gt2"])
            ACT(gt[:, 24:32], gt[:, 16:24], AF.Exp, ["gt2"], ["gt3"])
            ACT(gt[:, 32:40], gt[:, 24:32], AF.Ln, ["gt3"], ["gt4"], bias=1.0)
            DVE(lambda e, p=p: e.tensor_tensor(g3[:, p, :], gt[:, 32:40], nea, ALU.mult), ["gt4", "nea"], ["gates%d" % p])
        dump("qkvT", qkvT, ["qkvT%d" % c for c in range(6)])
        P.barrier()
        if cfg.get("cstop", 9) <= 2:
            return
        A.off = mark
        H = Arena(hT.bitcast(F32))
        o_acc = H.alloc(NT * 256)
        o3 = o_acc.rearrange("p (t c) -> p t c", t=NT)
        tokb = [A.alloc(768, BF16) for _ in range(2)]
        bv = [A.alloc(256, BF16) for _ in range(2)]
        bkg = [A.alloc(256, BF16) for _ in range(2)]
        kdec = [A.alloc(512, BF16) for _ in range(2)]
        qdec = [A.alloc(512, BF16) for _ in range(2)]
        sm = [A.alloc(64) for _ in range(2)]
        GT = [A.alloc(128) for _ in range(4)]
        dm = GT
        dec2 = [H.alloc(256) for _ in range(4)]
        LA = dec2
        NA = [H.alloc(128) for _ in range(4)]
        LN = [[H.alloc(256) for _ in range(2)] for _ in range(4)]
        Pm = [[H.alloc(128) for _ in range(2)] for _ in range(4)]
        TTb = [A.alloc(128, BF16) for _ in range(4)]
        u_sb = [A.alloc(256) for _ in range(2)]
        wT_sb = [A.alloc(512, BF16) for _ in range(2)]
        qdT_sb = [A.alloc(512, BF16) for _ in range(2)]
        atT_sb = [A.alloc(512, BF16) for _ in range(2)]
        S32 = [A.alloc(256) for _ in range(2)]
        Sb = [A.alloc(256, BF16) for _ in range(2)]
        vnew = [A.alloc(256, BF16) for _ in range(2)]
        ysq = A.alloc(256)
        yc = A.alloc(256, BF16)
        ycT = A.alloc(256, BF16)
        tmp = A.alloc(DM)
        for d in range(2):
            POOL(lambda e, d=d: e.memset(S32[d], 0.0), [], ["S32_%d" % d])
            POOL(lambda e, d=d: e.memset(Sb[d], 0.0), [], ["Sb_%d" % d])
        order = [list(range(NT)), [1, 0] + list(range(NT - 1, 1, -1))]
        visited = set()

        def prep(p, d):
            co = coff(p)
            tk = tokb[d]
            tk3 = tk.rearrange("p (c f) -> p c f", c=3)
            tkn = "tokb%d" % d
            s_ = sm[d]
            sn = "sm%d_" % d
            pT = pb[7].bitcast(BF16)
            TRS([(pT[:, c * 128:(c + 1) * 128], qkvT3[:, c, co:co + 128]) for c in range(6)],
                ["qkvT%d" % c for c in range(6)] + ["ident_b"], ["pb7"], ident_b)
            ACT(tk, pT[:, 0:768], AF.Copy, ["pb7"], [tkn])
            gsl = g3[:, p, d * 4:(d + 1) * 4]
            P.op("pe", lambda e, d=d, gsl=gsl: [
                e.matmul(pb[6][:, 0:4], gm3[:, d, :], gsl, start=True, stop=True),
                e.matmul(pb[6][:, 4:8], gm3[:, 2, :], gsl, start=True, stop=True),
                e.matmul(pb[6][:, 8:12], gm3[:, 3, :], gsl, start=True, stop=True)], ["gm", "gates%d" % p], ["pb6"])
            ACT(s_[:, 24:36], pb[6][:, 0:12], AF.Copy, ["pb6"], [sn + "raw"])
            ACT(s_[:, 0:4], s_[:, 24:28], AF.Exp, [sn + "raw"], [sn + "egc"])
            DVE(lambda e, s_=s_: e.tensor_tensor(s_[:, 20:24], s_[:, 28:32], s_[:, 24:28], ALU.subtract), [sn + "raw"], [sn + "t"])
            ACT(s_[:, 4:8], s_[:, 20:24], AF.Exp, [sn + "t"], [sn + "ekd"])
            ACT(s_[:, 8:16], s_[:, 28:36], AF.Exp, [sn + "raw"], [sn + "egl"])
            if cfg.get("pstop", 9) <= 1:
                return
            bsl = beta3[:, p, d * 4:(d + 1) * 4]
            DVE(lambda e, s_=s_, bsl=bsl: e.tensor_tensor(s_[:, 16:20], s_[:, 0:4], bsl, ALU.mult), [sn + "egc", "gates%d" % p], [sn + "bgk"])
            kt3 = tk3[:, 1, :].rearrange("p (h f) -> p h f", h=4)
            vt3 = tk3[:, 2, :].rearrange("p (h f) -> p h f", h=4)
            qt3 = tk3[:, 0, :].rearrange("p (h f) -> p h f", h=4)
            bc = lambda a: a.unsqueeze(2).broadcast_to([128, 4, 64])
            v4 = lambda a: a.rearrange("p (h f) -> p h f", h=4)
            DVE(lambda e, d=d, vt3=vt3, bsl=bsl: e.tensor_tensor(v4(bv[d]), vt3, bc(bsl), ALU.mult), [tkn, "gates%d" % p], ["bv%d" % d])
            DVE(lambda e, d=d, kt3=kt3, s_=s_: e.tensor_tensor(v4(bkg[d]), kt3, bc(s_[:, 16:20]), ALU.mult), [tkn, sn + "bgk"], ["bkg%d" % d])
            v42 = lambda a: a.rearrange("p (h r f) -> p h r f", h=4, r=2)
            bc2 = lambda a: a.unsqueeze(2).unsqueeze(3).broadcast_to([128, 4, 2, 64])
            dup = lambda a3: a3.unsqueeze(2).broadcast_to([128, 4, 2, 64])
            POOL(lambda e, d=d, kt3=kt3, s_=s_: e.tensor_tensor(v42(kdec[d]), dup(kt3), bc2(s_[:, 4:8]), ALU.mult), [tkn, sn + "ekd"], ["kdec%d" % d])
            POOL(lambda e, d=d, qt3=qt3, s_=s_: e.tensor_tensor(v42(qdec[d]), dup(qt3), bc2(s_[:, 0:4]), ALU.mult), [tkn, sn + "egc"], ["qdec%d" % d])
            if cfg.get("pstop", 9) <= 2:
                return
            P.op("pe", lambda e, d=d: [e.matmul(pb[5][:, h * 128:(h + 1) * 128], qdec[d][:, h * 128:(h + 1) * 128], ident_b,
                                                start=True, stop=True, skip_group_check=True) for h in range(4)],
                 ["qdec%d" % d, "ident_b"], ["pb5"])
            ACT(qdT_sb[d], pb[5], AF.Copy, ["pb5"], ["qdT%d" % d])
            if cfg.get("pstop", 9) <= 3:
                return
            for h in range(4):
                ctk = 2 + h // 2
                ctq = h // 2
                hs = slice((h % 2) * 64, (h % 2) * 64 + 64)
                kTh = qkvT3[hs, ctk, co:co + 128]
                qTh = qkvT3[hs, ctq, co:co + 128]
                hn = "%d" % h
                DVE(lambda e, h=h, d=d: e.tensor_scalar(GT[h], gm3[:, d, :], g3[:, p, d * 4 + h:d * 4 + h + 1], None, ALU.mult),
                    ["gm", "gates%d" % p], ["GT" + hn])
                pd = pb[h][:, 0:128]
                pga = pb[4 + h][:, 0:256]
                P.op("pe", lambda e, h=h, pd=pd: [e.matmul(pd, GT[h], ones_f, start=True, stop=False, skip_group_check=True),
                                                  e.matmul(pd, negones, GT[h], start=False, stop=True, skip_group_check=True)],
                     ["GT" + hn, "ones_f", "negones"], [pbn[h]])
                P.op("pe", lambda e, pga=pga, kTh=kTh, qTh=qTh: [e.matmul(pga[:, 0:128], kTh, kTh, start=True, stop=True, skip_group_check=True),
                                                                e.matmul(pga[:, 128:256], qTh, kTh, start=True, stop=True, skip_group_check=True)],
                     ["qkvT%d" % ctk, "qkvT%d" % ctq], [pbn[4 + h]])
                DVE(lambda e, h=h, pd=pd: e.tensor_scalar(dm[h], pd, 0.0, None, ALU.min), [pbn[h]], ["GT" + hn])
                ACT(dec2[h][:, 0:128], dm[h], AF.Exp, ["GT" + hn, "gates%d" % p], ["LA" + hn], bias=lnb3[:, p, d * 4 + h:d * 4 + h + 1])
                ACT(dec2[h][:, 128:256], dm[h], AF.Exp, ["GT" + hn], ["LA" + hn])
                POOL(lambda e, h=h, d=d: e.tensor_tensor(dec2[h], dec2[h], gm[:, (4 + 2 * d) * 128:(6 + 2 * d) * 128], ALU.mult),
                     ["LA" + hn, "gm"], ["LA" + hn])
                DVE(lambda e, h=h, pga=pga: e.tensor_tensor(LA[h], pga, dec2[h], ALU.mult), [pbn[4 + h], "LA" + hn], ["LA" + hn])
            if cfg.get("pstop", 9) <= 4:
                return
            for h in range(4):
                hn = "%d" % h
                pt_ = pb[h][:, 0:256]
                TRS([(pt_[:, 0:128], LA[h][:, 0:128]), (pt_[:, 128:256], LA[h][:, 128:256])], ["LA" + hn, "ident_f"], [pbn[h]], ident_f)
                ACT(NA[h], pt_[:, 0:128], AF.Copy, [pbn[h]], ["NA" + hn])
                ACT(atT_sb[d][:, h * 128:(h + 1) * 128], pt_[:, 128:256], AF.Copy, [pbn[h]], ["atT%d" % d])
                DVE(lambda e, h=h: e.tensor_tensor(Pm[h][0], ident_f, NA[h], ALU.subtract), ["NA" + hn, "ident_f"], ["P0_" + hn])
            if cfg.get("pstop", 9) <= 5:
                return
            curL = [LA[h][:, 0:128] for h in range(4)]
            curN = [NA[h] for h in range(4)]
            curLn = ["LA%d" % h for h in range(4)]
            curNn = ["NA%d" % h for h in range(4)]
            for lev in range(5):
                last = lev == 4
                for h in range(4):
                    hn = "%d" % h
                    pln = pb[4 + h][:, 0:256]
                    dst = LN[h][lev % 2]
                    dn = "LN%d_%d" % (h, lev % 2)
                    if not last:
                        P.op("pe", lambda e, pln=pln, h=h, cl=curL[h], cn=curN[h]: [
                            e.matmul(pln[:, 0:128], cn, cl, start=True, stop=True, skip_group_check=True),
                            e.matmul(pln[:, 128:256], cl, cn, start=True, stop=True, skip_group_check=True)],
                            [curLn[h], curNn[h]], [pbn[4 + h]])
                        ACT(dst, pln, AF.Copy, [pbn[4 + h]], [dn])
                    else:
                        P.op("pe", lambda e, pln=pln, h=h, cl=curL[h], cn=curN[h]: e.matmul(pln[:, 0:128], cn, cl, start=True, stop=True, skip_group_check=True),
                             [curLn[h], curNn[h]], [pbn[4 + h]])
                        ACT(dst[:, 0:128], pln[:, 0:128], AF.Copy, [pbn[4 + h]], [dn])
                    curL[h] = dst[:, 0:128]
                    curN[h] = dst[:, 128:256]
                    curLn[h] = dn
                    curNn[h] = dn
                for h in range(4):
                    hn = "%d" % h
                    ppd = pb[h][:, 0:128]
                    src = Pm[h][lev % 2]
                    dstp = Pm[h][(lev + 1) % 2]
                    P.op("pe", lambda e, ppd=ppd, cl=curL[h], src=src: [
                        e.matmul(ppd, ident_f, src, start=True, stop=False, skip_group_check=True),
                        e.matmul(ppd, cl, src, start=False, stop=True, skip_group_check=True)],
                         [curLn[h], "P%d_" % (lev % 2) + hn, "ident_f"], [pbn[h]])
                    if lev < 4:
                        ACT(dstp, ppd, AF.Copy, [pbn[h]], ["P%d_" % ((lev + 1) % 2) + hn])
                    else:
                        ACT(TTb[h], ppd, AF.Copy, [pbn[h]], ["TTb" + hn])
            if cfg.get("pstop", 9) <= 6:
                return
            for h in range(4):
                hn = "%d" % h
                TT = TTb[h]
                P.op("pe", lambda e, h=h, d=d, TT=TT: [
                    e.matmul(pb[4][:, h * 64:(h + 1) * 64], TT, bv[d][:, h * 64:(h + 1) * 64], start=True, stop=True, skip_group_check=True),
                    e.matmul(pb[6][0:64, h * 128:(h + 1) * 128], bkg[d][:, h * 64:(h + 1) * 64], TT, start=True, stop=True, skip_group_check=True)],
                    ["TTb" + hn, "bv%d" % d, "bkg%d" % d], ["pb4", "pb6"])
            ACT(u_sb[d], pb[4][:, 0:256], AF.Copy, ["pb4"], ["u_sb%d" % d])
            DVE(lambda e, d=d: e.tensor_scalar(wT_sb[d][0:64, :], pb[6][0:64, 0:512], 1.0, None, ALU.mult), ["pb6"], ["wT%d" % d])

        def scan(p, d, need_o):
            s_ = sm[d]
            sn = "sm%d_" % d
            chunks = [0, 1] if d == 0 else [1, 0]
            S3 = S32[d].rearrange("p (h f) -> p h f", h=4)
            for cc in chunks:
                rs = slice(cc * 64, cc * 64 + 64)
                P.op("pe", lambda e, d=d: [e.matmul(pb[4][:, h * 64:(h + 1) * 64], wT_sb[d][0:64, h * 128:(h + 1) * 128],
                                                    Sb[d][0:64, h * 64:(h + 1) * 64], start=True, stop=True, skip_group_check=True)
                                           for h in range(4)], ["wT%d" % d, "Sb_%d" % d], ["pb4"])
                DVE(lambda e, d=d, rs=rs: e.tensor_tensor(vnew[d][rs, :], u_sb[d][rs, :], pb[4][rs, 0:256], ALU.subtract),
                    ["pb4", "u_sb%d" % d], ["vnew%d" % d])
                if need_o:
                    def omm(e, d=d, rs=rs):
                        r_ = []
                        for h in range(4):
                            o_ = pb[5][:, h * 64:(h + 1) * 64]
                            r_.append(e.matmul(o_, qdT_sb[d][rs, h * 128:(h + 1) * 128], Sb[d][rs, h * 64:(h + 1) * 64],
                                               start=True, stop=False, skip_group_check=True))
                            r_.append(e.matmul(o_, atT_sb[d][rs, h * 128:(h + 1) * 128], vnew[d][rs, h * 64:(h + 1) * 64],
                                               start=False, stop=True, skip_group_check=True))
                        return r_
                    P.op("pe", omm, ["qdT%d" % d, "Sb_%d" % d, "atT%d" % d, "vnew%d" % d], ["pb5"])
                    key = (p, cc)
                    if key not in visited:
                        visited.add(key)
                        ACT(o3[rs, p, :], pb[5][rs, 0:256], AF.Copy, ["pb5"], ["oacc%d_%d" % (p, cc)])
                    else:
                        DVE(lambda e, rs=rs: e.tensor_tensor(o3[rs, p, :], o3[rs, p, :], pb[5][rs, 0:256], ALU.add),
                            ["pb5", "oacc%d_%d" % (p, cc)], ["oacc%d_%d" % (p, cc)])
                P.op("pe", lambda e, d=d, rs=rs: [e.matmul(pb[6][:, h * 64:(h + 1) * 64], kdec[d][rs, h * 128:(h + 1) * 128],
                                                          vnew[d][rs, h * 64:(h + 1) * 64], start=True, stop=True, skip_group_check=True)
                                                 for h in range(4)], ["kdec%d" % d, "vnew%d" % d], ["pb6"])
                for half in range(2):
                    hr = slice(half * 64, half * 64 + 64)
                    eg = s_[hr, 8:12] if cc == half else s_[hr, 12:16]
                    DVE(lambda e, S3=S3, eg=eg, hr=hr: e.tensor_tensor(S3[hr], S3[hr], eg.unsqueeze(2).broadcast_to([64, 4, 64]), ALU.mult),
                        [sn + "egl", "S32_%d" % d], ["S32_%d" % d])
                DVE(lambda e, d=d: e.tensor_tensor(S32[d], S32[d], pb[6][:, 0:256], ALU.add), ["pb6", "S32_%d" % d], ["S32_%d" % d])
                ACT(Sb[d], S32[d], AF.Copy, ["S32_%d" % d], ["Sb_%d" % d])

        for step in range(cfg.get("nsteps", NT)):
            for d in range(2):
                p = order[d][step]
                prep(p, d)
                if cfg.get("cstop", 9) <= 3:
                    continue
                scan(p, d, (p >= 2) or ctx_out)
        if cfg.get("cstop", 9) <= 4:
            P.barrier()
            return
        dump("oacc", o_acc, ["oacc%d_%d" % (p, cc) for p in range(2, NT) for cc in range(2)])
        for p in range(NT):
            if p < 2 and not ctx_out:
                continue
            which = 1 if p < 2 else 0
            on = ["oacc%d_%d" % (p, cc) for cc in range(2)]
            ov = o3[:, p, :]
            DVE(lambda e, ov=ov: e.tensor_tensor(ysq, ov, ov, ALU.mult), on, ["ysq"])
            DVE(lambda e: e.reduce_sum(sm[0][:, 44:48], ysq.rearrange("p (h f) -> p h f", h=4), AX.X), ["ysq"], ["yss"])
            ACT(sm[0][:, 48:52], sm[0][:, 44:48], AF.Sqrt, ["yss"], ["ysd"], scale=1.0 / 64, bias=EPS)
            DVE(lambda e: e.reciprocal(sm[0][:, 52:56], sm[0][:, 48:52]), ["ysd"], ["yrs"])
            DVE(lambda e, ov=ov: e.tensor_tensor(ysq.rearrange("p (h f) -> p h f", h=4), ov.rearrange("p (h f) -> p h f", h=4),
                                                 sm[0][:, 52:56].unsqueeze(2).broadcast_to([128, 4, 64]), ALU.mult), on + ["yrs", "ysq"], ["ysq"])
            DVE(lambda e, p=p: e.tensor_tensor(yc, ysq, szc3[:, p, :], ALU.mult), ["ysq", "szc%d" % p], ["yc"])
            pT = pb[0].bitcast(BF16)
            TRS([(pT[:, c * 128:(c + 1) * 128], yc[:, c * 128:(c + 1) * 128]) for c in range(2)], ["yc", "ident_b"], ["pb0"], ident_b)
            ACT(ycT, pT[:, 0:256], AF.Copy, ["pb0"], ["ycT"])
            for half in range(2):
                MM(pb[1 + half], [(ycT[:, c * 128:(c + 1) * 128], woC3[:, c, half * 512:(half + 1) * 512]) for c in range(2)],
                   ["ycT", "woC"], [pbn[1 + half]])
            x_update(p, which, pb[1], "pb1", pb[2], "pb2", tmp, "tmpC")
        P.barrier()

    def final(b):
        arena.reset()
        fnw_bc = arena.alloc(DM)
        ostages = [arena.alloc(DM) for _ in range(2)]
        DMA("sp", fnw_bc, fnw_d[0:1, :].partition_broadcast(128), [], ["fnw_bc"])
        for p in range(2, NT):
            ostage = ostages[p % 2]
            osn = "ostage%d" % (p % 2)
            xs = X3[:, p, :]
            ss = stat[:, 0:1]
            sd = stat[:, 1:2]
            rstd = stat[:, 2:3]
            ACT(xn, xs, AF.Square, ["X%d" % p], ["xn", "st0"], accum_out=ss)
            ACT(sd, ss, AF.Sqrt, ["st0"], ["st1"], scale=1.0 / DM, bias=EPS)
            DVE(lambda e: e.reciprocal(rstd, sd), ["st1"], ["st2"])
            DVE(lambda e, xs=xs, ostage=ostage: e.scalar_tensor_tensor(out=ostage, in0=xs, scalar=rstd, in1=fnw_bc, op0=ALU.mult, op1=ALU.mult),
                ["X%d" % p, "st2", "fnw_bc"], [osn])
            DMA("sp", out_d[b, (p - 2) * 128:(p - 1) * 128, :], ostage, [osn], [])

    for b in range(NB):
        for l in range(nlayers):
            ctx_out = l < nlayers - 1
            mod_prep(b, l, ctx_out)
            phase1(b, l)
            P.barrier()
            if useA:
                phaseA(b, l, ctx_out)
            if useB:
                phaseB(b, l, ctx_out)
            if useC:
                phaseC(b, l, ctx_out)
        final(b)
        P.barrier()
    stats = P.emit()
    return nc, stats


def host_consts():
    ident = np.eye(128, dtype=np.float32)
    perm = np.zeros((128, 128), np.float32)
    cos = np.zeros((128, SEQ), np.float32)
    sin = np.zeros((128, SEQ), np.float32)
    t = np.arange(SEQ)
    rowp = (t // 64).astype(np.float32)
    colp = (t % 64).astype(np.float32)
    inv_freq = (10000.0 ** (-np.arange(0, 32, 2, dtype=np.float32) / 32.0)).astype(np.float32)
    for pp in range(128):
        d = pp % 64
        half = d // 32
        dd = d % 32
        f = dd % 16
        pos = rowp if half == 0 else colp
        ang = (pos * inv_freq[f]).astype(np.float32)
        cos[pp] = np.cos(ang)
        sgn = -1.0 if dd < 16 else 1.0
        sin[pp] = sgn * np.sin(ang)
        partner = pp + 16 if dd < 16 else pp - 16
        perm[partner, pp] = 1.0
    r = np.arange(128)[:, None]
    c = np.arange(128)[None, :]
    same = (r // 64) == (c // 64)
    tri_f = same & (r <= c)
    tri_b = same & (r >= c)
    gm = np.stack([tri_f, tri_b, same, ~same, same & (r > c), same & (r >= c), same & (r < c), same & (r <= c)], 0).astype(np.float32)
    return ident, perm, cos, sin, gm.transpose(1, 0, 2).reshape(128, 8 * 128).copy()


def make_in_maps(inputs, NB, ncores):
    ident, perm, cos, sin, gm = host_consts()
    f = lambda a: np.ascontiguousarray(np.asarray(a, dtype=np.float32))
    c = f(inputs["c"])
    c_ctx = f(inputs["c_ctx"])
    shared = {
        "w_ada": f(inputs["w_ada"]),
        "b_adaT": f(f(inputs["b_ada"]).reshape(2, 24, 128).transpose(2, 0, 1).reshape(128, 48)),
        "norm_wT": f(f(inputs["norm_w"]).reshape(2, 8, 128).transpose(2, 0, 1).reshape(128, 16)),
        "w_in": f(inputs["w_in"]),
        "w_out": f(inputs["w_out"]),
        "a_ln_w": f(inputs["a_ln_w"]),
        "a_wsT": f(f(inputs["a_ws"]).transpose(0, 3, 1, 2).reshape(2, 128, 512)),
        "a_bsT": f(f(inputs["a_bs"]).transpose(0, 2, 1)),
        "b_lam": f(f(inputs["b_lam"]).reshape(2, 256)),
        "b_subln_w": f(inputs["b_subln_w"]),
        "conv_wT": f(f(inputs["c_conv_w"]).reshape(2, 5, 6, 128).transpose(0, 3, 2, 1).reshape(2, 128, 30)),
        "c_a_log": f(f(inputs["c_a_log"]).reshape(2, 8)),
        "c_dt_bias": f(f(inputs["c_dt_bias"]).reshape(2, 8)),
        "c_norm_w": f(inputs["c_norm_w"]),
        "final_norm_w": f(f(inputs["final_norm_w"]).reshape(1, DM)),
        "ident": ident, "perm": perm, "ropecos": cos, "ropesin": sin, "gmasks": gm,
    }
    x = inputs["x"]
    ctx = inputs["ctx"]
    maps = []
    for i in range(ncores):
        b0 = i * NB
        rows = np.concatenate([c[b0:b0 + NB], c_ctx[None, :]], 0)
        cT = rows.reshape(NB + 1, 8, 128).transpose(2, 1, 0).reshape(128, 8 * (NB + 1))
        m = dict(shared)
        m["x"] = f(x[b0:b0 + NB])
        m["ctx"] = f(ctx[b0:b0 + NB])
        m["cT"] = f(cT)
        maps.append(m)
    return maps


_CACHE = {}


def kernel(**inputs):
    NB = 4
    key = ("full", NB)
    if key not in _CACHE:
        _CACHE[key] = build(NB, {})
    nc, _ = _CACHE[key]
    maps = make_in_maps(inputs, NB, NCORES)
    res = run_bass_kernel_spmd(nc, maps, core_ids=list(range(NCORES)))
    out = np.concatenate([r["out"] for r in res.results], axis=0)
    return out.astype(np.float32)
```

```python
import math
import numpy as np
import concourse.bass as bass
import concourse.mybir as mybir
from concourse.bass_utils import run_bass_kernel_spmd

F32 = mybir.dt.float32
BF16 = mybir.dt.bfloat16
AF = mybir.ActivationFunctionType
ALU = mybir.AluOpType
AX = mybir.AxisListType

NCORES = 8
DM = 1024
SEQ = 2048
CTXL = 256
T = SEQ + CTXL
NT = T // 128
DIN = 3856
EPS = 1e-6


class _Res:
    __slots__ = ("last_w", "readers")

    def __init__(self):
        self.last_w = None
        self.readers = []


class _Op:
    __slots__ = ("eng", "fn", "deps", "dma", "idx", "sig", "needed", "dsem", "dval", "dprev")

    def __init__(self, eng, fn, deps, dma, idx):
        self.eng = eng
        self.fn = fn
        self.deps = deps
        self.dma = dma
        self.idx = idx
        self.sig = None
        self.needed = False
        self.dsem = None
        self.dval = 0
        self.dprev = 0


class Prog:
    ENGS = ("pe", "act", "dve", "pool", "sp")

    def __init__(self, nc, n_dma_sems=8):
        self.nc = nc
        self.ops = []
        self.res = {}
        self.n_dma_sems = n_dma_sems
        self.cur_barrier = None
        self.last_on = {}
        self.dmas_since = []

    def _st(self, r):
        s = self.res.get(r)
        if s is None:
            s = self.res[r] = _Res()
        return s

    def op(self, eng, fn, reads=(), writes=(), dma=False):
        idx = len(self.ops)
        deps = set()
        for r in reads:
            st = self._st(r)
            if st.last_w is not None:
                deps.add(st.last_w)
        for w in writes:
            st = self._st(w)
            if st.last_w is not None:
                deps.add(st.last_w)
            deps.update(st.readers)
        for r in reads:
            self._st(r).readers.append(idx)
        for w in writes:
            st = self._st(w)
            st.last_w = idx
            st.readers = []
        if self.cur_barrier is not None:
            deps.add(self.cur_barrier)
        deps.discard(idx)
        self.ops.append(_Op(eng, fn, deps, dma, idx))
        if dma:
            self.dmas_since.append(idx)
        else:
            self.last_on[eng] = idx
        return idx

    def dma(self, eng, fn, reads=(), writes=()):
        return self.op(eng, fn, reads, writes, dma=True)

    def barrier(self):
        deps = set(self.last_on.values()) | set(self.dmas_since)
        idx = len(self.ops)
        o = _Op("pool", lambda e: e.nop(), deps, False, idx)
        self.ops.append(o)
        self.cur_barrier = idx
        self.last_on = {"pool": idx}
        self.dmas_since = []
        return idx

    def emit(self):
        nc = self.nc
        ops = self.ops
        for o in ops:
            for d in o.deps:
                ops[d].needed = True
        cnt = {e: 0 for e in self.ENGS}
        for o in ops:
            if o.dma:
                continue
            if o.needed:
                cnt[o.eng] += 1
                o.sig = cnt[o.eng]
        dcount = {}
        dma_n = {e: 0 for e in self.ENGS}
        for o in ops:
            if not o.dma:
                continue
            k = dma_n[o.eng] % self.n_dma_sems
            dma_n[o.eng] += 1
            key = (o.eng, k)
            o.dsem = key
            o.dprev = dcount.get(key, 0)
            o.dval = o.dprev + 16
            dcount[key] = o.dval
        sem = {}
        self._handles = []
        for e in self.ENGS:
            h = nc.semaphore("sg_" + e)
            self._handles.append(h)
            sem[e] = h.__enter__()
        for key in dcount:
            h = nc.semaphore("sd_%s_%d" % key)
            self._handles.append(h)
            sem[key] = h.__enter__()
        per_eng = {e: [o for o in ops if o.eng == e] for e in self.ENGS}
        stats = {"ops": len(ops), "waits": 0, "sig": dict(cnt)}

        def run_engine(ename, eng):
            seen = {}
            for o in per_eng[ename]:
                waits = {}
                for d in o.deps:
                    p = ops[d]
                    if p.dma:
                        key, val = p.dsem, p.dval
                    else:
                        if p.eng == ename and ename == "pe":
                            continue
                        key, val = p.eng, p.sig
                    if seen.get(key, 0) >= val:
                        continue
                    if waits.get(key, 0) < val:
                        waits[key] = val
                if o.dma and o.dprev > 0 and seen.get(o.dsem, 0) < o.dprev:
                    if waits.get(o.dsem, 0) < o.dprev:
                        waits[o.dsem] = o.dprev
                wl = list(waits.items())
                for key, val in wl:
                    seen[key] = val
                stats["waits"] += len(wl)
                for key, val in wl[:-1]:
                    eng.wait_ge(sem[key], val)
                r = o.fn(eng)
                if isinstance(r, (list, tuple)):
                    first, last = r[0], r[-1]
                else:
                    first = last = r
                if wl:
                    key, val = wl[-1]
                    first._wait_ge(sem[key], val)
                if o.dma:
                    last.then_inc(sem[o.dsem], 16)
                elif o.sig is not None:
                    last.then_inc(sem[ename], 1)
            if ename == "sp":
                for key, val in dcount.items():
                    if seen.get(key, 0) < val:
                        eng.wait_ge(sem[key], val)

        with nc.Block() as block:
            @block.tensor
            def _(eng):
                run_engine("pe", eng)

            @block.scalar
            def _(eng):
                run_engine("act", eng)

            @block.vector
            def _(eng):
                run_engine("dve", eng)

            @block.gpsimd
            def _(eng):
                run_engine("pool", eng)

            @block.sync
            def _(eng):
                run_engine("sp", eng)
        return stats


class Arena:
    def __init__(self, ap_f32):
        self.t = ap_f32
        self.cap = ap_f32.shape[1] * 4
        self.off = 0

    def reset(self):
        self.off = 0

    def alloc(self, n, dt=F32):
        size = n * (4 if dt == F32 else 2)
        size = (size + 63) // 64 * 64
        assert self.off + size <= self.cap, ("arena overflow", self.off, size, self.cap)
        v = self.t[:, self.off // 4:(self.off + size) // 4]
        self.off += size
        if dt != F32:
            v = v.bitcast(dt)
        return v[:, 0:n]


def build(NB, cfg):
    useA, useB, useC = cfg.get("A", True), cfg.get("B", True), cfg.get("C", True)
    nlayers = cfg.get("layers", 2)
    nc = bass.Bass("TRN2", target_bir_lowering=False)
    P = Prog(nc)
    RR = NB + 1

    def din(name, shape):
        return nc.dram_tensor(name, shape, F32, kind="ExternalInput").ap()

    x_d = din("x", [NB, SEQ, DM])
    ctx_d = din("ctx", [NB, CTXL, DM])
    cT_d = din("cT", [128, 8 * RR])
    w_ada_d = din("w_ada", [2, DM, 3 * DM])
    b_adaT_d = din("b_adaT", [128, 48])
    norm_wT_d = din("norm_wT", [128, 16])
    w_in_d = din("w_in", [2, DM, DIN])
    w_out_d = din("w_out", [2, DM, DM])
    a_ln_w_d = din("a_ln_w", [2, 256])
    a_wsT_d = din("a_wsT", [2, 128, 512])
    a_bsT_d = din("a_bsT", [2, 128, 4])
    b_lam_d = din("b_lam", [2, 256])
    b_subln_d = din("b_subln_w", [2, 128])
    conv_wT_d = din("conv_wT", [2, 128, 30])
    c_alog_d = din("c_a_log", [2, 8])
    c_dtb_d = din("c_dt_bias", [2, 8])
    c_normw_d = din("c_norm_w", [2, 64])
    fnw_d = din("final_norm_w", [1, DM])
    ident_d = din("ident", [128, 128])
    perm_d = din("perm", [128, 128])
    cos_d = din("ropecos", [128, SEQ])
    sin_d = din("ropesin", [128, SEQ])
    gmask_d = din("gmasks", [128, 8 * 128])
    out_d = nc.dram_tensor("out", [NB, SEQ, DM], F32, kind="ExternalOutput").ap()
    dbg_d = {}
    for nm, shp in cfg.get("dbg", {}).items():
        dbg_d[nm] = nc.dram_tensor("dbg_" + nm, shp, F32, kind="ExternalOutput").ap()

    cnt = [0]

    def sb(shape, dt=F32, name=None):
        cnt[0] += 1
        return nc.alloc_sbuf_tensor("s_" + (name or ("t%d" % cnt[0])), shape, dt).ap()

    pb = [nc.alloc_psum_tensor("pb%d" % i, [128, 512], F32).ap() for i in range(8)]
    pbn = ["pb%d" % i for i in range(8)]

    X = sb([128, NT * DM], F32, "X")
    X3 = X.rearrange("p (t d) -> p t d", t=NT)
    hT = sb([128, 8 * T], BF16, "hT")
    hT3 = hT.rearrange("p (k t) -> p k t", k=8)
    ident_f = sb([128, 128], F32, "ident_f")
    ident_b = sb([128, 128], BF16, "ident_b")
    ones_f = sb([128, 128], F32, "ones_f")
    cT = sb([128, 8 * RR], F32, "cT")
    cact = sb([128, 8 * RR], BF16, "cact")
    cact3 = cact.rearrange("p (k r) -> p k r", k=8)
    b_adaT = sb([128, 48], F32, "b_adaT")
    norm_wT = sb([128, 16], F32, "norm_wT")
    modT = [sb([128, 24 * RR], F32, "modT%d" % l) for l in range(2)]
    modT3 = [m.rearrange("p (t r) -> p t r", t=24) for m in modT]
    g1T = [sb([128, 8], F32, "g1T%d" % i) for i in range(2)]
    gate_bc = [sb([128, DM], F32, "gate_bc%d" % i) for i in range(2)]
    dg = [sb([128, 128], F32, "dg%d" % i) for i in range(2)]
    xn = sb([128, DM], BF16, "xn")
    stat = sb([128, 64], F32, "stat")
    ARENA_BYTES = cfg.get("arena", 82 * 1024)
    arena = Arena(sb([128, ARENA_BYTES // 4], F32, "arena"))

    def ACT(out, in_, func, reads, writes, **kw):
        P.op("act", lambda e: e.activation(out, in_, func, **kw), reads, writes)

    def MM(out, pairs, reads, writes):
        def fn(e):
            n = len(pairs)
            return [e.matmul(out, l, r, start=(i == 0), stop=(i == n - 1)) for i, (l, r) in enumerate(pairs)]
        P.op("pe", fn, reads, writes)

    def TRS(items, reads, writes, ident):
        P.op("pe", lambda e: [e.transpose(o, i, ident) for (o, i) in items], reads, writes)

    def DVE(fn, reads, writes):
        P.op("dve", fn, reads, writes)

    def POOL(fn, reads, writes):
        P.op("pool", fn, reads, writes)

    def DMA(q, out, in_, reads, writes):
        P.dma(q, lambda e: e.dma_start(out=out, in_=in_), reads, writes)

    def dump(name, src_ap, reads, dst=None):
        if name in dbg_d:
            d = dbg_d[name] if dst is None else dst
            DMA("pool", d, src_ap, reads, [])

    DMA("sp", ident_f, ident_d, [], ["ident_f"])
    DMA("pool", ident_b, ident_d, [], ["ident_b"])
    POOL(lambda e: e.memset(ones_f, 1.0), [], ["ones_f"])
    DMA("sp", cT, cT_d, [], ["cT"])
    DMA("sp", b_adaT, b_adaT_d, [], ["b_adaT"])
    DMA("sp", norm_wT, norm_wT_d, [], ["norm_wT"])
    ACT(cact, cT, AF.Silu, ["cT"], ["cact"])

    arena.reset()
    wa_buf = [arena.alloc(8 * 512, BF16) for _ in range(2)]
    for l in range(nlayers):
        wsrc = w_ada_d[l].rearrange("(k p) n -> p k n", p=128)
        for g in range(6):
            wa = wa_buf[g % 2]
            wa3 = wa.rearrange("p (k n) -> p k n", k=8)
            wn = "wa%d" % (g % 2)
            DMA("pool", wa3, wsrc[:, :, g * 512:(g + 1) * 512], [], [wn])
            for t in range(4):
                tt = g * 4 + t
                MM(pb[0][:, tt * RR:(tt + 1) * RR],
                   [(wa3[:, k, t * 128:(t + 1) * 128], cact3[:, k, :]) for k in range(8)],
                   [wn, "cact"], ["pb0"])
        DVE(lambda e, l=l: e.tensor_tensor(
            modT3[l], pb[0][:, 0:24 * RR].rearrange("p (t r) -> p t r", t=24),
            b_adaT[:, l * 24:(l + 1) * 24].unsqueeze(2).broadcast_to([128, 24, RR]), ALU.add),
            ["pb0", "b_adaT"], ["modT%d" % l])
    P.barrier()

    def mod_prep(b, l, ctx_out):
        rows = [(0, b)] + ([(1, NB)] if True else [])
        for which, r in rows:
            DVE(lambda e, which=which, r=r: e.scalar_tensor_tensor(
                out=g1T[which], in0=modT3[l][:, 8:16, r], scalar=1.0, in1=norm_wT[:, l * 8:(l + 1) * 8],
                op0=ALU.add, op1=ALU.mult), ["modT%d" % l, "norm_wT"], ["g1T%d" % which])
            if which == 1 and not ctx_out:
                continue
            for k in range(8):
                d = dg[k % 2]
                dn = "dg%d" % (k % 2)
                DVE(lambda e, d=d, k=k, r=r: e.tensor_scalar(d, ident_f, modT3[l][:, 16 + k, r:r + 1], None, ALU.mult),
                    ["ident_f", "modT%d" % l], [dn])
                bank = 6 + k // 4
                MM(pb[bank][:, (k % 4) * 128:(k % 4 + 1) * 128], [(ones_f, d)], ["ones_f", dn], [pbn[bank]])
            ACT(gate_bc[which][:, 0:512], pb[6], AF.Copy, ["pb6"], ["gate_bc%d" % which])
            ACT(gate_bc[which][:, 512:1024], pb[7], AF.Copy, ["pb7"], ["gate_bc%d" % which])

    def phase1(b, l):
        if l == 0:
            DMA("sp", X3[:, 0:2, :], ctx_d[b].rearrange("(t p) d -> p t d", p=128), [], ["X0", "X1"])
            for g in range(4):
                DMA("sp", X3[:, 2 + 4 * g:6 + 4 * g, :],
                    x_d[b, g * 512:(g + 1) * 512, :].rearrange("(t p) d -> p t d", p=128),
                    [], ["X%d" % (2 + 4 * g + i) for i in range(4)])
        for p in range(NT):
            which = 1 if p < 2 else 0
            r = NB if p < 2 else b
            xs = X3[:, p, :]
            ss = stat[:, 0:1]
            sd = stat[:, 1:2]
            rstd = stat[:, 2:3]
            ACT(xn, xs, AF.Square, ["X%d" % p], ["xn", "st0"], accum_out=ss)
            ACT(sd, ss, AF.Sqrt, ["st0"], ["st1"], scale=1.0 / DM, bias=EPS)
            DVE(lambda e: e.reciprocal(rstd, sd), ["st1"], ["st2"])
            ACT(xn, xs, AF.Copy, ["X%d" % p, "st2"], ["xn"], scale=rstd)
            pT = pb[p % 2].bitcast(BF16)
            pTn = pbn[p % 2]
            TRS([(pT[:, k * 128:(k + 1) * 128], xn[:, k * 128:(k + 1) * 128]) for k in range(8)],
                ["xn", "ident_b"], [pTn], ident_b)
            for k in range(8):
                o = hT3[:, k, p * 128:(p + 1) * 128]
                i_ = pT[:, k * 128:(k + 1) * 128]
                sc = g1T[which][:, k:k + 1]
                bi = modT3[l][:, k, r:r + 1]
                if k % 2 == 0:
                    ACT(o, i_, AF.Identity, [pTn, "g1T%d" % which, "modT%d" % l], ["hT%d" % p], scale=sc, bias=bi)
                else:
                    DVE(lambda e, o=o, i_=i_, sc=sc, bi=bi: e.tensor_scalar(o, i_, sc, bi, ALU.mult, ALU.add),
                        [pTn, "g1T%d" % which, "modT%d" % l], ["hT%d" % p])

    def x_update(p, which, pa, pan, pbk, pbkn, tmp, tmpn):
        for half, (bank, bn) in enumerate(((pa, pan), (pbk, pbkn))):
            sl = slice(half * 512, (half + 1) * 512)
            DVE(lambda e, bank=bank, sl=sl: e.tensor_tensor(tmp[:, sl], bank, gate_bc[which][:, sl], ALU.mult),
                [bn, "gate_bc%d" % which], [tmpn])
            POOL(lambda e, sl=sl: e.tensor_tensor(X3[:, p, sl], X3[:, p, sl], tmp[:, sl], ALU.add),
                 [tmpn, "X%d" % p], ["X%d" % p])

    def phaseA(b, l, ctx_out):
        arena.reset()
        wA = arena.alloc(8 * 768, BF16)
        wA3 = wA.rearrange("p (k n) -> p k n", k=8)
        woA = arena.alloc(2 * DM, BF16)
        woA3 = woA.rearrange("p (k n) -> p k n", k=2)
        awsT = arena.alloc(512, BF16)
        abs_ = arena.alloc(4, F32)
        aln = arena.alloc(256, F32)
        vsb = arena.alloc(256, F32)
        vc = arena.alloc(256, BF16)
        sz = arena.alloc(256, F32)
        uz = arena.alloc(256, F32)
        ya = arena.alloc(256, BF16)
        yT = arena.alloc(256, BF16)
        tmp = arena.alloc(DM, F32)
        st = arena.alloc(16, F32)
        DMA("pool", wA3, w_in_d[l].rearrange("(k p) n -> p k n", p=128)[:, :, 0:768], [], ["wA"])
        DMA("pool", woA3, w_out_d[l, 0:256, :].rearrange("(k p) n -> p k n", p=128), [], ["woA"])
        DMA("pool", awsT, a_wsT_d[l], [], ["awsT"])
        DMA("sp", abs_, a_bsT_d[l], [], ["abs"])
        DMA("sp", aln, a_ln_w_d[l:l + 1, :].partition_broadcast(128), [], ["aln"])
        for p in range(NT):
            if p < 2 and not ctx_out:
                continue
            which = 1 if p < 2 else 0
            tok = slice(p * 128, (p + 1) * 128)
            MM(pb[2], [(hT3[:, k, tok], wA3[:, k, 0:512]) for k in range(8)], ["hT%d" % p, "wA"], ["pb2"])
            MM(pb[3][:, 0:256], [(hT3[:, k, tok], wA3[:, k, 512:768]) for k in range(8)], ["hT%d" % p, "wA"], ["pb3"])
            DVE(lambda e: e.bn_stats(st[:, 0:6], pb[2][:, 256:512]), ["pb2"], ["Ast"])
            DVE(lambda e: e.bn_aggr(st[:, 6:8], st[:, 0:6]), ["Ast"], ["Amv"])
            ACT(st[:, 8:9], st[:, 7:8], AF.Sqrt, ["Amv"], ["Asd"], bias=1e-5)
            DVE(lambda e: e.reciprocal(st[:, 9:10], st[:, 8:9]), ["Asd"], ["Ars"])
            DVE(lambda e: e.scalar_tensor_tensor(out=st[:, 10:11], in0=st[:, 6:7], scalar=-1.0, in1=st[:, 9:10],
                                                 op0=ALU.mult, op1=ALU.mult), ["Amv", "Ars"], ["Anb"])
            ACT(vsb, pb[2][:, 256:512], AF.Identity, ["pb2", "Ars", "Anb"], ["vsb"], scale=st[:, 9:10], bias=st[:, 10:11])
            DVE(lambda e: e.tensor_tensor(vc, vsb, aln, ALU.mult), ["vsb", "aln"], ["vc"])
            awsT3 = awsT.rearrange("p (g i) -> p g i", g=4)
            P.op("pe", lambda e: [e.matmul(pb[7][:, g * 64:(g + 1) * 64], awsT3[:, g, :], vc[:, g * 64:(g + 1) * 64],
                                           start=True, stop=True) for g in range(4)], ["awsT", "vc"], ["pb7"])
            ACT(sz, pb[3][:, 0:256], AF.Silu, ["pb3"], ["sz"])
            DVE(lambda e: e.tensor_tensor(uz, pb[2][:, 0:256], sz, ALU.mult), ["pb2", "sz"], ["uz"])
            for g in range(4):
                cs = slice(g * 64, (g + 1) * 64)
                DVE(lambda e, g=g, cs=cs: e.scalar_tensor_tensor(
                    out=ya[:, cs], in0=pb[7][:, g * 64:(g + 1) * 64], scalar=abs_[:, g:g + 1], in1=uz[:, cs],
                    op0=ALU.add, op1=ALU.mult), ["pb7", "uz", "abs"], ["ya"])
            pT = pb[4].bitcast(BF16)
            TRS([(pT[:, c * 128:(c + 1) * 128], ya[:, c * 128:(c + 1) * 128]) for c in range(2)], ["ya", "ident_b"], ["pb4"], ident_b)
            ACT(yT, pT[:, 0:256], AF.Copy, ["pb4"], ["yT"])
            for half in range(2):
                MM(pb[5 + half], [(yT[:, c * 128:(c + 1) * 128], woA3[:, c, half * 512:(half + 1) * 512]) for c in range(2)],
                   ["yT", "woA"], [pbn[5 + half]])
            x_update(p, which, pb[5], "pb5", pb[6], "pb6", tmp, "tmpA")
        P.barrier()

    def phaseB(b, l, ctx_out):
        arena.reset()
        lam_init = 0.8 - 0.6 * math.exp(-0.3 * l)
        cosb = arena.alloc(SEQ, BF16)
        sinb = arena.alloc(SEQ, BF16)
        permb = arena.alloc(128, BF16)
        woB = arena.alloc(4 * DM, BF16)
        woB3 = woB.rearrange("p (k n) -> p k n", k=4)
        lamb = arena.alloc(256, F32)
        lamb4 = lamb.rearrange("p (a w d) -> p a w d", a=2, w=2)
        lprod = arena.alloc(128, F32)
        lst = arena.alloc(16, F32)
        sublnw = arena.alloc(128, F32)
        wB = arena.alloc(8 * 512, BF16)
        wB3 = wB.rearrange("p (k n) -> p k n", k=8)
        qT = arena.alloc(T, BF16)
        kT = arena.alloc(T, BF16)
        vaug = arena.alloc(NT * 130, BF16)
        vaug3 = vaug.rearrange("p (t c) -> p t c", t=NT)
        szw = arena.alloc(NT * 128, BF16)
        szw3 = szw.rearrange("p (t c) -> p t c", t=NT)
        qraw = [arena.alloc(512, BF16) for _ in range(2)]
        t1 = [arena.alloc(512, F32) for _ in range(2)]
        t2 = [arena.alloc(512, F32) for _ in range(2)]
        PT = [arena.alloc(512, BF16) for _ in range(3)]
        szf = arena.alloc(128, F32)
        o32 = arena.alloc(128, F32)
        junk = arena.alloc(128, BF16)
        yb = arena.alloc(128, BF16)
        yTb = arena.alloc(128, BF16)
        fst = arena.alloc(16, F32)
        tmp = arena.alloc(DM, F32)
        DMA("pool", cosb, cos_d, [], ["cosb"])
        DMA("pool", sinb, sin_d, [], ["sinb"])
        DMA("pool", permb, perm_d, [], ["permb"])
        DMA("pool", woB3, w_out_d[l, 256:768, :].rearrange("(k p) n -> p k n", p=128), [], ["woB"])
        DMA("sp", lamb, b_lam_d[l:l + 1, :].partition_broadcast(128), [], ["lamb"])
        DMA("sp", sublnw, b_subln_d[l:l + 1, :].partition_broadcast(128), [], ["sublnw"])
        ACT(sublnw, sublnw, AF.Copy, ["sublnw"], ["sublnw"], scale=(1.0 - lam_init))
        POOL(lambda e: e.memset(vaug3[:, :, 128:130], 1.0), [], ["vaug_ones"])
        lprod3 = lprod.rearrange("p (a d) -> p a d", a=2)
        DVE(lambda e: e.tensor_tensor(lprod3, lamb4[:, :, 0, :], lamb4[:, :, 1, :], ALU.mult), ["lamb"], ["lprod"])
        DVE(lambda e: e.reduce_sum(lst[:, 0:2], lprod3, AX.X), ["lprod"], ["lst0"])
        ACT(lst[:, 2:4], lst[:, 0:2], AF.Exp, ["lst0"], ["lst1"])
        DVE(lambda e: e.tensor_tensor(lst[:, 4:5], lst[:, 3:4], lst[:, 2:3], ALU.subtract), ["lst1"], ["lst2"])
        DVE(lambda e: e.tensor_scalar(lst[:, 5:6], lst[:, 4:5], -lam_init, None, ALU.add), ["lst2"], ["nlam"])
        nlam = lst[:, 5:6]
        wsrc = w_in_d[l].rearrange("(k p) n -> p k n", p=128)
        groups = [(0, 256, [0, 1])] + [(256 + 512 * g, 512, [2 + 4 * g + i for i in range(4)]) for g in range(4)]
        for hp in range(4):
            for ci, c0 in enumerate((768, 1280, 1792, 2304)):
                DMA("pool", wB3[:, :, ci * 128:(ci + 1) * 128], wsrc[:, :, c0 + hp * 128:c0 + (hp + 1) * 128], [], ["wB%d" % ci])
            cntr = 0
            for wi, (dst, dname) in enumerate(((qT, "qT"), (kT, "kT"))):
                for gi, (t0, n, tiles) in enumerate(groups):
                    if wi == 0 and gi == 0 and not ctx_out:
                        continue
                    pp = pb[cntr % 2]
                    ppn = pbn[cntr % 2]
                    MM(pp[:, 0:n], [(wB3[:, k, wi * 128:(wi + 1) * 128], hT3[:, k, t0:t0 + n]) for k in range(8)],
                       ["wB%d" % wi] + ["hT%d" % p for p in tiles], [ppn])
                    rn = "%s%d" % (dname, gi)
                    if gi == 0:
                        ACT(dst[:, t0:t0 + n], pp[:, 0:n], AF.Copy, [ppn], [rn])
                    else:
                        qr = qraw[cntr % 2]
                        qrn = "qraw%d" % (cntr % 2)
                        pq = pb[2 + cntr % 2]
                        pqn = pbn[2 + cntr % 2]
                        ta = t1[cntr % 2]
                        tan = "t1_%d" % (cntr % 2)
                        tb = t2[cntr % 2]
                        tbn = "t2_%d" % (cntr % 2)
                        ps = slice(t0 - 256, t0 - 256 + n)
                        ACT(qr, pp, AF.Copy, [ppn], [qrn])
                        MM(pq, [(permb, qr)], ["permb", qrn], [pqn])
                        POOL(lambda e, ta=ta, qr=qr, ps=ps: e.tensor_tensor(ta, qr, cosb[:, ps], ALU.mult), [qrn, "cosb"], [tan])
                        DVE(lambda e, tb=tb, pq=pq, ps=ps: e.tensor_tensor(tb, pq, sinb[:, ps], ALU.mult), [pqn, "sinb"], [tbn])
                        POOL(lambda e, dst=dst, t0=t0, n=n, ta=ta, tb=tb: e.tensor_tensor(dst[:, t0:t0 + n], ta, tb, ALU.add),
                             [tan, tbn], [rn])
                    cntr += 1
            for p in range(NT):
                tok = slice(p * 128, (p + 1) * 128)
                need_z = (p >= 2) or ctx_out
                ncols = 256 if need_z else 128
                pp = pb[4 + p % 2]
                ppn = pbn[4 + p % 2]
                MM(pp[:, 0:ncols], [(hT3[:, k, tok], wB3[:, k, 256:256 + ncols]) for k in range(8)],
                   ["hT%d" % p, "wB2", "wB3"], [ppn])
                ACT(vaug3[:, p, 0:128], pp[:, 0:128], AF.Copy, [ppn], ["vaug%d" % p])
                if need_z:
                    ACT(szf, pp[:, 128:256], AF.Silu, [ppn], ["szf"])
                    DVE(lambda e, p=p: e.tensor_tensor(szw3[:, p, :], szf, sublnw, ALU.mult), ["szf", "sublnw"], ["szw%d" % p])
            def kgroup(kt):
                return 0 if kt < 2 else 1 + (kt - 2) // 4
            qgroups = [(gi, g) for gi, g in enumerate(groups) if gi > 0]
            if ctx_out:
                qgroups = [(0, groups[0])] + qgroups
            iters = []
            for gi, (t0, n, tiles) in qgroups:
                keyt = [0, 1] if gi == 0 else list(range(NT))
                for j in range(2):
                    for ki, kt in enumerate(keyt):
                        iters.append((gi, t0, n, tiles, j, ki, kt, len(keyt)))

            def oslot_of(nq, j, qi):
                s_ = j * nq + qi
                return pb[2 + s_ // 3][:, (s_ % 3) * 129:(s_ % 3) * 129 + 129]

            def emit_st(ei):
                gi, t0, n, tiles, j, ki, kt, nk = iters[ei]
                hs = slice(j * 64, (j + 1) * 64)
                MM(pb[ei % 2][:, 0:n], [(kT[hs, kt * 128:(kt + 1) * 128], qT[hs, t0:t0 + n])],
                   ["kT%d" % kgroup(kt), "qT%d" % gi], [pbn[ei % 2]])

            def emit_finalize(gi, tiles):
                nq = len(tiles)
                for qi, p in enumerate(tiles):
                    which = 1 if p < 2 else 0
                    O0 = oslot_of(nq, 0, qi)
                    O1 = oslot_of(nq, 1, qi)
                    b0n = pbn[2 + (0 * nq + qi) // 3]
                    b1n = pbn[2 + (1 * nq + qi) // 3]
                    DVE(lambda e, O0=O0: e.reciprocal(fst[:, 0:1], O0[:, 128:129]), [b0n], ["f0"])
                    DVE(lambda e, O1=O1: e.reciprocal(fst[:, 1:2], O1[:, 128:129]), [b1n], ["f1"])
                    DVE(lambda e: e.tensor_tensor(fst[:, 2:3], fst[:, 1:2], nlam, ALU.mult), ["f1", "nlam"], ["f2"])
                    DVE(lambda e, O0=O0: e.tensor_scalar(o32, O0[:, 0:128], fst[:, 0:1], None, ALU.mult), [b0n, "f0"], ["o32"])
                    DVE(lambda e, O1=O1: e.scalar_tensor_tensor(out=o32, in0=O1[:, 0:128], scalar=fst[:, 2:3], in1=o32,
                                                              op0=ALU.mult, op1=ALU.add), [b1n, "f2", "o32"], ["o32"])
                    ACT(junk, o32, AF.Square, ["o32"], ["junkB", "f3"], accum_out=fst[:, 3:4])
                    ACT(fst[:, 4:5], fst[:, 3:4], AF.Sqrt, ["f3"], ["f4"], scale=1.0 / 128, bias=EPS)
                    DVE(lambda e: e.reciprocal(fst[:, 5:6], fst[:, 4:5]), ["f4"], ["f5"])
                    DVE(lambda e, p=p: e.scalar_tensor_tensor(out=yb, in0=o32, scalar=fst[:, 5:6], in1=szw3[:, p, :],
                                                             op0=ALU.mult, op1=ALU.mult), ["o32", "f5", "szw%d" % p], ["yb"])
                    pT = pb[5].bitcast(BF16)
                    TRS([(pT[:, 0:128], yb)], ["yb", "ident_b"], ["pb5"], ident_b)
                    ACT(yTb, pT[:, 0:128], AF.Copy, ["pb5"], ["yTb"])
                    for half in range(2):
                        MM(pb[6 + half], [(yTb, woB3[:, hp, half * 512:(half + 1) * 512])], ["yTb", "woB"], [pbn[6 + half]])
                    x_update(p, which, pb[6], "pb6", pb[7], "pb7", tmp, "tmpB")

            if iters:
                emit_st(0)
            for ei, (gi, t0, n, tiles, j, ki, kt, nk) in enumerate(iters):
                nq = len(tiles)
                sp_ = pb[ei % 2]
                spn = pbn[ei % 2]
                pt = PT[ei % 3]
                ptn = "PT%d" % (ei % 3)
                ACT(pt[:, 0:n], sp_[:, 0:n], AF.Exp, [spn], [ptn], scale=0.125)
                if ei + 1 < len(iters):
                    emit_st(ei + 1)

                def pv(e, j=j, kt=kt, ki=ki, pt=pt, nq=nq, nk=nk):
                    seen_b = set()
                    r_ = []
                    for qi in range(nq):
                        bank = 2 + (j * nq + qi) // 3
                        st_ = (ki == 0) and (bank not in seen_b)
                        seen_b.add(bank)
                        r_.append(e.matmul(oslot_of(nq, j, qi), pt[:, qi * 128:(qi + 1) * 128], vaug3[:, kt, 0:129],
                                           start=st_, stop=(ki == nk - 1), skip_group_check=True))
                    return r_
                P.op("pe", pv, [ptn, "vaug%d" % kt, "vaug_ones"], ["pb2", "pb3", "pb4"])
                if j == 1 and ki == nk - 1:
                    emit_finalize(gi, tiles)
        P.barrier()

    TC = T + 4

    def coff(p):
        return p * 128 if p < 2 else 4 + p * 128

    def phaseC(b, l, ctx_out):
        arena.reset()
        A = arena
        convw = A.alloc(30)
        alog = A.alloc(8)
        dtb = A.alloc(8)
        nea = A.alloc(8)
        cn4 = A.alloc(256)
        gm = A.alloc(8 * 128)
        gm3 = gm.rearrange("p (m c) -> p m c", m=8)
        sameb = A.alloc(128, BF16)
        negones = A.alloc(128)
        woC = A.alloc(2 * DM, BF16)
        woC3 = woC.rearrange("p (k n) -> p k n", k=2)
        qkvT = A.alloc(6 * TC, BF16)
        qkvT3 = qkvT.rearrange("p (c t) -> p c t", c=6)
        szc = A.alloc(NT * 256, BF16)
        szc3 = szc.rearrange("p (t c) -> p t c", t=NT)
        g_all = A.alloc(NT * 8)
        g3 = g_all.rearrange("p (t c) -> p t c", t=NT)
        beta_all = A.alloc(NT * 8)
        beta3 = beta_all.rearrange("p (t c) -> p t c", t=NT)
        lnb_all = A.alloc(NT * 8)
        lnb3 = lnb_all.rearrange("p (t c) -> p t c", t=NT)
        mark = A.off
        xin = A.alloc(TC + 4, BF16)
        acc = A.alloc(TC)
        sqb = xin[:, 0:TC]
        wC = [A.alloc(8 * 128, BF16) for _ in range(2)]
        wCz = A.alloc(8 * 272, BF16)
        wCz3 = wCz.rearrange("p (k n) -> p k n", k=8)
        rin = [A.alloc(512) for _ in range(2)]
        szt = A.alloc(256)
        gt = A.alloc(64)
        DMA("sp", convw, conv_wT_d[l], [], ["convw"])
        DMA("sp", alog, c_alog_d[l:l + 1, :].partition_broadcast(128), [], ["alog"])
        DMA("sp", dtb, c_dtb_d[l:l + 1, :].partition_broadcast(128), [], ["dtb"])
        for h in range(4):
            DMA("sp", cn4[:, h * 64:(h + 1) * 64], c_normw_d[l:l + 1, :].partition_broadcast(128), [], ["cn4"])
        DMA("sp", gm, gmask_d, [], ["gm"])
        DMA("pool", sameb, gmask_d[:, 256:384], [], ["sameb"])
        POOL(lambda e: e.memset(negones, -1.0), [], ["negones"])
        DMA("pool", woC3, w_out_d[l, 768:1024, :].rearrange("(k p) n -> p k n", p=128), [], ["woC"])
        POOL(lambda e: e.memset(xin, 0.0), [], ["xin"])
        ACT(nea, alog, AF.Exp, ["alog"], ["nea"])
        DVE(lambda e: e.tensor_scalar(nea, nea, -1.0, None, ALU.mult), ["nea"], ["nea"])
        wsrc = w_in_d[l].rearrange("(k p) n -> p k n", p=128)
        DMA("pool", wCz3, wsrc[:, :, 3584:3856], [], ["wCz"])
        groups = [(0, 256, [0, 1])] + [(256 + 512 * g, 512, [2 + 4 * g + i for i in range(4)]) for g in range(4)]
        convw3 = convw.rearrange("p (c j) -> p c j", c=6)
        for ct in range(6):
            w_ = wC[ct % 2]
            w3 = w_.rearrange("p (k n) -> p k n", k=8)
            wn = "wC%d" % (ct % 2)
            DMA("pool", w3, wsrc[:, :, 2816 + ct * 128:2816 + (ct + 1) * 128], [], [wn])
            if ct > 0:
                POOL(lambda e: e.memset(xin, 0.0), [], ["xin"])
            for gi, (t0, n, tiles) in enumerate(groups):
                pp = pb[gi % 2]
                ppn = pbn[gi % 2]
                MM(pp[:, 0:n], [(w3[:, k, :], hT3[:, k, t0:t0 + n]) for k in range(8)], [wn] + ["hT%d" % p for p in tiles], [ppn])
                c0 = 2 + (t0 if gi == 0 else t0 + 4)
                ACT(xin[:, c0:c0 + n], pp[:, 0:n], AF.Copy, [ppn], ["xin"])
            DVE(lambda e, ct=ct: e.tensor_scalar(acc, xin[:, 0:TC], convw3[:, ct, 0:1], None, ALU.mult), ["xin", "convw"], ["acc"])
            for j in range(1, 5):
                DVE(lambda e, ct=ct, j=j: e.scalar_tensor_tensor(out=acc, in0=xin[:, j:j + TC], scalar=convw3[:, ct, j:j + 1], in1=acc,
                                                                 op0=ALU.mult, op1=ALU.add), ["xin", "convw", "acc"], ["acc"])
            if ct >= 4:
                ACT(qkvT3[:, ct, :], acc, AF.Silu, ["acc"], ["qkvT%d" % ct])
            else:
                ACT(acc, acc, AF.Silu, ["acc"], ["acc"])
                POOL(lambda e: e.tensor_tensor(sqb, acc, acc, ALU.mult), ["acc"], ["xin"])
                for ci, c0 in enumerate(range(0, TC, 512)):
                    n = min(512, TC - c0)
                    pp = pb[2 + ci % 2]
                    ppn = pbn[2 + ci % 2]
                    r_ = rin[ci % 2]
                    rn = "rin%d" % (ci % 2)
                    MM(pp[:, 0:n], [(sameb, sqb[:, c0:c0 + n])], ["sameb", "xin"], [ppn])
                    ACT(r_[:, 0:n], pp[:, 0:n], AF.Sqrt, [ppn], [rn], bias=1e-6)
                    DVE(lambda e, r_=r_, n=n: e.reciprocal(r_[:, 0:n], r_[:, 0:n]), [rn], [rn])
                    sc = 0.125 if ct < 2 else 1.0
                    DVE(lambda e, ct=ct, c0=c0, n=n, r_=r_, sc=sc: e.scalar_tensor_tensor(
                        out=qkvT3[:, ct, c0:c0 + n], in0=acc[:, c0:c0 + n], scalar=sc, in1=r_[:, 0:n], op0=ALU.mult, op1=ALU.mult),
                        ["acc", rn], ["qkvT%d" % ct])
        for p in range(NT):
            tok = slice(p * 128, (p + 1) * 128)
            pp = pb[4 + p % 2]
            ppn = pbn[4 + p % 2]
            MM(pp[:, 0:272], [(hT3[:, k, tok], wCz3[:, k, :]) for k in range(8)], ["hT%d" % p, "wCz"], [ppn])
            if p >= 2 or ctx_out:
                ACT(szt, pp[:, 0:256], AF.Silu, [ppn], ["szt"])
                DVE(lambda e, p=p: e.tensor_tensor(szc3[:, p, :], szt, cn4, ALU.mult), ["szt", "cn4"], ["szc%d" % p])
            ACT(gt[:, 0:8], pp[:, 256:264], AF.Exp, [ppn], ["gt0"], scale=-1.0)
            DVE(lambda e: e.tensor_scalar(gt[:, 0:8], gt[:, 0:8], 1.0, None, ALU.add), ["gt0"], ["gt0"])
            DVE(lambda e, p=p: e.reciprocal(beta3[:, p, :], gt[:, 0:8]), ["gt0"], ["gates%d" % p])
            ACT(gt[:, 8:16], gt[:, 0:8], AF.Ln, ["gt0"], ["gt1"])
            DVE(lambda e, p=p: e.tensor_scalar(lnb3[:, p, :], gt[:, 8:16], -1.0, None, ALU.mult), ["gt1"], ["gates%d" % p])
            DVE(lambda e, pp=pp: e.tensor_tensor(gt[:, 16:24], pp[:, 264:272], dtb, ALU.add), [ppn, "dtb"], ["gt2"])
            ACT(gt[:, 24:32], gt[:, 16:24], AF.Exp, ["gt2"], ["gt3"])
            ACT(gt[:, 32:40], gt[:, 24:32], AF.Ln, ["gt3"], ["gt4"], bias=1.0)
            DVE(lambda e, p=p: e.tensor_tensor(g3[:, p, :], gt[:, 32:40], nea, ALU.mult), ["gt4", "nea"], ["gates%d" % p])
        dump("qkvT", qkvT, ["qkvT%d" % c for c in range(6)])
        P.barrier()
        if cfg.get("cstop", 9) <= 2:
            return
        A.off = mark
        H = Arena(hT.bitcast(F32))
        o_acc = H.alloc(NT * 256)
        o3 = o_acc.rearrange("p (t c) -> p t c", t=NT)
        tokb = [A.alloc(768, BF16) for _ in range(2)]
        bv = [A.alloc(256, BF16) for _ in range(2)]
        bkg = [A.alloc(256, BF16) for _ in range(2)]
        kdec = [A.alloc(512, BF16) for _ in range(2)]
        qdec = [A.alloc(512, BF16) for _ in range(2)]
        sm = [A.alloc(64) for _ in range(2)]
        GT = [A.alloc(128) for _ in range(4)]
        dm = GT
        dec2 = [H.alloc(256) for _ in range(4)]
        LA = dec2
        NA = [H.alloc(128) for _ in range(4)]
        LN = [[H.alloc(256) for _ in range(2)] for _ in range(4)]
        Pm = [[H.alloc(128) for _ in range(2)] for _ in range(4)]
        TTb = [A.alloc(128, BF16) for _ in range(4)]
        u_sb = [A.alloc(256) for _ in range(2)]
        wT_sb = [A.alloc(512, BF16) for _ in range(2)]
        qdT_sb = [A.alloc(512, BF16) for _ in range(2)]
        atT_sb = [A.alloc(512, BF16) for _ in range(2)]
        S32 = [A.alloc(256) for _ in range(2)]
        Sb = [A.alloc(256, BF16) for _ in range(2)]
        vnew = [A.alloc(256, BF16) for _ in range(2)]
        ysq = A.alloc(256)
        yc = A.alloc(256, BF16)
        ycT = A.alloc(256, BF16)
        tmp = A.alloc(DM)
        for d in range(2):
            POOL(lambda e, d=d: e.memset(S32[d], 0.0), [], ["S32_%d" % d])
            POOL(lambda e, d=d: e.memset(Sb[d], 0.0), [], ["Sb_%d" % d])
        order = [list(range(NT)), [1, 0] + list(range(NT - 1, 1, -1))]
        visited = set()

        def prep(p, d):
            co = coff(p)
            tk = tokb[d]
            tk3 = tk.rearrange("p (c f) -> p c f", c=3)
            tkn = "tokb%d" % d
            s_ = sm[d]
            sn = "sm%d_" % d
            pT = pb[7].bitcast(BF16)
            TRS([(pT[:, c * 128:(c + 1) * 128], qkvT3[:, c, co:co + 128]) for c in range(6)],
                ["qkvT%d" % c for c in range(6)] + ["ident_b"], ["pb7"], ident_b)
            ACT(tk, pT[:, 0:768], AF.Copy, ["pb7"], [tkn])
            gsl = g3[:, p, d * 4:(d + 1) * 4]
            P.op("pe", lambda e, d=d, gsl=gsl: [
                e.matmul(pb[6][:, 0:4], gm3[:, d, :], gsl, start=True, stop=True),
                e.matmul(pb[6][:, 4:8], gm3[:, 2, :], gsl, start=True, stop=True),
                e.matmul(pb[6][:, 8:12], gm3[:, 3, :], gsl, start=True, stop=True)], ["gm", "gates%d" % p], ["pb6"])
            ACT(s_[:, 24:36], pb[6][:, 0:12], AF.Copy, ["pb6"], [sn + "raw"])
            ACT(s_[:, 0:4], s_[:, 24:28], AF.Exp, [sn + "raw"], [sn + "egc"])
            DVE(lambda e, s_=s_: e.tensor_tensor(s_[:, 20:24], s_[:, 28:32], s_[:, 24:28], ALU.subtract), [sn + "raw"], [sn + "t"])
            ACT(s_[:, 4:8], s_[:, 20:24], AF.Exp, [sn + "t"], [sn + "ekd"])
            ACT(s_[:, 8:16], s_[:, 28:36], AF.Exp, [sn + "raw"], [sn + "egl"])
            if cfg.get("pstop", 9) <= 1:
                return
            bsl = beta3[:, p, d * 4:(d + 1) * 4]
            DVE(lambda e, s_=s_, bsl=bsl: e.tensor_tensor(s_[:, 16:20], s_[:, 0:4], bsl, ALU.mult), [sn + "egc", "gates%d" % p], [sn + "bgk"])
            kt3 = tk3[:, 1, :].rearrange("p (h f) -> p h f", h=4)
            vt3 = tk3[:, 2, :].rearrange("p (h f) -> p h f", h=4)
            qt3 = tk3[:, 0, :].rearrange("p (h f) -> p h f", h=4)
            bc = lambda a: a.unsqueeze(2).broadcast_to([128, 4, 64])
            v4 = lambda a: a.rearrange("p (h f) -> p h f", h=4)
            DVE(lambda e, d=d, vt3=vt3, bsl=bsl: e.tensor_tensor(v4(bv[d]), vt3, bc(bsl), ALU.mult), [tkn, "gates%d" % p], ["bv%d" % d])
            DVE(lambda e, d=d, kt3=kt3, s_=s_: e.tensor_tensor(v4(bkg[d]), kt3, bc(s_[:, 16:20]), ALU.mult), [tkn, sn + "bgk"], ["bkg%d" % d])
            v42 = lambda a: a.rearrange("p (h r f) -> p h r f", h=4, r=2)
            bc2 = lambda a: a.unsqueeze(2).unsqueeze(3).broadcast_to([128, 4, 2, 64])
            dup = lambda a3: a3.unsqueeze(2).broadcast_to([128, 4, 2, 64])
            POOL(lambda e, d=d, kt3=kt3, s_=s_: e.tensor_tensor(v42(kdec[d]), dup(kt3), bc2(s_[:, 4:8]), ALU.mult), [tkn, sn + "ekd"], ["kdec%d" % d])
            POOL(lambda e, d=d, qt3=qt3, s_=s_: e.tensor_tensor(v42(qdec[d]), dup(qt3), bc2(s_[:, 0:4]), ALU.mult), [tkn, sn + "egc"], ["qdec%d" % d])
            if cfg.get("pstop", 9) <= 2:
                return
            P.op("pe", lambda e, d=d: [e.matmul(pb[5][:, h * 128:(h + 1) * 128], qdec[d][:, h * 128:(h + 1) * 128], ident_b,
                                                start=True, stop=True, skip_group_check=True) for h in range(4)],
                 ["qdec%d" % d, "ident_b"], ["pb5"])
            ACT(qdT_sb[d], pb[5], AF.Copy, ["pb5"], ["qdT%d" % d])
            if cfg.get("pstop", 9) <= 3:
                return
            def hv(h):
                ctk = 2 + h // 2
                ctq = h // 2
                hs = slice((h % 2) * 64, (h % 2) * 64 + 64)
                return (ctk, ctq, qkvT3[hs, ctk, co:co + 128], qkvT3[hs, ctq, co:co + 128], pb[h][:, 0:128], pb[4 + h][:, 0:256])
            for h in range(4):
                hn = "%d" % h
                DVE(lambda e, h=h, d=d: e.tensor_scalar(GT[h], gm3[:, d, :], g3[:, p, d * 4 + h:d * 4 + h + 1], None, ALU.mult),
                    ["gm", "gates%d" % p], ["GT" + hn])
            for h in range(4):
                hn = "%d" % h
                ctk, ctq, kTh, qTh, pd, pga = hv(h)
                P.op("pe", lambda e, h=h, pd=pd: [e.matmul(pd, GT[h], ones_f, start=True, stop=False, skip_group_check=True),
                                                  e.matmul(pd, negones, GT[h], start=False, stop=True, skip_group_check=True)],
                     ["GT" + hn, "ones_f", "negones"], [pbn[h]])
                P.op("pe", lambda e, pga=pga, kTh=kTh, qTh=qTh: [e.matmul(pga[:, 0:128], kTh, kTh, start=True, stop=True, skip_group_check=True),
                                                                e.matmul(pga[:, 128:256], qTh, kTh, start=True, stop=True, skip_group_check=True)],
                     ["qkvT%d" % ctk, "qkvT%d" % ctq], [pbn[4 + h]])
            for h in range(4):
                hn = "%d" % h
                ctk, ctq, kTh, qTh, pd, pga = hv(h)
                DVE(lambda e, h=h, pd=pd: e.tensor_scalar(dm[h], pd, 0.0, None, ALU.min), [pbn[h]], ["GT" + hn])
            for h in range(4):
                hn = "%d" % h
                ACT(dec2[h][:, 0:128], dm[h], AF.Exp, ["GT" + hn, "gates%d" % p], ["LA" + hn], bias=lnb3[:, p, d * 4 + h:d * 4 + h + 1])
                ACT(dec2[h][:, 128:256], dm[h], AF.Exp, ["GT" + hn], ["LA" + hn])
            for h in range(4):
                hn = "%d" % h
                POOL(lambda e, h=h, d=d: e.tensor_tensor(dec2[h], dec2[h], gm[:, (4 + 2 * d) * 128:(6 + 2 * d) * 128], ALU.mult),
                     ["LA" + hn, "gm"], ["LA" + hn])
            for h in range(4):
                hn = "%d" % h
                ctk, ctq, kTh, qTh, pd, pga = hv(h)
                DVE(lambda e, h=h, pga=pga: e.tensor_tensor(LA[h], pga, dec2[h], ALU.mult), [pbn[4 + h], "LA" + hn], ["LA" + hn])
            if cfg.get("pstop", 9) <= 4:
                return
            for h in range(4):
                hn = "%d" % h
                pt_ = pb[h][:, 0:256]
                TRS([(pt_[:, 0:128], LA[h][:, 0:128]), (pt_[:, 128:256], LA[h][:, 128:256])], ["LA" + hn, "ident_f"], [pbn[h]], ident_f)
                ACT(NA[h], pt_[:, 0:128], AF.Copy, [pbn[h]], ["NA" + hn])
                ACT(atT_sb[d][:, h * 128:(h + 1) * 128], pt_[:, 128:256], AF.Copy, [pbn[h]], ["atT%d" % d])
                DVE(lambda e, h=h: e.tensor_tensor(Pm[h][0], ident_f, NA[h], ALU.subtract), ["NA" + hn, "ident_f"], ["P0_" + hn])
            if cfg.get("pstop", 9) <= 5:
                return
            curL = [LA[h][:, 0:128] for h in range(4)]
            curN = [NA[h] for h in range(4)]
            curLn = ["LA%d" % h for h in range(4)]
            curNn = ["NA%d" % h for h in range(4)]
            for lev in range(5):
                last = lev == 4
                for h in range(4):
                    hn = "%d" % h
                    pln = pb[4 + h][:, 0:256]
                    dst = LN[h][lev % 2]
                    dn = "LN%d_%d" % (h, lev % 2)
                    if not last:
                        P.op("pe", lambda e, pln=pln, h=h, cl=curL[h], cn=curN[h]: [
                            e.matmul(pln[:, 0:128], cn, cl, start=True, stop=True, skip_group_check=True),
                            e.matmul(pln[:, 128:256], cl, cn, start=True, stop=True, skip_group_check=True)],
                            [curLn[h], curNn[h]], [pbn[4 + h]])
                        if h % 2 == 0:
                            ACT(dst, pln, AF.Copy, [pbn[4 + h]], [dn])
                        else:
                            DVE(lambda e, dst=dst, pln=pln: e.tensor_copy(dst, pln), [pbn[4 + h]], [dn])
                    else:
                        P.op("pe", lambda e, pln=pln, h=h, cl=curL[h], cn=curN[h]: e.matmul(pln[:, 0:128], cn, cl, start=True, stop=True, skip_group_check=True),
                             [curLn[h], curNn[h]], [pbn[4 + h]])
                        ACT(dst[:, 0:128], pln[:, 0:128], AF.Copy, [pbn[4 + h]], [dn])
                    curL[h] = dst[:, 0:128]
                    curN[h] = dst[:, 128:256]
                    curLn[h] = dn
                    curNn[h] = dn
                for h in range(4):
                    hn = "%d" % h
                    ppd = pb[h][:, 0:128]
                    src = Pm[h][lev % 2]
                    dstp = Pm[h][(lev + 1) % 2]
                    P.op("pe", lambda e, ppd=ppd, cl=curL[h], src=src: e.matmul(ppd, cl, src, start=True, stop=True, skip_group_check=True),
                         [curLn[h], "P%d_" % (lev % 2) + hn], [pbn[h]])
                    if lev < 4:
                        DVE(lambda e, dstp=dstp, ppd=ppd, src=src: e.tensor_tensor(dstp, ppd, src, ALU.add),
                            [pbn[h], "P%d_" % (lev % 2) + hn], ["P%d_" % ((lev + 1) % 2) + hn])
                    else:
                        DVE(lambda e, h=h, ppd=ppd, src=src: e.tensor_tensor(TTb[h], ppd, src, ALU.add),
                            [pbn[h], "P%d_" % (lev % 2) + hn], ["TTb" + hn])
            if cfg.get("pstop", 9) <= 6:
                return
            for h in range(4):
                hn = "%d" % h
                TT = TTb[h]
                P.op("pe", lambda e, h=h, d=d, TT=TT: [
                    e.matmul(pb[4][:, h * 64:(h + 1) * 64], TT, bv[d][:, h * 64:(h + 1) * 64], start=True, stop=True, skip_group_check=True),
                    e.matmul(pb[6][0:64, h * 128:(h + 1) * 128], bkg[d][:, h * 64:(h + 1) * 64], TT, start=True, stop=True, skip_group_check=True)],
                    ["TTb" + hn, "bv%d" % d, "bkg%d" % d], ["pb4", "pb6"])
            ACT(u_sb[d], pb[4][:, 0:256], AF.Copy, ["pb4"], ["u_sb%d" % d])
            DVE(lambda e, d=d: e.tensor_scalar(wT_sb[d][0:64, :], pb[6][0:64, 0:512], 1.0, None, ALU.mult), ["pb6"], ["wT%d" % d])

        def scan(p, d, need_o):
            s_ = sm[d]
            sn = "sm%d_" % d
            chunks = [0, 1] if d == 0 else [1, 0]
            S3 = S32[d].rearrange("p (h f) -> p h f", h=4)
            bw, bo, bs = (4, 5, 6) if d == 0 else (1, 2, 3)
            for cc in chunks:
                rs = slice(cc * 64, cc * 64 + 64)
                P.op("pe", lambda e, d=d: [e.matmul(pb[bw][:, h * 64:(h + 1) * 64], wT_sb[d][0:64, h * 128:(h + 1) * 128],
                                                    Sb[d][0:64, h * 64:(h + 1) * 64], start=True, stop=True, skip_group_check=True)
                                           for h in range(4)], ["wT%d" % d, "Sb_%d" % d], [pbn[bw]])
                yield
                DVE(lambda e, d=d, rs=rs: e.tensor_tensor(vnew[d][rs, :], u_sb[d][rs, :], pb[bw][rs, 0:256], ALU.subtract),
                    [pbn[bw], "u_sb%d" % d], ["vnew%d" % d])
                yield
                if need_o:
                    def omm(e, d=d, rs=rs):
                        r_ = []
                        for h in range(4):
                            o_ = pb[bo][:, h * 64:(h + 1) * 64]
                            r_.append(e.matmul(o_, qdT_sb[d][rs, h * 128:(h + 1) * 128], Sb[d][rs, h * 64:(h + 1) * 64],
                                               start=True, stop=False, skip_group_check=True))
                            r_.append(e.matmul(o_, atT_sb[d][rs, h * 128:(h + 1) * 128], vnew[d][rs, h * 64:(h + 1) * 64],
                                               start=False, stop=True, skip_group_check=True))
                        return r_
                    P.op("pe", omm, ["qdT%d" % d, "Sb_%d" % d, "atT%d" % d, "vnew%d" % d], [pbn[bo]])
                    yield
                    key = (p, cc)
                    if key not in visited:
                        visited.add(key)
                        ACT(o3[rs, p, :], pb[bo][rs, 0:256], AF.Copy, [pbn[bo]], ["oacc%d_%d" % (p, cc)])
                    else:
                        DVE(lambda e, rs=rs: e.tensor_tensor(o3[rs, p, :], o3[rs, p, :], pb[bo][rs, 0:256], ALU.add),
                            [pbn[bo], "oacc%d_%d" % (p, cc)], ["oacc%d_%d" % (p, cc)])
                    yield
                P.op("pe", lambda e, d=d, rs=rs: [e.matmul(pb[bs][:, h * 64:(h + 1) * 64], kdec[d][rs, h * 128:(h + 1) * 128],
                                                          vnew[d][rs, h * 64:(h + 1) * 64], start=True, stop=True, skip_group_check=True)
                                                 for h in range(4)], ["kdec%d" % d, "vnew%d" % d], [pbn[bs]])
                yield
                for half in range(2):
                    hr = slice(half * 64, half * 64 + 64)
                    eg = s_[hr, 8:12] if cc == half else s_[hr, 12:16]
                    DVE(lambda e, S3=S3, eg=eg, hr=hr: e.tensor_tensor(S3[hr], S3[hr], eg.unsqueeze(2).broadcast_to([64, 4, 64]), ALU.mult),
                        [sn + "egl", "S32_%d" % d], ["S32_%d" % d])
                yield
                DVE(lambda e, d=d: e.tensor_tensor(S32[d], S32[d], pb[bs][:, 0:256], ALU.add), [pbn[bs], "S32_%d" % d], ["S32_%d" % d])
                yield
                ACT(Sb[d], S32[d], AF.Copy, ["S32_%d" % d], ["Sb_%d" % d])
                yield

        def run_interleaved(gens):
            gens = list(gens)
            while gens:
                for g in list(gens):
                    try:
                        next(g)
                    except StopIteration:
                        gens.remove(g)

        for step in range(cfg.get("nsteps", NT)):
            for d in range(2):
                prep(order[d][step], d)
            if cfg.get("cstop", 9) <= 3:
                continue
            run_interleaved([scan(order[d][step], d, (order[d][step] >= 2) or ctx_out) for d in range(2)])
        if cfg.get("cstop", 9) <= 4:
            P.barrier()
            return
        dump("oacc", o_acc, ["oacc%d_%d" % (p, cc) for p in range(2, NT) for cc in range(2)])
        for p in range(NT):
            if p < 2 and not ctx_out:
                continue
            which = 1 if p < 2 else 0
            on = ["oacc%d_%d" % (p, cc) for cc in range(2)]
            ov = o3[:, p, :]
            DVE(lambda e, ov=ov: e.tensor_tensor(ysq, ov, ov, ALU.mult), on, ["ysq"])
            DVE(lambda e: e.reduce_sum(sm[0][:, 44:48], ysq.rearrange("p (h f) -> p h f", h=4), AX.X), ["ysq"], ["yss"])
            ACT(sm[0][:, 48:52], sm[0][:, 44:48], AF.Sqrt, ["yss"], ["ysd"], scale=1.0 / 64, bias=EPS)
            DVE(lambda e: e.reciprocal(sm[0][:, 52:56], sm[0][:, 48:52]), ["ysd"], ["yrs"])
            DVE(lambda e, ov=ov: e.tensor_tensor(ysq.rearrange("p (h f) -> p h f", h=4), ov.rearrange("p (h f) -> p h f", h=4),
                                                 sm[0][:, 52:56].unsqueeze(2).broadcast_to([128, 4, 64]), ALU.mult), on + ["yrs", "ysq"], ["ysq"])
            DVE(lambda e, p=p: e.tensor_tensor(yc, ysq, szc3[:, p, :], ALU.mult), ["ysq", "szc%d" % p], ["yc"])
            pT = pb[0].bitcast(BF16)
            TRS([(pT[:, c * 128:(c + 1) * 128], yc[:, c * 128:(c + 1) * 128]) for c in range(2)], ["yc", "ident_b"], ["pb0"], ident_b)
            ACT(ycT, pT[:, 0:256], AF.Copy, ["pb0"], ["ycT"])
            for half in range(2):
                MM(pb[1 + half], [(ycT[:, c * 128:(c + 1) * 128], woC3[:, c, half * 512:(half + 1) * 512]) for c in range(2)],
                   ["ycT", "woC"], [pbn[1 + half]])
            x_update(p, which, pb[1], "pb1", pb[2], "pb2", tmp, "tmpC")
        P.barrier()

    def final(b):
        arena.reset()
        fnw_bc = arena.alloc(DM)
        ostages = [arena.alloc(DM) for _ in range(2)]
        DMA("sp", fnw_bc, fnw_d[0:1, :].partition_broadcast(128), [], ["fnw_bc"])
        for p in range(2, NT):
            ostage = ostages[p % 2]
            osn = "ostage%d" % (p % 2)
            xs = X3[:, p, :]
            ss = stat[:, 0:1]
            sd = stat[:, 1:2]
            rstd = stat[:, 2:3]
            ACT(xn, xs, AF.Square, ["X%d" % p], ["xn", "st0"], accum_out=ss)
            ACT(sd, ss, AF.Sqrt, ["st0"], ["st1"], scale=1.0 / DM, bias=EPS)
            DVE(lambda e: e.reciprocal(rstd, sd), ["st1"], ["st2"])
            DVE(lambda e, xs=xs, ostage=ostage: e.scalar_tensor_tensor(out=ostage, in0=xs, scalar=rstd, in1=fnw_bc, op0=ALU.mult, op1=ALU.mult),
                ["X%d" % p, "st2", "fnw_bc"], [osn])
            DMA("sp", out_d[b, (p - 2) * 128:(p - 1) * 128, :], ostage, [osn], [])

    for b in range(NB):
        for l in range(nlayers):
            ctx_out = l < nlayers - 1
            mod_prep(b, l, ctx_out)
            phase1(b, l)
            P.barrier()
            if useA:
                phaseA(b, l, ctx_out)
            if useB:
                phaseB(b, l, ctx_out)
            if useC:
                phaseC(b, l, ctx_out)
        final(b)
        P.barrier()
    stats = P.emit()
    return nc, stats


def host_consts():
    ident = np.eye(128, dtype=np.float32)
    perm = np.zeros((128, 128), np.float32)
    cos = np.zeros((128, SEQ), np.float32)
    sin = np.zeros((128, SEQ), np.float32)
    t = np.arange(SEQ)
    rowp = (t // 64).astype(np.float32)
    colp = (t % 64).astype(np.float32)
    inv_freq = (10000.0 ** (-np.arange(0, 32, 2, dtype=np.float32) / 32.0)).astype(np.float32)
    for pp in range(128):
        d = pp % 64
        half = d // 32
        dd = d % 32
        f = dd % 16
        pos = rowp if half == 0 else colp
        ang = (pos * inv_freq[f]).astype(np.float32)
        cos[pp] = np.cos(ang)
        sgn = -1.0 if dd < 16 else 1.0
        sin[pp] = sgn * np.sin(ang)
        partner = pp + 16 if dd < 16 else pp - 16
        perm[partner, pp] = 1.0
    r = np.arange(128)[:, None]
    c = np.arange(128)[None, :]
    same = (r // 64) == (c // 64)
    tri_f = same & (r <= c)
    tri_b = same & (r >= c)
    gm = np.stack([tri_f, tri_b, same, ~same, same & (r > c), same & (r >= c), same & (r < c), same & (r <= c)], 0).astype(np.float32)
    return ident, perm, cos, sin, gm.transpose(1, 0, 2).reshape(128, 8 * 128).copy()


def make_in_maps(inputs, NB, ncores):
    ident, perm, cos, sin, gm = host_consts()
    f = lambda a: np.ascontiguousarray(np.asarray(a, dtype=np.float32))
    c = f(inputs["c"])
    c_ctx = f(inputs["c_ctx"])
    shared = {
        "w_ada": f(inputs["w_ada"]),
        "b_adaT": f(f(inputs["b_ada"]).reshape(2, 24, 128).transpose(2, 0, 1).reshape(128, 48)),
        "norm_wT": f(f(inputs["norm_w"]).reshape(2, 8, 128).transpose(2, 0, 1).reshape(128, 16)),
        "w_in": f(inputs["w_in"]),
        "w_out": f(inputs["w_out"]),
        "a_ln_w": f(inputs["a_ln_w"]),
        "a_wsT": f(f(inputs["a_ws"]).transpose(0, 3, 1, 2).reshape(2, 128, 512)),
        "a_bsT": f(f(inputs["a_bs"]).transpose(0, 2, 1)),
        "b_lam": f(f(inputs["b_lam"]).reshape(2, 256)),
        "b_subln_w": f(inputs["b_subln_w"]),
        "conv_wT": f(f(inputs["c_conv_w"]).reshape(2, 5, 6, 128).transpose(0, 3, 2, 1).reshape(2, 128, 30)),
        "c_a_log": f(f(inputs["c_a_log"]).reshape(2, 8)),
        "c_dt_bias": f(f(inputs["c_dt_bias"]).reshape(2, 8)),
        "c_norm_w": f(inputs["c_norm_w"]),
        "final_norm_w": f(f(inputs["final_norm_w"]).reshape(1, DM)),
        "ident": ident, "perm": perm, "ropecos": cos, "ropesin": sin, "gmasks": gm,
    }
    x = inputs["x"]
    ctx = inputs["ctx"]
    maps = []
    for i in range(ncores):
        b0 = i * NB
        rows = np.concatenate([c[b0:b0 + NB], c_ctx[None, :]], 0)
        cT = rows.reshape(NB + 1, 8, 128).transpose(2, 1, 0).reshape(128, 8 * (NB + 1))
        m = dict(shared)
        m["x"] = f(x[b0:b0 + NB])
        m["ctx"] = f(ctx[b0:b0 + NB])
        m["cT"] = f(cT)
        maps.append(m)
    return maps


_CACHE = {}


def kernel(**inputs):
    NB = 4
    key = ("full", NB)
    if key not in _CACHE:
        _CACHE[key] = build(NB, {})
    nc, _ = _CACHE[key]
    maps = make_in_maps(inputs, NB, NCORES)
    res = run_bass_kernel_spmd(nc, maps, core_ids=list(range(NCORES)))
    out = np.concatenate([r["out"] for r in res.results], axis=0)
    return out.astype(np.float32)
```

```python
import math
import numpy as np
import concourse.bass as bass
import concourse.mybir as mybir
from concourse.bass_utils import run_bass_kernel_spmd

F32 = mybir.dt.float32
BF16 = mybir.dt.bfloat16
AF = mybir.ActivationFunctionType
ALU = mybir.AluOpType
AX = mybir.AxisListType

NCORES = 8
DM = 1024
SEQ = 2048
CTXL = 256
T = SEQ + CTXL
NT = T // 128
DIN = 3856
EPS = 1e-6


class _Res:
    __slots__ = ("last_w", "readers")

    def __init__(self):
        self.last_w = None
        self.readers = []


class _Op:
    __slots__ = ("eng", "fn", "deps", "dma", "idx", "sig", "needed", "dsem", "dval", "dprev")

    def __init__(self, eng, fn, deps, dma, idx):
        self.eng = eng
        self.fn = fn
        self.deps = deps
        self.dma = dma
        self.idx = idx
        self.sig = None
        self.needed = False
        self.dsem = None
        self.dval = 0
        self.dprev = 0


class Prog:
    ENGS = ("pe", "act", "dve", "pool", "sp")

    def __init__(self, nc, n_dma_sems=8):
        self.nc = nc
        self.ops = []
        self.res = {}
        self.n_dma_sems = n_dma_sems
        self.cur_barrier = None
        self.last_on = {}
        self.dmas_since = []

    def _st(self, r):
        s = self.res.get(r)
        if s is None:
            s = self.res[r] = _Res()
        return s

    def op(self, eng, fn, reads=(), writes=(), dma=False):
        idx = len(self.ops)
        deps = set()
        for r in reads:
            st = self._st(r)
            if st.last_w is not None:
                deps.add(st.last_w)
        for w in writes:
            st = self._st(w)
            if st.last_w is not None:
                deps.add(st.last_w)
            deps.update(st.readers)
        for r in reads:
            self._st(r).readers.append(idx)
        for w in writes:
            st = self._st(w)
            st.last_w = idx
            st.readers = []
        if self.cur_barrier is not None:
            deps.add(self.cur_barrier)
        deps.discard(idx)
        self.ops.append(_Op(eng, fn, deps, dma, idx))
        if dma:
            self.dmas_since.append(idx)
        else:
            self.last_on[eng] = idx
        return idx

    def dma(self, eng, fn, reads=(), writes=()):
        return self.op(eng, fn, reads, writes, dma=True)

    def barrier(self):
        deps = set(self.last_on.values()) | set(self.dmas_since)
        idx = len(self.ops)
        o = _Op("pool", lambda e: e.nop(), deps, False, idx)
        self.ops.append(o)
        self.cur_barrier = idx
        self.last_on = {"pool": idx}
        self.dmas_since = []
        return idx

    def emit(self):
        nc = self.nc
        ops = self.ops
        for o in ops:
            for d in o.deps:
                ops[d].needed = True
        cnt = {e: 0 for e in self.ENGS}
        for o in ops:
            if o.dma:
                continue
            if o.needed:
                cnt[o.eng] += 1
                o.sig = cnt[o.eng]
        dcount = {}
        dma_n = {e: 0 for e in self.ENGS}
        for o in ops:
            if not o.dma:
                continue
            k = dma_n[o.eng] % self.n_dma_sems
            dma_n[o.eng] += 1
            key = (o.eng, k)
            o.dsem = key
            o.dprev = dcount.get(key, 0)
            o.dval = o.dprev + 16
            dcount[key] = o.dval
        sem = {}
        self._handles = []
        for e in self.ENGS:
            h = nc.semaphore("sg_" + e)
            self._handles.append(h)
            sem[e] = h.__enter__()
        for key in dcount:
            h = nc.semaphore("sd_%s_%d" % key)
            self._handles.append(h)
            sem[key] = h.__enter__()
        per_eng = {e: [o for o in ops if o.eng == e] for e in self.ENGS}
        stats = {"ops": len(ops), "waits": 0, "sig": dict(cnt)}

        def run_engine(ename, eng):
            seen = {}
            for o in per_eng[ename]:
                waits = {}
                for d in o.deps:
                    p = ops[d]
                    if p.dma:
                        key, val = p.dsem, p.dval
                    else:
                        if p.eng == ename and ename == "pe":
                            continue
                        key, val = p.eng, p.sig
                    if seen.get(key, 0) >= val:
                        continue
                    if waits.get(key, 0) < val:
                        waits[key] = val
                if o.dma and o.dprev > 0 and seen.get(o.dsem, 0) < o.dprev:
                    if waits.get(o.dsem, 0) < o.dprev:
                        waits[o.dsem] = o.dprev
                wl = list(waits.items())
                for key, val in wl:
                    seen[key] = val
                stats["waits"] += len(wl)
                for key, val in wl[:-1]:
                    eng.wait_ge(sem[key], val)
                r = o.fn(eng)
                if isinstance(r, (list, tuple)):
                    first, last = r[0], r[-1]
                else:
                    first = last = r
                if wl:
                    key, val = wl[-1]
                    first._wait_ge(sem[key], val)
                if o.dma:
                    last.then_inc(sem[o.dsem], 16)
                elif o.sig is not None:
                    last.then_inc(sem[ename], 1)
            if ename == "sp":
                for key, val in dcount.items():
                    if seen.get(key, 0) < val:
                        eng.wait_ge(sem[key], val)

        with nc.Block() as block:
            @block.tensor
            def _(eng):
                run_engine("pe", eng)

            @block.scalar
            def _(eng):
                run_engine("act", eng)

            @block.vector
            def _(eng):
                run_engine("dve", eng)

            @block.gpsimd
            def _(eng):
                run_engine("pool", eng)

            @block.sync
            def _(eng):
                run_engine("sp", eng)
        return stats


class Arena:
    def __init__(self, ap_f32):
        self.t = ap_f32
        self.cap = ap_f32.shape[1] * 4
        self.off = 0

    def reset(self):
        self.off = 0

    def alloc(self, n, dt=F32):
        size = n * (4 if dt == F32 else 2)
        size = (size + 63) // 64 * 64
        assert self.off + size <= self.cap, ("arena overflow", self.off, size, self.cap)
        v = self.t[:, self.off // 4:(self.off + size) // 4]
        self.off += size
        if dt != F32:
            v = v.bitcast(dt)
        return v[:, 0:n]


def build(NB, cfg):
    useA, useB, useC = cfg.get("A", True), cfg.get("B", True), cfg.get("C", True)
    nlayers = cfg.get("layers", 2)
    nc = bass.Bass("TRN2", target_bir_lowering=False)
    P = Prog(nc)
    RR = NB + 1

    def din(name, shape):
        return nc.dram_tensor(name, shape, F32, kind="ExternalInput").ap()

    x_d = din("x", [NB, SEQ, DM])
    ctx_d = din("ctx", [NB, CTXL, DM])
    cT_d = din("cT", [128, 8 * RR])
    w_ada_d = din("w_ada", [2, DM, 3 * DM])
    b_adaT_d = din("b_adaT", [128, 48])
    norm_wT_d = din("norm_wT", [128, 16])
    w_in_d = din("w_in", [2, DM, DIN])
    w_out_d = din("w_out", [2, DM, DM])
    a_ln_w_d = din("a_ln_w", [2, 256])
    a_wsT_d = din("a_wsT", [2, 128, 512])
    a_bsT_d = din("a_bsT", [2, 128, 4])
    b_lam_d = din("b_lam", [2, 256])
    b_subln_d = din("b_subln_w", [2, 128])
    conv_wT_d = din("conv_wT", [2, 128, 30])
    c_alog_d = din("c_a_log", [2, 8])
    c_dtb_d = din("c_dt_bias", [2, 8])
    c_normw_d = din("c_norm_w", [2, 64])
    fnw_d = din("final_norm_w", [1, DM])
    ident_d = din("ident", [128, 128])
    perm_d = din("perm", [128, 128])
    cos_d = din("ropecos", [128, SEQ])
    sin_d = din("ropesin", [128, SEQ])
    gmask_d = din("gmasks", [128, 8 * 128])
    out_d = nc.dram_tensor("out", [NB, SEQ, DM], F32, kind="ExternalOutput").ap()
    dbg_d = {}
    for nm, shp in cfg.get("dbg", {}).items():
        dbg_d[nm] = nc.dram_tensor("dbg_" + nm, shp, F32, kind="ExternalOutput").ap()

    cnt = [0]

    def sb(shape, dt=F32, name=None):
        cnt[0] += 1
        return nc.alloc_sbuf_tensor("s_" + (name or ("t%d" % cnt[0])), shape, dt).ap()

    pb = [nc.alloc_psum_tensor("pb%d" % i, [128, 512], F32).ap() for i in range(8)]
    pbn = ["pb%d" % i for i in range(8)]

    X = sb([128, NT * DM], F32, "X")
    X3 = X.rearrange("p (t d) -> p t d", t=NT)
    hT = sb([128, 8 * T], BF16, "hT")
    hT3 = hT.rearrange("p (k t) -> p k t", k=8)
    ident_f = sb([128, 128], F32, "ident_f")
    ident_b = sb([128, 128], BF16, "ident_b")
    ones_f = sb([128, 128], F32, "ones_f")
    cT = sb([128, 8 * RR], F32, "cT")
    cact = sb([128, 8 * RR], BF16, "cact")
    cact3 = cact.rearrange("p (k r) -> p k r", k=8)
    b_adaT = sb([128, 48], F32, "b_adaT")
    norm_wT = sb([128, 16], F32, "norm_wT")
    modT = [sb([128, 24 * RR], F32, "modT%d" % l) for l in range(2)]
    modT3 = [m.rearrange("p (t r) -> p t r", t=24) for m in modT]
    g1T = [sb([128, 8], F32, "g1T%d" % i) for i in range(2)]
    gate_bc = [sb([128, DM], F32, "gate_bc%d" % i) for i in range(2)]
    dg = [sb([128, 128], F32, "dg%d" % i) for i in range(2)]
    xn = sb([128, DM], BF16, "xn")
    stat = sb([128, 64], F32, "stat")
    ARENA_BYTES = cfg.get("arena", 82 * 1024)
    arena = Arena(sb([128, ARENA_BYTES // 4], F32, "arena"))

    def ACT(out, in_, func, reads, writes, **kw):
        P.op("act", lambda e: e.activation(out, in_, func, **kw), reads, writes)

    def MM(out, pairs, reads, writes):
        def fn(e):
            n = len(pairs)
            return [e.matmul(out, l, r, start=(i == 0), stop=(i == n - 1)) for i, (l, r) in enumerate(pairs)]
        P.op("pe", fn, reads, writes)

    def TRS(items, reads, writes, ident):
        P.op("pe", lambda e: [e.transpose(o, i, ident) for (o, i) in items], reads, writes)

    def DVE(fn, reads, writes):
        P.op("dve", fn, reads, writes)

    def POOL(fn, reads, writes):
        P.op("pool", fn, reads, writes)

    def DMA(q, out, in_, reads, writes):
        P.dma(q, lambda e: e.dma_start(out=out, in_=in_), reads, writes)

    def dump(name, src_ap, reads, dst=None):
        if name in dbg_d:
            d = dbg_d[name] if dst is None else dst
            DMA("pool", d, src_ap, reads, [])

    DMA("sp", ident_f, ident_d, [], ["ident_f"])
    DMA("pool", ident_b, ident_d, [], ["ident_b"])
    POOL(lambda e: e.memset(ones_f, 1.0), [], ["ones_f"])
    DMA("sp", cT, cT_d, [], ["cT"])
    DMA("sp", b_adaT, b_adaT_d, [], ["b_adaT"])
    DMA("sp", norm_wT, norm_wT_d, [], ["norm_wT"])
    ACT(cact, cT, AF.Silu, ["cT"], ["cact"])

    arena.reset()
    wa_buf = [arena.alloc(8 * 512, BF16) for _ in range(2)]
    for l in range(nlayers):
        wsrc = w_ada_d[l].rearrange("(k p) n -> p k n", p=128)
        for g in range(6):
            wa = wa_buf[g % 2]
            wa3 = wa.rearrange("p (k n) -> p k n", k=8)
            wn = "wa%d" % (g % 2)
            DMA("pool", wa3, wsrc[:, :, g * 512:(g + 1) * 512], [], [wn])
            for t in range(4):
                tt = g * 4 + t
                MM(pb[0][:, tt * RR:(tt + 1) * RR],
                   [(wa3[:, k, t * 128:(t + 1) * 128], cact3[:, k, :]) for k in range(8)],
                   [wn, "cact"], ["pb0"])
        DVE(lambda e, l=l: e.tensor_tensor(
            modT3[l], pb[0][:, 0:24 * RR].rearrange("p (t r) -> p t r", t=24),
            b_adaT[:, l * 24:(l + 1) * 24].unsqueeze(2).broadcast_to([128, 24, RR]), ALU.add),
            ["pb0", "b_adaT"], ["modT%d" % l])
    P.barrier()

    def mod_prep(b, l, ctx_out):
        rows = [(0, b)] + ([(1, NB)] if True else [])
        for which, r in rows:
            DVE(lambda e, which=which, r=r: e.scalar_tensor_tensor(
                out=g1T[which], in0=modT3[l][:, 8:16, r], scalar=1.0, in1=norm_wT[:, l * 8:(l + 1) * 8],
                op0=ALU.add, op1=ALU.mult), ["modT%d" % l, "norm_wT"], ["g1T%d" % which])
            if which == 1 and not ctx_out:
                continue
            for k in range(8):
                d = dg[k % 2]
                dn = "dg%d" % (k % 2)
                DVE(lambda e, d=d, k=k, r=r: e.tensor_scalar(d, ident_f, modT3[l][:, 16 + k, r:r + 1], None, ALU.mult),
                    ["ident_f", "modT%d" % l], [dn])
                bank = 6 + k // 4
                MM(pb[bank][:, (k % 4) * 128:(k % 4 + 1) * 128], [(ones_f, d)], ["ones_f", dn], [pbn[bank]])
            ACT(gate_bc[which][:, 0:512], pb[6], AF.Copy, ["pb6"], ["gate_bc%d" % which])
            ACT(gate_bc[which][:, 512:1024], pb[7], AF.Copy, ["pb7"], ["gate_bc%d" % which])

    def phase1(b, l):
        if l == 0:
            DMA("sp", X3[:, 0:2, :], ctx_d[b].rearrange("(t p) d -> p t d", p=128), [], ["X0", "X1"])
            for g in range(4):
                DMA("sp", X3[:, 2 + 4 * g:6 + 4 * g, :],
                    x_d[b, g * 512:(g + 1) * 512, :].rearrange("(t p) d -> p t d", p=128),
                    [], ["X%d" % (2 + 4 * g + i) for i in range(4)])
        for p in range(NT):
            which = 1 if p < 2 else 0
            r = NB if p < 2 else b
            xs = X3[:, p, :]
            ss = stat[:, 0:1]
            sd = stat[:, 1:2]
            rstd = stat[:, 2:3]
            ACT(xn, xs, AF.Square, ["X%d" % p], ["xn", "st0"], accum_out=ss)
            ACT(sd, ss, AF.Sqrt, ["st0"], ["st1"], scale=1.0 / DM, bias=EPS)
            DVE(lambda e: e.reciprocal(rstd, sd), ["st1"], ["st2"])
            ACT(xn, xs, AF.Copy, ["X%d" % p, "st2"], ["xn"], scale=rstd)
            pT = pb[p % 2].bitcast(BF16)
            pTn = pbn[p % 2]
            TRS([(pT[:, k * 128:(k + 1) * 128], xn[:, k * 128:(k + 1) * 128]) for k in range(8)],
                ["xn", "ident_b"], [pTn], ident_b)
            for k in range(8):
                o = hT3[:, k, p * 128:(p + 1) * 128]
                i_ = pT[:, k * 128:(k + 1) * 128]
                sc = g1T[which][:, k:k + 1]
                bi = modT3[l][:, k, r:r + 1]
                if k % 2 == 0:
                    ACT(o, i_, AF.Identity, [pTn, "g1T%d" % which, "modT%d" % l], ["hT%d" % p], scale=sc, bias=bi)
                else:
                    DVE(lambda e, o=o, i_=i_, sc=sc, bi=bi: e.tensor_scalar(o, i_, sc, bi, ALU.mult, ALU.add),
                        [pTn, "g1T%d" % which, "modT%d" % l], ["hT%d" % p])

    def x_update(p, which, pa, pan, pbk, pbkn, tmp, tmpn):
        for half, (bank, bn) in enumerate(((pa, pan), (pbk, pbkn))):
            sl = slice(half * 512, (half + 1) * 512)
            DVE(lambda e, bank=bank, sl=sl: e.tensor_tensor(tmp[:, sl], bank, gate_bc[which][:, sl], ALU.mult),
                [bn, "gate_bc%d" % which], [tmpn])
            POOL(lambda e, sl=sl: e.tensor_tensor(X3[:, p, sl], X3[:, p, sl], tmp[:, sl], ALU.add),
                 [tmpn, "X%d" % p], ["X%d" % p])

    def phaseA(b, l, ctx_out):
        arena.reset()
        wA = arena.alloc(8 * 768, BF16)
        wA3 = wA.rearrange("p (k n) -> p k n", k=8)
        woA = arena.alloc(2 * DM, BF16)
        woA3 = woA.rearrange("p (k n) -> p k n", k=2)
        awsT = arena.alloc(512, BF16)
        abs_ = arena.alloc(4, F32)
        aln = arena.alloc(256, F32)
        vsb = arena.alloc(256, F32)
        vc = arena.alloc(256, BF16)
        sz = arena.alloc(256, F32)
        uz = arena.alloc(256, F32)
        ya = arena.alloc(256, BF16)
        yT = arena.alloc(256, BF16)
        tmp = arena.alloc(DM, F32)
        st = arena.alloc(16, F32)
        DMA("pool", wA3, w_in_d[l].rearrange("(k p) n -> p k n", p=128)[:, :, 0:768], [], ["wA"])
        DMA("pool", woA3, w_out_d[l, 0:256, :].rearrange("(k p) n -> p k n", p=128), [], ["woA"])
        DMA("pool", awsT, a_wsT_d[l], [], ["awsT"])
        DMA("sp", abs_, a_bsT_d[l], [], ["abs"])
        DMA("sp", aln, a_ln_w_d[l:l + 1, :].partition_broadcast(128), [], ["aln"])
        for p in range(NT):
            if p < 2 and not ctx_out:
                continue
            which = 1 if p < 2 else 0
            tok = slice(p * 128, (p + 1) * 128)
            MM(pb[2], [(hT3[:, k, tok], wA3[:, k, 0:512]) for k in range(8)], ["hT%d" % p, "wA"], ["pb2"])
            MM(pb[3][:, 0:256], [(hT3[:, k, tok], wA3[:, k, 512:768]) for k in range(8)], ["hT%d" % p, "wA"], ["pb3"])
            DVE(lambda e: e.bn_stats(st[:, 0:6], pb[2][:, 256:512]), ["pb2"], ["Ast"])
            DVE(lambda e: e.bn_aggr(st[:, 6:8], st[:, 0:6]), ["Ast"], ["Amv"])
            ACT(st[:, 8:9], st[:, 7:8], AF.Sqrt, ["Amv"], ["Asd"], bias=1e-5)
            DVE(lambda e: e.reciprocal(st[:, 9:10], st[:, 8:9]), ["Asd"], ["Ars"])
            DVE(lambda e: e.scalar_tensor_tensor(out=st[:, 10:11], in0=st[:, 6:7], scalar=-1.0, in1=st[:, 9:10],
                                                 op0=ALU.mult, op1=ALU.mult), ["Amv", "Ars"], ["Anb"])
            ACT(vsb, pb[2][:, 256:512], AF.Identity, ["pb2", "Ars", "Anb"], ["vsb"], scale=st[:, 9:10], bias=st[:, 10:11])
            DVE(lambda e: e.tensor_tensor(vc, vsb, aln, ALU.mult), ["vsb", "aln"], ["vc"])
            awsT3 = awsT.rearrange("p (g i) -> p g i", g=4)
            P.op("pe", lambda e: [e.matmul(pb[7][:, g * 64:(g + 1) * 64], awsT3[:, g, :], vc[:, g * 64:(g + 1) * 64],
                                           start=True, stop=True) for g in range(4)], ["awsT", "vc"], ["pb7"])
            ACT(sz, pb[3][:, 0:256], AF.Silu, ["pb3"], ["sz"])
            DVE(lambda e: e.tensor_tensor(uz, pb[2][:, 0:256], sz, ALU.mult), ["pb2", "sz"], ["uz"])
            for g in range(4):
                cs = slice(g * 64, (g + 1) * 64)
                DVE(lambda e, g=g, cs=cs: e.scalar_tensor_tensor(
                    out=ya[:, cs], in0=pb[7][:, g * 64:(g + 1) * 64], scalar=abs_[:, g:g + 1], in1=uz[:, cs],
                    op0=ALU.add, op1=ALU.mult), ["pb7", "uz", "abs"], ["ya"])
            pT = pb[4].bitcast(BF16)
            TRS([(pT[:, c * 128:(c + 1) * 128], ya[:, c * 128:(c + 1) * 128]) for c in range(2)], ["ya", "ident_b"], ["pb4"], ident_b)
            ACT(yT, pT[:, 0:256], AF.Copy, ["pb4"], ["yT"])
            for half in range(2):
                MM(pb[5 + half], [(yT[:, c * 128:(c + 1) * 128], woA3[:, c, half * 512:(half + 1) * 512]) for c in range(2)],
                   ["yT", "woA"], [pbn[5 + half]])
            x_update(p, which, pb[5], "pb5", pb[6], "pb6", tmp, "tmpA")
        P.barrier()

    def phaseB(b, l, ctx_out):
        arena.reset()
        lam_init = 0.8 - 0.6 * math.exp(-0.3 * l)
        cosb = arena.alloc(SEQ, BF16)
        sinb = arena.alloc(SEQ, BF16)
        permb = arena.alloc(128, BF16)
        woB = arena.alloc(4 * DM, BF16)
        woB3 = woB.rearrange("p (k n) -> p k n", k=4)
        lamb = arena.alloc(256, F32)
        lamb4 = lamb.rearrange("p (a w d) -> p a w d", a=2, w=2)
        lprod = arena.alloc(128, F32)
        lst = arena.alloc(16, F32)
        sublnw = arena.alloc(128, F32)
        wB = arena.alloc(8 * 512, BF16)
        wB3 = wB.rearrange("p (k n) -> p k n", k=8)
        qT = arena.alloc(T, BF16)
        kT = arena.alloc(T, BF16)
        vaug = arena.alloc(NT * 130, BF16)
        vaug3 = vaug.rearrange("p (t c) -> p t c", t=NT)
        szw = arena.alloc(NT * 128, BF16)
        szw3 = szw.rearrange("p (t c) -> p t c", t=NT)
        qraw = [arena.alloc(512, BF16) for _ in range(2)]
        t1 = [arena.alloc(512, F32) for _ in range(2)]
        t2 = [arena.alloc(512, F32) for _ in range(2)]
        PT = [arena.alloc(512, BF16) for _ in range(3)]
        szf = arena.alloc(128, F32)
        o32s = [arena.alloc(128, F32) for _ in range(4)]
        junks = [arena.alloc(128, BF16) for _ in range(4)]
        ybs = [arena.alloc(128, BF16) for _ in range(4)]
        yTb = arena.alloc(128, BF16)
        fsts = [arena.alloc(16, F32) for _ in range(4)]
        tmp = arena.alloc(DM, F32)
        DMA("pool", cosb, cos_d, [], ["cosb"])
        DMA("pool", sinb, sin_d, [], ["sinb"])
        DMA("pool", permb, perm_d, [], ["permb"])
        DMA("pool", woB3, w_out_d[l, 256:768, :].rearrange("(k p) n -> p k n", p=128), [], ["woB"])
        DMA("sp", lamb, b_lam_d[l:l + 1, :].partition_broadcast(128), [], ["lamb"])
        DMA("sp", sublnw, b_subln_d[l:l + 1, :].partition_broadcast(128), [], ["sublnw"])
        ACT(sublnw, sublnw, AF.Copy, ["sublnw"], ["sublnw"], scale=(1.0 - lam_init))
        POOL(lambda e: e.memset(vaug3[:, :, 128:130], 1.0), [], ["vaug_ones"])
        lprod3 = lprod.rearrange("p (a d) -> p a d", a=2)
        DVE(lambda e: e.tensor_tensor(lprod3, lamb4[:, :, 0, :], lamb4[:, :, 1, :], ALU.mult), ["lamb"], ["lprod"])
        DVE(lambda e: e.reduce_sum(lst[:, 0:2], lprod3, AX.X), ["lprod"], ["lst0"])
        ACT(lst[:, 2:4], lst[:, 0:2], AF.Exp, ["lst0"], ["lst1"])
        DVE(lambda e: e.tensor_tensor(lst[:, 4:5], lst[:, 3:4], lst[:, 2:3], ALU.subtract), ["lst1"], ["lst2"])
        DVE(lambda e: e.tensor_scalar(lst[:, 5:6], lst[:, 4:5], -lam_init, None, ALU.add), ["lst2"], ["nlam"])
        nlam = lst[:, 5:6]
        wsrc = w_in_d[l].rearrange("(k p) n -> p k n", p=128)
        groups = [(0, 256, [0, 1])] + [(256 + 512 * g, 512, [2 + 4 * g + i for i in range(4)]) for g in range(4)]
        for hp in range(4):
            for ci, c0 in enumerate((768, 1280, 1792, 2304)):
                DMA("pool", wB3[:, :, ci * 128:(ci + 1) * 128], wsrc[:, :, c0 + hp * 128:c0 + (hp + 1) * 128], [], ["wB%d" % ci])
            cntr = 0
            for wi, (dst, dname) in enumerate(((qT, "qT"), (kT, "kT"))):
                for gi, (t0, n, tiles) in enumerate(groups):
                    if wi == 0 and gi == 0 and not ctx_out:
                        continue
                    pp = pb[cntr % 2]
                    ppn = pbn[cntr % 2]
                    MM(pp[:, 0:n], [(wB3[:, k, wi * 128:(wi + 1) * 128], hT3[:, k, t0:t0 + n]) for k in range(8)],
                       ["wB%d" % wi] + ["hT%d" % p for p in tiles], [ppn])
                    rn = "%s%d" % (dname, gi)
                    if gi == 0:
                        ACT(dst[:, t0:t0 + n], pp[:, 0:n], AF.Copy, [ppn], [rn])
                    else:
                        qr = qraw[cntr % 2]
                        qrn = "qraw%d" % (cntr % 2)
                        pq = pb[2 + cntr % 2]
                        pqn = pbn[2 + cntr % 2]
                        ta = t1[cntr % 2]
                        tan = "t1_%d" % (cntr % 2)
                        tb = t2[cntr % 2]
                        tbn = "t2_%d" % (cntr % 2)
                        ps = slice(t0 - 256, t0 - 256 + n)
                        ACT(qr, pp, AF.Copy, [ppn], [qrn])
                        MM(pq, [(permb, qr)], ["permb", qrn], [pqn])
                        POOL(lambda e, ta=ta, qr=qr, ps=ps: e.tensor_tensor(ta, qr, cosb[:, ps], ALU.mult), [qrn, "cosb"], [tan])
                        DVE(lambda e, tb=tb, pq=pq, ps=ps: e.tensor_tensor(tb, pq, sinb[:, ps], ALU.mult), [pqn, "sinb"], [tbn])
                        POOL(lambda e, dst=dst, t0=t0, n=n, ta=ta, tb=tb: e.tensor_tensor(dst[:, t0:t0 + n], ta, tb, ALU.add),
                             [tan, tbn], [rn])
                    cntr += 1
            for p in range(NT):
                tok = slice(p * 128, (p + 1) * 128)
                need_z = (p >= 2) or ctx_out
                ncols = 256 if need_z else 128
                pp = pb[4 + p % 2]
                ppn = pbn[4 + p % 2]
                MM(pp[:, 0:ncols], [(hT3[:, k, tok], wB3[:, k, 256:256 + ncols]) for k in range(8)],
                   ["hT%d" % p, "wB2", "wB3"], [ppn])
                ACT(vaug3[:, p, 0:128], pp[:, 0:128], AF.Copy, [ppn], ["vaug%d" % p])
                if need_z:
                    ACT(szf, pp[:, 128:256], AF.Silu, [ppn], ["szf"])
                    DVE(lambda e, p=p: e.tensor_tensor(szw3[:, p, :], szf, sublnw, ALU.mult), ["szf", "sublnw"], ["szw%d" % p])
            def kgroup(kt):
                return 0 if kt < 2 else 1 + (kt - 2) // 4
            qgroups = [(gi, g) for gi, g in enumerate(groups) if gi > 0]
            if ctx_out:
                qgroups = [(0, groups[0])] + qgroups
            iters = []
            for gi, (t0, n, tiles) in qgroups:
                keyt = [0, 1] if gi == 0 else list(range(NT))
                for j in range(2):
                    for ki, kt in enumerate(keyt):
                        iters.append((gi, t0, n, tiles, j, ki, kt, len(keyt)))

            def oslot_of(nq, j, qi):
                s_ = j * nq + qi
                return pb[2 + s_ // 3][:, (s_ % 3) * 129:(s_ % 3) * 129 + 129]

            def emit_st(ei):
                gi, t0, n, tiles, j, ki, kt, nk = iters[ei]
                hs = slice(j * 64, (j + 1) * 64)
                MM(pb[ei % 2][:, 0:n], [(kT[hs, kt * 128:(kt + 1) * 128], qT[hs, t0:t0 + n])],
                   ["kT%d" % kgroup(kt), "qT%d" % gi], [pbn[ei % 2]])

            def fin_tile(nq, qi, p):
                which = 1 if p < 2 else 0
                fst = fsts[qi]
                o32 = o32s[qi]
                junk = junks[qi]
                yb = ybs[qi]
                q_ = "_%d" % qi
                O0 = oslot_of(nq, 0, qi)
                O1 = oslot_of(nq, 1, qi)
                b0n = pbn[2 + (0 * nq + qi) // 3]
                b1n = pbn[2 + (1 * nq + qi) // 3]
                DVE(lambda e: e.reciprocal(fst[:, 0:1], O0[:, 128:129]), [b0n], ["f0" + q_])
                DVE(lambda e: e.reciprocal(fst[:, 1:2], O1[:, 128:129]), [b1n], ["f1" + q_])
                yield
                DVE(lambda e: e.tensor_tensor(fst[:, 2:3], fst[:, 1:2], nlam, ALU.mult), ["f1" + q_, "nlam"], ["f2" + q_])
                DVE(lambda e: e.tensor_scalar(o32, O0[:, 0:128], fst[:, 0:1], None, ALU.mult), [b0n, "f0" + q_], ["o32" + q_])
                yield
                DVE(lambda e: e.scalar_tensor_tensor(out=o32, in0=O1[:, 0:128], scalar=fst[:, 2:3], in1=o32,
                                                     op0=ALU.mult, op1=ALU.add), [b1n, "f2" + q_, "o32" + q_], ["o32" + q_])
                yield
                ACT(junk, o32, AF.Square, ["o32" + q_], ["junkB" + q_, "f3" + q_], accum_out=fst[:, 3:4])
                yield
                ACT(fst[:, 4:5], fst[:, 3:4], AF.Sqrt, ["f3" + q_], ["f4" + q_], scale=1.0 / 128, bias=EPS)
                yield
                DVE(lambda e: e.reciprocal(fst[:, 5:6], fst[:, 4:5]), ["f4" + q_], ["f5" + q_])
                yield
                DVE(lambda e: e.scalar_tensor_tensor(out=yb, in0=o32, scalar=fst[:, 5:6], in1=szw3[:, p, :],
                                                     op0=ALU.mult, op1=ALU.mult), ["o32" + q_, "f5" + q_, "szw%d" % p], ["yb" + q_])
                yield
                pT = pb[5].bitcast(BF16)
                TRS([(pT[:, 0:128], yb)], ["yb" + q_, "ident_b"], ["pb5"], ident_b)
                ACT(yTb, pT[:, 0:128], AF.Copy, ["pb5"], ["yTb"])
                for half in range(2):
                    MM(pb[6 + half], [(yTb, woB3[:, hp, half * 512:(half + 1) * 512])], ["yTb", "woB"], [pbn[6 + half]])
                x_update(p, which, pb[6], "pb6", pb[7], "pb7", tmp, "tmpB")
                yield

            def emit_finalize(gi, tiles):
                gens = [fin_tile(len(tiles), qi, p) for qi, p in enumerate(tiles)]
                while gens:
                    for g in list(gens):
                        try:
                            next(g)
                        except StopIteration:
                            gens.remove(g)

            if iters:
                emit_st(0)
            for ei, (gi, t0, n, tiles, j, ki, kt, nk) in enumerate(iters):
                nq = len(tiles)
                sp_ = pb[ei % 2]
                spn = pbn[ei % 2]
                pt = PT[ei % 3]
                ptn = "PT%d" % (ei % 3)
                ACT(pt[:, 0:n], sp_[:, 0:n], AF.Exp, [spn], [ptn], scale=0.125)
                if ei + 1 < len(iters):
                    emit_st(ei + 1)

                def pv(e, j=j, kt=kt, ki=ki, pt=pt, nq=nq, nk=nk):
                    seen_b = set()
                    r_ = []
                    for qi in range(nq):
                        bank = 2 + (j * nq + qi) // 3
                        st_ = (ki == 0) and (bank not in seen_b)
                        seen_b.add(bank)
                        r_.append(e.matmul(oslot_of(nq, j, qi), pt[:, qi * 128:(qi + 1) * 128], vaug3[:, kt, 0:129],
                                           start=st_, stop=(ki == nk - 1), skip_group_check=True))
                    return r_
                P.op("pe", pv, [ptn, "vaug%d" % kt, "vaug_ones"], ["pb2", "pb3", "pb4"])
                if j == 1 and ki == nk - 1:
                    emit_finalize(gi, tiles)
        P.barrier()

    TC = T + 4

    def coff(p):
        return p * 128 if p < 2 else 4 + p * 128

    def phaseC(b, l, ctx_out):
        arena.reset()
        A = arena
        convw = A.alloc(30)
        alog = A.alloc(8)
        dtb = A.alloc(8)
        nea = A.alloc(8)
        cn4 = A.alloc(256)
        gm = A.alloc(8 * 128)
        gm3 = gm.rearrange("p (m c) -> p m c", m=8)
        sameb = A.alloc(128, BF16)
        negones = A.alloc(128)
        woC = A.alloc(2 * DM, BF16)
        woC3 = woC.rearrange("p (k n) -> p k n", k=2)
        qkvT = A.alloc(6 * TC, BF16)
        qkvT3 = qkvT.rearrange("p (c t) -> p c t", c=6)
        szc = A.alloc(NT * 256, BF16)
        szc3 = szc.rearrange("p (t c) -> p t c", t=NT)
        g_all = A.alloc(NT * 8)
        g3 = g_all.rearrange("p (t c) -> p t c", t=NT)
        beta_all = A.alloc(NT * 8)
        beta3 = beta_all.rearrange("p (t c) -> p t c", t=NT)
        lnb_all = A.alloc(NT * 8)
        lnb3 = lnb_all.rearrange("p (t c) -> p t c", t=NT)
        mark = A.off
        xin = A.alloc(TC + 4, BF16)
        acc = A.alloc(TC)
        sqb = xin[:, 0:TC]
        wC = [A.alloc(8 * 128, BF16) for _ in range(2)]
        wCz = A.alloc(8 * 272, BF16)
        wCz3 = wCz.rearrange("p (k n) -> p k n", k=8)
        rin = [A.alloc(512) for _ in range(2)]
        szt = A.alloc(256)
        gt = A.alloc(64)
        DMA("sp", convw, conv_wT_d[l], [], ["convw"])
        DMA("sp", alog, c_alog_d[l:l + 1, :].partition_broadcast(128), [], ["alog"])
        DMA("sp", dtb, c_dtb_d[l:l + 1, :].partition_broadcast(128), [], ["dtb"])
        for h in range(4):
            DMA("sp", cn4[:, h * 64:(h + 1) * 64], c_normw_d[l:l + 1, :].partition_broadcast(128), [], ["cn4"])
        DMA("sp", gm, gmask_d, [], ["gm"])
        DMA("pool", sameb, gmask_d[:, 256:384], [], ["sameb"])
        POOL(lambda e: e.memset(negones, -1.0), [], ["negones"])
        DMA("pool", woC3, w_out_d[l, 768:1024, :].rearrange("(k p) n -> p k n", p=128), [], ["woC"])
        POOL(lambda e: e.memset(xin, 0.0), [], ["xin"])
        ACT(nea, alog, AF.Exp, ["alog"], ["nea"])
        DVE(lambda e: e.tensor_scalar(nea, nea, -1.0, None, ALU.mult), ["nea"], ["nea"])
        wsrc = w_in_d[l].rearrange("(k p) n -> p k n", p=128)
        DMA("pool", wCz3, wsrc[:, :, 3584:3856], [], ["wCz"])
        groups = [(0, 256, [0, 1])] + [(256 + 512 * g, 512, [2 + 4 * g + i for i in range(4)]) for g in range(4)]
        convw3 = convw.rearrange("p (c j) -> p c j", c=6)
        for ct in range(6):
            w_ = wC[ct % 2]
            w3 = w_.rearrange("p (k n) -> p k n", k=8)
            wn = "wC%d" % (ct % 2)
            DMA("pool", w3, wsrc[:, :, 2816 + ct * 128:2816 + (ct + 1) * 128], [], [wn])
            if ct > 0:
                POOL(lambda e: e.memset(xin, 0.0), [], ["xin"])
            for gi, (t0, n, tiles) in enumerate(groups):
                pp = pb[gi % 2]
                ppn = pbn[gi % 2]
                MM(pp[:, 0:n], [(w3[:, k, :], hT3[:, k, t0:t0 + n]) for k in range(8)], [wn] + ["hT%d" % p for p in tiles], [ppn])
                c0 = 2 + (t0 if gi == 0 else t0 + 4)
                ACT(xin[:, c0:c0 + n], pp[:, 0:n], AF.Copy, [ppn], ["xin"])
            DVE(lambda e, ct=ct: e.tensor_scalar(acc, xin[:, 0:TC], convw3[:, ct, 0:1], None, ALU.mult), ["xin", "convw"], ["acc"])
            for j in range(1, 5):
                DVE(lambda e, ct=ct, j=j: e.scalar_tensor_tensor(out=acc, in0=xin[:, j:j + TC], scalar=convw3[:, ct, j:j + 1], in1=acc,
                                                                 op0=ALU.mult, op1=ALU.add), ["xin", "convw", "acc"], ["acc"])
            if ct >= 4:
                ACT(qkvT3[:, ct, :], acc, AF.Silu, ["acc"], ["qkvT%d" % ct])
            else:
                ACT(acc, acc, AF.Silu, ["acc"], ["acc"])
                POOL(lambda e: e.tensor_tensor(sqb, acc, acc, ALU.mult), ["acc"], ["xin"])
                for ci, c0 in enumerate(range(0, TC, 512)):
                    n = min(512, TC - c0)
                    pp = pb[2 + ci % 2]
                    ppn = pbn[2 + ci % 2]
                    r_ = rin[ci % 2]
                    rn = "rin%d" % (ci % 2)
                    MM(pp[:, 0:n], [(sameb, sqb[:, c0:c0 + n])], ["sameb", "xin"], [ppn])
                    ACT(r_[:, 0:n], pp[:, 0:n], AF.Sqrt, [ppn], [rn], bias=1e-6)
                    DVE(lambda e, r_=r_, n=n: e.reciprocal(r_[:, 0:n], r_[:, 0:n]), [rn], [rn])
                    sc = 0.125 if ct < 2 else 1.0
                    DVE(lambda e, ct=ct, c0=c0, n=n, r_=r_, sc=sc: e.scalar_tensor_tensor(
                        out=qkvT3[:, ct, c0:c0 + n], in0=acc[:, c0:c0 + n], scalar=sc, in1=r_[:, 0:n], op0=ALU.mult, op1=ALU.mult),
                        ["acc", rn], ["qkvT%d" % ct])
        for p in range(NT):
            tok = slice(p * 128, (p + 1) * 128)
            pp = pb[4 + p % 2]
            ppn = pbn[4 + p % 2]
            MM(pp[:, 0:272], [(hT3[:, k, tok], wCz3[:, k, :]) for k in range(8)], ["hT%d" % p, "wCz"], [ppn])
            if p >= 2 or ctx_out:
                ACT(szt, pp[:, 0:256], AF.Silu, [ppn], ["szt"])
                DVE(lambda e, p=p: e.tensor_tensor(szc3[:, p, :], szt, cn4, ALU.mult), ["szt", "cn4"], ["szc%d" % p])
            ACT(gt[:, 0:8], pp[:, 256:264], AF.Exp, [ppn], ["gt0"], scale=-1.0)
            DVE(lambda e: e.tensor_scalar(gt[:, 0:8], gt[:, 0:8], 1.0, None, ALU.add), ["gt0"], ["gt0"])
            DVE(lambda e, p=p: e.reciprocal(beta3[:, p, :], gt[:, 0:8]), ["gt0"], ["gates%d" % p])
            ACT(gt[:, 8:16], gt[:, 0:8], AF.Ln, ["gt0"], ["gt1"])
            DVE(lambda e, p=p: e.tensor_scalar(lnb3[:, p, :], gt[:, 8:16], -1.0, None, ALU.mult), ["gt1"], ["gates%d" % p])
            DVE(lambda e, pp=pp: e.tensor_tensor(gt[:, 16:24], pp[:, 264:272], dtb, ALU.add), [ppn, "dtb"], ["gt2"])
            ACT(gt[:, 24:32], gt[:, 16:24], AF.Exp, ["gt2"], ["gt3"])
            ACT(gt[:, 32:40], gt[:, 24:32], AF.Ln, ["gt3"], ["gt4"], bias=1.0)
            DVE(lambda e, p=p: e.tensor_tensor(g3[:, p, :], gt[:, 32:40], nea, ALU.mult), ["gt4", "nea"], ["gates%d" % p])
        dump("qkvT", qkvT, ["qkvT%d" % c for c in range(6)])
        P.barrier()
        if cfg.get("cstop", 9) <= 2:
            return
        A.off = mark
        H = Arena(hT.bitcast(F32))
        o_acc = H.alloc(NT * 256)
        o3 = o_acc.rearrange("p (t c) -> p t c", t=NT)
        tokb = [A.alloc(768, BF16) for _ in range(2)]
        bv = [A.alloc(256, BF16) for _ in range(2)]
        bkg = [A.alloc(256, BF16) for _ in range(2)]
        kdec = [A.alloc(512, BF16) for _ in range(2)]
        qdec = [A.alloc(512, BF16) for _ in range(2)]
        sm = [A.alloc(64) for _ in range(2)]
        GT = [A.alloc(128) for _ in range(4)]
        dm = GT
        dec2 = [H.alloc(256) for _ in range(4)]
        LA = dec2
        NA = [H.alloc(128) for _ in range(4)]
        LN = [[H.alloc(256) for _ in range(2)] for _ in range(4)]
        Pm = [[H.alloc(128) for _ in range(2)] for _ in range(4)]
        TTb = [A.alloc(128, BF16) for _ in range(4)]
        u_sb = [A.alloc(256) for _ in range(2)]
        wT_sb = [A.alloc(512, BF16) for _ in range(2)]
        qdT_sb = [A.alloc(512, BF16) for _ in range(2)]
        atT_sb = [A.alloc(512, BF16) for _ in range(2)]
        S32 = [A.alloc(256) for _ in range(2)]
        Sb = [A.alloc(256, BF16) for _ in range(2)]
        vnew = [A.alloc(256, BF16) for _ in range(2)]
        ysq = A.alloc(256)
        yc = A.alloc(256, BF16)
        ycT = A.alloc(256, BF16)
        tmp = A.alloc(DM)
        for d in range(2):
            POOL(lambda e, d=d: e.memset(S32[d], 0.0), [], ["S32_%d" % d])
            POOL(lambda e, d=d: e.memset(Sb[d], 0.0), [], ["Sb_%d" % d])
        order = [list(range(NT)), [1, 0] + list(range(NT - 1, 1, -1))]
        visited = set()

        def prep(p, d):
            co = coff(p)
            tk = tokb[d]
            tk3 = tk.rearrange("p (c f) -> p c f", c=3)
            tkn = "tokb%d" % d
            s_ = sm[d]
            sn = "sm%d_" % d
            pT = pb[7].bitcast(BF16)
            TRS([(pT[:, c * 128:(c + 1) * 128], qkvT3[:, c, co:co + 128]) for c in range(6)],
                ["qkvT%d" % c for c in range(6)] + ["ident_b"], ["pb7"], ident_b)
            ACT(tk, pT[:, 0:768], AF.Copy, ["pb7"], [tkn])
            gsl = g3[:, p, d * 4:(d + 1) * 4]
            P.op("pe", lambda e, d=d, gsl=gsl: [
                e.matmul(pb[6][:, 0:4], gm3[:, d, :], gsl, start=True, stop=True),
                e.matmul(pb[6][:, 4:8], gm3[:, 2, :], gsl, start=True, stop=True),
                e.matmul(pb[6][:, 8:12], gm3[:, 3, :], gsl, start=True, stop=True)], ["gm", "gates%d" % p], ["pb6"])
            ACT(s_[:, 24:36], pb[6][:, 0:12], AF.Copy, ["pb6"], [sn + "raw"])
            ACT(s_[:, 0:4], s_[:, 24:28], AF.Exp, [sn + "raw"], [sn + "egc"])
            DVE(lambda e, s_=s_: e.tensor_tensor(s_[:, 20:24], s_[:, 28:32], s_[:, 24:28], ALU.subtract), [sn + "raw"], [sn + "t"])
            ACT(s_[:, 4:8], s_[:, 20:24], AF.Exp, [sn + "t"], [sn + "ekd"])
            ACT(s_[:, 8:16], s_[:, 28:36], AF.Exp, [sn + "raw"], [sn + "egl"])
            if cfg.get("pstop", 9) <= 1:
                return
            bsl = beta3[:, p, d * 4:(d + 1) * 4]
            DVE(lambda e, s_=s_, bsl=bsl: e.tensor_tensor(s_[:, 16:20], s_[:, 0:4], bsl, ALU.mult), [sn + "egc", "gates%d" % p], [sn + "bgk"])
            kt3 = tk3[:, 1, :].rearrange("p (h f) -> p h f", h=4)
            vt3 = tk3[:, 2, :].rearrange("p (h f) -> p h f", h=4)
            qt3 = tk3[:, 0, :].rearrange("p (h f) -> p h f", h=4)
            bc = lambda a: a.unsqueeze(2).broadcast_to([128, 4, 64])
            v4 = lambda a: a.rearrange("p (h f) -> p h f", h=4)
            DVE(lambda e, d=d, vt3=vt3, bsl=bsl: e.tensor_tensor(v4(bv[d]), vt3, bc(bsl), ALU.mult), [tkn, "gates%d" % p], ["bv%d" % d])
            DVE(lambda e, d=d, kt3=kt3, s_=s_: e.tensor_tensor(v4(bkg[d]), kt3, bc(s_[:, 16:20]), ALU.mult), [tkn, sn + "bgk"], ["bkg%d" % d])
            v42 = lambda a: a.rearrange("p (h r f) -> p h r f", h=4, r=2)
            bc2 = lambda a: a.unsqueeze(2).unsqueeze(3).broadcast_to([128, 4, 2, 64])
            dup = lambda a3: a3.unsqueeze(2).broadcast_to([128, 4, 2, 64])
            POOL(lambda e, d=d, kt3=kt3, s_=s_: e.tensor_tensor(v42(kdec[d]), dup(kt3), bc2(s_[:, 4:8]), ALU.mult), [tkn, sn + "ekd"], ["kdec%d" % d])
            POOL(lambda e, d=d, qt3=qt3, s_=s_: e.tensor_tensor(v42(qdec[d]), dup(qt3), bc2(s_[:, 0:4]), ALU.mult), [tkn, sn + "egc"], ["qdec%d" % d])
            if cfg.get("pstop", 9) <= 2:
                return
            P.op("pe", lambda e, d=d: [e.matmul(pb[5][:, h * 128:(h + 1) * 128], qdec[d][:, h * 128:(h + 1) * 128], ident_b,
                                                start=True, stop=True, skip_group_check=True) for h in range(4)],
                 ["qdec%d" % d, "ident_b"], ["pb5"])
            ACT(qdT_sb[d], pb[5], AF.Copy, ["pb5"], ["qdT%d" % d])
            if cfg.get("pstop", 9) <= 3:
                return
            def hv(h):
                ctk = 2 + h // 2
                ctq = h // 2
                hs = slice((h % 2) * 64, (h % 2) * 64 + 64)
                return (ctk, ctq, qkvT3[hs, ctk, co:co + 128], qkvT3[hs, ctq, co:co + 128], pb[h][:, 0:128], pb[4 + h][:, 0:256])
            for h in range(4):
                hn = "%d" % h
                DVE(lambda e, h=h, d=d: e.tensor_scalar(GT[h], gm3[:, d, :], g3[:, p, d * 4 + h:d * 4 + h + 1], None, ALU.mult),
                    ["gm", "gates%d" % p], ["GT" + hn])
            for h in range(4):
                hn = "%d" % h
                ctk, ctq, kTh, qTh, pd, pga = hv(h)
                P.op("pe", lambda e, h=h, pd=pd: [e.matmul(pd, GT[h], ones_f, start=True, stop=False, skip_group_check=True),
                                                  e.matmul(pd, negones, GT[h], start=False, stop=True, skip_group_check=True)],
                     ["GT" + hn, "ones_f", "negones"], [pbn[h]])
                P.op("pe", lambda e, pga=pga, kTh=kTh, qTh=qTh: [e.matmul(pga[:, 0:128], kTh, kTh, start=True, stop=True, skip_group_check=True),
                                                                e.matmul(pga[:, 128:256], qTh, kTh, start=True, stop=True, skip_group_check=True)],
                     ["qkvT%d" % ctk, "qkvT%d" % ctq], [pbn[4 + h]])
            for h in range(4):
                hn = "%d" % h
                ctk, ctq, kTh, qTh, pd, pga = hv(h)
                DVE(lambda e, h=h, pd=pd: e.tensor_scalar(dm[h], pd, 0.0, None, ALU.min), [pbn[h]], ["GT" + hn])
            for h in range(4):
                hn = "%d" % h
                ACT(dec2[h][:, 0:128], dm[h], AF.Exp, ["GT" + hn, "gates%d" % p], ["LA" + hn], bias=lnb3[:, p, d * 4 + h:d * 4 + h + 1])
                ACT(dec2[h][:, 128:256], dm[h], AF.Exp, ["GT" + hn], ["LA" + hn])
            for h in range(4):
                hn = "%d" % h
                POOL(lambda e, h=h, d=d: e.tensor_tensor(dec2[h], dec2[h], gm[:, (4 + 2 * d) * 128:(6 + 2 * d) * 128], ALU.mult),
                     ["LA" + hn, "gm"], ["LA" + hn])
            for h in range(4):
                hn = "%d" % h
                ctk, ctq, kTh, qTh, pd, pga = hv(h)
                DVE(lambda e, h=h, pga=pga: e.tensor_tensor(LA[h], pga, dec2[h], ALU.mult), [pbn[4 + h], "LA" + hn], ["LA" + hn])
            if cfg.get("pstop", 9) <= 4:
                return
            for h in range(4):
                hn = "%d" % h
                pt_ = pb[h][:, 0:256]
                TRS([(pt_[:, 0:128], LA[h][:, 0:128]), (pt_[:, 128:256], LA[h][:, 128:256])], ["LA" + hn, "ident_f"], [pbn[h]], ident_f)
                ACT(NA[h], pt_[:, 0:128], AF.Copy, [pbn[h]], ["NA" + hn])
                ACT(atT_sb[d][:, h * 128:(h + 1) * 128], pt_[:, 128:256], AF.Copy, [pbn[h]], ["atT%d" % d])
                DVE(lambda e, h=h: e.tensor_tensor(Pm[h][0], ident_f, NA[h], ALU.subtract), ["NA" + hn, "ident_f"], ["P0_" + hn])
            if cfg.get("pstop", 9) <= 5:
                return
            curL = [LA[h][:, 0:128] for h in range(4)]
            curN = [NA[h] for h in range(4)]
            curLn = ["LA%d" % h for h in range(4)]
            curNn = ["NA%d" % h for h in range(4)]
            for lev in range(5):
                last = lev == 4
                for h in range(4):
                    hn = "%d" % h
                    pln = pb[4 + h][:, 0:256]
                    dst = LN[h][lev % 2]
                    dn = "LN%d_%d" % (h, lev % 2)
                    if not last:
                        P.op("pe", lambda e, pln=pln, h=h, cl=curL[h], cn=curN[h]: [
                            e.matmul(pln[:, 0:128], cn, cl, start=True, stop=True, skip_group_check=True),
                            e.matmul(pln[:, 128:256], cl, cn, start=True, stop=True, skip_group_check=True)],
                            [curLn[h], curNn[h]], [pbn[4 + h]])
                        if h % 2 == 0:
                            ACT(dst, pln, AF.Copy, [pbn[4 + h]], [dn])
                        else:
                            DVE(lambda e, dst=dst, pln=pln: e.tensor_copy(dst, pln), [pbn[4 + h]], [dn])
                    else:
                        P.op("pe", lambda e, pln=pln, h=h, cl=curL[h], cn=curN[h]: e.matmul(pln[:, 0:128], cn, cl, start=True, stop=True, skip_group_check=True),
                             [curLn[h], curNn[h]], [pbn[4 + h]])
                        ACT(dst[:, 0:128], pln[:, 0:128], AF.Copy, [pbn[4 + h]], [dn])
                    curL[h] = dst[:, 0:128]
                    curN[h] = dst[:, 128:256]
                    curLn[h] = dn
                    curNn[h] = dn
                for h in range(4):
                    hn = "%d" % h
                    ppd = pb[h][:, 0:128]
                    src = Pm[h][lev % 2]
                    dstp = Pm[h][(lev + 1) % 2]
                    P.op("pe", lambda e, ppd=ppd, cl=curL[h], src=src: e.matmul(ppd, cl, src, start=True, stop=True, skip_group_check=True),
                         [curLn[h], "P%d_" % (lev % 2) + hn], [pbn[h]])
                    if lev < 4:
                        DVE(lambda e, dstp=dstp, ppd=ppd, src=src: e.tensor_tensor(dstp, ppd, src, ALU.add),
                            [pbn[h], "P%d_" % (lev % 2) + hn], ["P%d_" % ((lev + 1) % 2) + hn])
                    else:
                        DVE(lambda e, h=h, ppd=ppd, src=src: e.tensor_tensor(TTb[h], ppd, src, ALU.add),
                            [pbn[h], "P%d_" % (lev % 2) + hn], ["TTb" + hn])
            if cfg.get("pstop", 9) <= 6:
                return
            for h in range(4):
                hn = "%d" % h
                TT = TTb[h]
                P.op("pe", lambda e, h=h, d=d, TT=TT: [
                    e.matmul(pb[4][:, h * 64:(h + 1) * 64], TT, bv[d][:, h * 64:(h + 1) * 64], start=True, stop=True, skip_group_check=True),
                    e.matmul(pb[6][0:64, h * 128:(h + 1) * 128], bkg[d][:, h * 64:(h + 1) * 64], TT, start=True, stop=True, skip_group_check=True)],
                    ["TTb" + hn, "bv%d" % d, "bkg%d" % d], ["pb4", "pb6"])
            ACT(u_sb[d], pb[4][:, 0:256], AF.Copy, ["pb4"], ["u_sb%d" % d])
            DVE(lambda e, d=d: e.tensor_scalar(wT_sb[d][0:64, :], pb[6][0:64, 0:512], 1.0, None, ALU.mult), ["pb6"], ["wT%d" % d])

        def scan(p, d, need_o):
            s_ = sm[d]
            sn = "sm%d_" % d
            chunks = [0, 1] if d == 0 else [1, 0]
            S3 = S32[d].rearrange("p (h f) -> p h f", h=4)
            bw, bo, bs = (4, 5, 6) if d == 0 else (1, 2, 3)
            for cc in chunks:
                rs = slice(cc * 64, cc * 64 + 64)
                P.op("pe", lambda e, d=d: [e.matmul(pb[bw][:, h * 64:(h + 1) * 64], wT_sb[d][0:64, h * 128:(h + 1) * 128],
                                                    Sb[d][0:64, h * 64:(h + 1) * 64], start=True, stop=True, skip_group_check=True)
                                           for h in range(4)], ["wT%d" % d, "Sb_%d" % d], [pbn[bw]])
                yield
                DVE(lambda e, d=d, rs=rs: e.tensor_tensor(vnew[d][rs, :], u_sb[d][rs, :], pb[bw][rs, 0:256], ALU.subtract),
                    [pbn[bw], "u_sb%d" % d], ["vnew%d" % d])
                yield
                if need_o:
                    def omm(e, d=d, rs=rs):
                        r_ = []
                        for h in range(4):
                            o_ = pb[bo][:, h * 64:(h + 1) * 64]
                            r_.append(e.matmul(o_, qdT_sb[d][rs, h * 128:(h + 1) * 128], Sb[d][rs, h * 64:(h + 1) * 64],
                                               start=True, stop=False, skip_group_check=True))
                            r_.append(e.matmul(o_, atT_sb[d][rs, h * 128:(h + 1) * 128], vnew[d][rs, h * 64:(h + 1) * 64],
                                               start=False, stop=True, skip_group_check=True))
                        return r_
                    P.op("pe", omm, ["qdT%d" % d, "Sb_%d" % d, "atT%d" % d, "vnew%d" % d], [pbn[bo]])
                    yield
                    key = (p, cc)
                    if key not in visited:
                        visited.add(key)
                        ACT(o3[rs, p, :], pb[bo][rs, 0:256], AF.Copy, [pbn[bo]], ["oacc%d_%d" % (p, cc)])
                    else:
                        DVE(lambda e, rs=rs: e.tensor_tensor(o3[rs, p, :], o3[rs, p, :], pb[bo][rs, 0:256], ALU.add),
                            [pbn[bo], "oacc%d_%d" % (p, cc)], ["oacc%d_%d" % (p, cc)])
                    yield
                P.op("pe", lambda e, d=d, rs=rs: [e.matmul(pb[bs][:, h * 64:(h + 1) * 64], kdec[d][rs, h * 128:(h + 1) * 128],
                                                          vnew[d][rs, h * 64:(h + 1) * 64], start=True, stop=True, skip_group_check=True)
                                                 for h in range(4)], ["kdec%d" % d, "vnew%d" % d], [pbn[bs]])
                yield
                for half in range(2):
                    hr = slice(half * 64, half * 64 + 64)
                    eg = s_[hr, 8:12] if cc == half else s_[hr, 12:16]
                    DVE(lambda e, S3=S3, eg=eg, hr=hr: e.tensor_tensor(S3[hr], S3[hr], eg.unsqueeze(2).broadcast_to([64, 4, 64]), ALU.mult),
                        [sn + "egl", "S32_%d" % d], ["S32_%d" % d])
                yield
                DVE(lambda e, d=d: e.tensor_tensor(S32[d], S32[d], pb[bs][:, 0:256], ALU.add), [pbn[bs], "S32_%d" % d], ["S32_%d" % d])
                yield
                ACT(Sb[d], S32[d], AF.Copy, ["S32_%d" % d], ["Sb_%d" % d])
                yield

        def run_interleaved(gens):
            gens = list(gens)
            while gens:
                for g in list(gens):
                    try:
                        next(g)
                    except StopIteration:
                        gens.remove(g)

        for step in range(cfg.get("nsteps", NT)):
            for d in range(2):
                prep(order[d][step], d)
            if cfg.get("cstop", 9) <= 3:
                continue
            run_interleaved([scan(order[d][step], d, (order[d][step] >= 2) or ctx_out) for d in range(2)])
        if cfg.get("cstop", 9) <= 4:
            P.barrier()
            return
        dump("oacc", o_acc, ["oacc%d_%d" % (p, cc) for p in range(2, NT) for cc in range(2)])
        for p in range(NT):
            if p < 2 and not ctx_out:
                continue
            which = 1 if p < 2 else 0
            on = ["oacc%d_%d" % (p, cc) for cc in range(2)]
            ov = o3[:, p, :]
            DVE(lambda e, ov=ov: e.tensor_tensor(ysq, ov, ov, ALU.mult), on, ["ysq"])
            DVE(lambda e: e.reduce_sum(sm[0][:, 44:48], ysq.rearrange("p (h f) -> p h f", h=4), AX.X), ["ysq"], ["yss"])
            ACT(sm[0][:, 48:52], sm[0][:, 44:48], AF.Sqrt, ["yss"], ["ysd"], scale=1.0 / 64, bias=EPS)
            DVE(lambda e: e.reciprocal(sm[0][:, 52:56], sm[0][:, 48:52]), ["ysd"], ["yrs"])
            DVE(lambda e, ov=ov: e.tensor_tensor(ysq.rearrange("p (h f) -> p h f", h=4), ov.rearrange("p (h f) -> p h f", h=4),
                                                 sm[0][:, 52:56].unsqueeze(2).broadcast_to([128, 4, 64]), ALU.mult), on + ["yrs", "ysq"], ["ysq"])
            DVE(lambda e, p=p: e.tensor_tensor(yc, ysq, szc3[:, p, :], ALU.mult), ["ysq", "szc%d" % p], ["yc"])
            pT = pb[0].bitcast(BF16)
            TRS([(pT[:, c * 128:(c + 1) * 128], yc[:, c * 128:(c + 1) * 128]) for c in range(2)], ["yc", "ident_b"], ["pb0"], ident_b)
            ACT(ycT, pT[:, 0:256], AF.Copy, ["pb0"], ["ycT"])
            for half in range(2):
                MM(pb[1 + half], [(ycT[:, c * 128:(c + 1) * 128], woC3[:, c, half * 512:(half + 1) * 512]) for c in range(2)],
                   ["ycT", "woC"], [pbn[1 + half]])
            x_update(p, which, pb[1], "pb1", pb[2], "pb2", tmp, "tmpC")
        P.barrier()

    def final(b):
        arena.reset()
        fnw_bc = arena.alloc(DM)
        ostages = [arena.alloc(DM) for _ in range(2)]
        DMA("sp", fnw_bc, fnw_d[0:1, :].partition_broadcast(128), [], ["fnw_bc"])
        for p in range(2, NT):
            ostage = ostages[p % 2]
            osn = "ostage%d" % (p % 2)
            xs = X3[:, p, :]
            ss = stat[:, 0:1]
            sd = stat[:, 1:2]
            rstd = stat[:, 2:3]
            ACT(xn, xs, AF.Square, ["X%d" % p], ["xn", "st0"], accum_out=ss)
            ACT(sd, ss, AF.Sqrt, ["st0"], ["st1"], scale=1.0 / DM, bias=EPS)
            DVE(lambda e: e.reciprocal(rstd, sd), ["st1"], ["st2"])
            DVE(lambda e, xs=xs, ostage=ostage: e.scalar_tensor_tensor(out=ostage, in0=xs, scalar=rstd, in1=fnw_bc, op0=ALU.mult, op1=ALU.mult),
                ["X%d" % p, "st2", "fnw_bc"], [osn])
            DMA("sp", out_d[b, (p - 2) * 128:(p - 1) * 128, :], ostage, [osn], [])

    for b in range(NB):
        for l in range(nlayers):
            ctx_out = l < nlayers - 1
            mod_prep(b, l, ctx_out)
            phase1(b, l)
            P.barrier()
            if useA:
                phaseA(b, l, ctx_out)
            if useB:
                phaseB(b, l, ctx_out)
            if useC:
                phaseC(b, l, ctx_out)
        final(b)
        P.barrier()
    stats = P.emit()
    return nc, stats


def host_consts():
    ident = np.eye(128, dtype=np.float32)
    perm = np.zeros((128, 128), np.float32)
    cos = np.zeros((128, SEQ), np.float32)
    sin = np.zeros((128, SEQ), np.float32)
    t = np.arange(SEQ)
    rowp = (t // 64).astype(np.float32)
    colp = (t % 64).astype(np.float32)
    inv_freq = (10000.0 ** (-np.arange(0, 32, 2, dtype=np.float32) / 32.0)).astype(np.float32)
    for pp in range(128):
        d = pp % 64
        half = d // 32
        dd = d % 32
        f = dd % 16
        pos = rowp if half == 0 else colp
        ang = (pos * inv_freq[f]).astype(np.float32)
        cos[pp] = np.cos(ang)
        sgn = -1.0 if dd < 16 else 1.0
        sin[pp] = sgn * np.sin(ang)
        partner = pp + 16 if dd < 16 else pp - 16
        perm[partner, pp] = 1.0
    r = np.arange(128)[:, None]
    c = np.arange(128)[None, :]
    same = (r // 64) == (c // 64)
    tri_f = same & (r <= c)
    tri_b = same & (r >= c)
    gm = np.stack([tri_f, tri_b, same, ~same, same & (r > c), same & (r >= c), same & (r < c), same & (r <= c)], 0).astype(np.float32)
    return ident, perm, cos, sin, gm.transpose(1, 0, 2).reshape(128, 8 * 128).copy()


def make_in_maps(inputs, NB, ncores):
    ident, perm, cos, sin, gm = host_consts()
    f = lambda a: np.ascontiguousarray(np.asarray(a, dtype=np.float32))
    c = f(inputs["c"])
    c_ctx = f(inputs["c_ctx"])
    shared = {
        "w_ada": f(inputs["w_ada"]),
        "b_adaT": f(f(inputs["b_ada"]).reshape(2, 24, 128).transpose(2, 0, 1).reshape(128, 48)),
        "norm_wT": f(f(inputs["norm_w"]).reshape(2, 8, 128).transpose(2, 0, 1).reshape(128, 16)),
        "w_in": f(inputs["w_in"]),
        "w_out": f(inputs["w_out"]),
        "a_ln_w": f(inputs["a_ln_w"]),
        "a_wsT": f(f(inputs["a_ws"]).transpose(0, 3, 1, 2).reshape(2, 128, 512)),
        "a_bsT": f(f(inputs["a_bs"]).transpose(0, 2, 1)),
        "b_lam": f(f(inputs["b_lam"]).reshape(2, 256)),
        "b_subln_w": f(inputs["b_subln_w"]),
        "conv_wT": f(f(inputs["c_conv_w"]).reshape(2, 5, 6, 128).transpose(0, 3, 2, 1).reshape(2, 128, 30)),
        "c_a_log": f(f(inputs["c_a_log"]).reshape(2, 8)),
        "c_dt_bias": f(f(inputs["c_dt_bias"]).reshape(2, 8)),
        "c_norm_w": f(inputs["c_norm_w"]),
        "final_norm_w": f(f(inputs["final_norm_w"]).reshape(1, DM)),
        "ident": ident, "perm": perm, "ropecos": cos, "ropesin": sin, "gmasks": gm,
    }
    x = inputs["x"]
    ctx = inputs["ctx"]
    maps = []
    for i in range(ncores):
        b0 = i * NB
        rows = np.concatenate([c[b0:b0 + NB], c_ctx[None, :]], 0)
        cT = rows.reshape(NB + 1, 8, 128).transpose(2, 1, 0).reshape(128, 8 * (NB + 1))
        m = dict(shared)
        m["x"] = f(x[b0:b0 + NB])
        m["ctx"] = f(ctx[b0:b0 + NB])
        m["cT"] = f(cT)
        maps.append(m)
    return maps


_CACHE = {}


def kernel(**inputs):
    NB = 4
    key = ("full", NB)
    if key not in _CACHE:
        _CACHE[key] = build(NB, {})
    nc, _ = _CACHE[key]
    maps = make_in_maps(inputs, NB, NCORES)
    res = run_bass_kernel_spmd(nc, maps, core_ids=list(range(NCORES)))
    out = np.concatenate([r["out"] for r in res.results], axis=0)
    return out.astype(np.float32)
```
